# Optimizing a Trainium2 kernel written in Bass

```python
import math
import jax, jax.numpy as jnp
from jax import lax
import numpy as np

D_MODEL = 1024
BATCH = 4
SEQ = 4096
DEPTH = 1

HEAD_DIM = 64
DIL_GROUPS = ((128, 1), (512, 4), (2048, 16))
N_DIL_GROUPS = 3
A_HEADS_PER_GROUP = 4
A_HEADS = N_DIL_GROUPS * A_HEADS_PER_GROUP
A_WIDTH = A_HEADS * HEAD_DIM
A_OUT = A_HEADS_PER_GROUP * HEAD_DIM
SB_HEADS = 8
SB_WIDTH = SB_HEADS * HEAD_DIM
BLOCK = 128
ROPE_THETA = 10000.0
MEM_LEN = 256
X_HEADS = 4
X_HEAD_DIM = 128
X_WIDTH = X_HEADS * X_HEAD_DIM
D_FF = ((8 * D_MODEL // 3 + 127) // 128) * 128
CONV_WIDTH = 3
EPS = 1e-6
IN_COLS = 3 * A_WIDTH + 3 * SB_WIDTH + 2 * D_MODEL

kernel_name = "hybrid_dilated_stickbreak_gated_block"


def rms_norm(x, g):
    xf = x.astype(jnp.float32)
    y = xf * lax.rsqrt(jnp.mean(xf * xf, axis=-1, keepdims=True) + EPS)
    return (y * g.astype(jnp.float32)).astype(x.dtype)


def rope(x, pos):
    dh = x.shape[-1]
    half = dh // 2
    inv_freq = 1.0 / (ROPE_THETA ** (jnp.arange(half, dtype=jnp.float32) * (2.0 / dh)))
    ang = pos.astype(jnp.float32)[:, None] * inv_freq[None, :]
    cos = jnp.cos(ang)[None, :, None, :]
    sin = jnp.sin(ang)[None, :, None, :]
    xf = x.astype(jnp.float32)
    x1, x2 = xf[..., :half], xf[..., half:]
    out = jnp.concatenate([x1 * cos - x2 * sin, x2 * cos + x1 * sin], axis=-1)
    return out.astype(x.dtype)


def local_window_attn(q, k, v, n_back):
    n, L, dh = q.shape
    nb = -(-L // BLOCK)
    lp = nb * BLOCK
    pad = lp - L
    qp = jnp.pad(q.astype(jnp.float32), ((0, 0), (0, pad), (0, 0)))
    kp = jnp.pad(k.astype(jnp.float32), ((0, 0), (BLOCK, pad), (0, 0)))
    vp = jnp.pad(v.astype(jnp.float32), ((0, 0), (BLOCK, pad), (0, 0)))
    qb = qp.reshape(n, nb, BLOCK, dh)
    kb = jnp.concatenate([kp[:, :lp].reshape(n, nb, BLOCK, dh),
                          kp[:, BLOCK:].reshape(n, nb, BLOCK, dh)], axis=2)
    vb = jnp.concatenate([vp[:, :lp].reshape(n, nb, BLOCK, dh),
                          vp[:, BLOCK:].reshape(n, nb, BLOCK, dh)], axis=2)
    s = jnp.einsum('nbqd,nbkd->nbqk', qb, kb) * (dh ** -0.5)
    a_idx = jnp.arange(BLOCK)[:, None]
    c_idx = jnp.arange(2 * BLOCK)[None, :]
    dist = a_idx - c_idx + BLOCK
    kpos = jnp.arange(nb)[:, None, None] * BLOCK + c_idx[None] - BLOCK
    mask = ((dist >= 0) & (dist <= n_back))[None] & (kpos >= 0)
    s = jnp.where(mask[None], s, -jnp.inf)
    lse = jax.nn.logsumexp(s, axis=-1)
    p = jnp.exp(s - lse[..., None])
    o = jnp.einsum('nbqk,nbkd->nbqd', p, vb)
    return o.reshape(n, lp, dh)[:, :L], lse.reshape(n, lp)[:, :L]


def dilated_group(q, k, v, dilation, n_back):
    b, s, h, dh = q.shape
    L = s // dilation

    def split(t):
        return t.reshape(b, L, dilation, h, dh).transpose(0, 3, 2, 1, 4).reshape(b * h * dilation, L, dh)

    o, lse = local_window_attn(split(q), split(k), split(v), n_back)
    o = o.reshape(b, h, dilation, L, dh).transpose(0, 3, 2, 1, 4).reshape(b, s, h, dh)
    lse = lse.reshape(b, h, dilation, L).transpose(0, 3, 2, 1).reshape(b, s, h)
    return o, lse


def dilated_mixture(q, k, v):
    b, s = q.shape[0], q.shape[1]
    outs, lses = [], []
    for g, (window, dilation) in enumerate(DIL_GROUPS):
        sl = slice(g * A_HEADS_PER_GROUP, (g + 1) * A_HEADS_PER_GROUP)
        o, l = dilated_group(q[:, :, sl], k[:, :, sl], v[:, :, sl], dilation, window // dilation)
        outs.append(o)
        lses.append(l)
    o = jnp.stack(outs, axis=0)
    wts = jax.nn.softmax(jnp.stack(lses, axis=0), axis=0)
    y = jnp.sum(wts[..., None] * o, axis=0)
    return y.reshape(b, s, A_OUT)


def stick_breaking(q, k, v):
    b, s, h, dh = q.shape
    nb = s // BLOCK
    qf = q.astype(jnp.float32) * (dh ** -0.5)
    kf = k.astype(jnp.float32)
    vf = v.astype(jnp.float32)
    qb = qf.reshape(b, nb, BLOCK, h, dh).transpose(1, 0, 3, 2, 4)
    kpos = jnp.arange(s)

    def block(args):
        qblk, t0 = args
        z = jnp.einsum('bhqd,bshd->bhqs', qblk, kf)
        tq = t0 + jnp.arange(BLOCK)
        mask = kpos[None, :] < tq[:, None]
        log_keep = jnp.where(mask, jax.nn.log_sigmoid(-z), 0.0)
        after = lax.cumsum(log_keep, axis=3, reverse=True) - log_keep
        a = jnp.where(mask, jnp.exp(jax.nn.log_sigmoid(z) + after), 0.0)
        return jnp.einsum('bhqs,bshd->bhqd', a, vf)

    o = lax.map(block, (qb, jnp.arange(nb) * BLOCK))
    return o.transpose(1, 0, 3, 2, 4).reshape(b, s, h * dh)


def memory_cross_attn(n, m, w_xq, w_xkv, w_xo):
    b, s, _ = n.shape
    ml = m.shape[1]
    q = (n @ w_xq).reshape(b, s, X_HEADS, X_HEAD_DIM)
    kv = m @ w_xkv
    k = kv[..., :X_WIDTH].reshape(b, ml, X_HEADS, X_HEAD_DIM)
    v = kv[..., X_WIDTH:].reshape(b, ml, X_HEADS, X_HEAD_DIM)
    sc = jnp.einsum('bshd,bmhd->bhsm', q.astype(jnp.float32), k.astype(jnp.float32)) * (X_HEAD_DIM ** -0.5)
    p = jax.nn.softmax(sc, axis=-1)
    o = jnp.einsum('bhsm,bmhd->bshd', p, v.astype(jnp.float32)).reshape(b, s, X_WIDTH)
    return o.astype(n.dtype) @ w_xo


def conv_gated_mlp(n, w_up, conv_w, conv_b, w_down):
    u = n @ w_up
    c = u.shape[-1]
    u = lax.conv_general_dilated(
        u, conv_w.astype(u.dtype)[:, None, :], window_strides=(1,),
        padding=[(CONV_WIDTH - 1, 0)], dimension_numbers=('NWC', 'WIO', 'NWC'),
        feature_group_count=c) + conv_b.astype(u.dtype)
    gate, val = u[..., :D_FF], u[..., D_FF:]
    return (jax.nn.silu(gate) * val) @ w_down


def setup_inputs(seed: int = 0) -> dict:
    key = jax.random.key(seed)
    ks = jax.random.split(key, 20)
    f32 = jnp.float32

    def w(k, shape, fan_in):
        return jax.random.normal(k, shape, f32) * (fan_in ** -0.5)

    def gain(k, shape):
        return 1.0 + 0.02 * jax.random.normal(k, shape, f32)

    L = DEPTH
    return {
        "x": jax.random.normal(ks[0], (BATCH, SEQ, D_MODEL), f32),
        "mem": jax.random.normal(ks[1], (BATCH, MEM_LEN, D_MODEL), f32),
        "ln_mix_g": gain(ks[2], (L, D_MODEL)),
        "w_in": w(ks[3], (L, D_MODEL, IN_COLS), D_MODEL),
        "b_gate": 0.02 * jax.random.normal(ks[4], (L, 2 * D_MODEL), f32),
        "w_branch_a": w(ks[5], (L, A_OUT, D_MODEL), A_OUT),
        "w_branch_b": w(ks[6], (L, SB_WIDTH, D_MODEL), SB_WIDTH),
        "w_out": w(ks[7], (L, D_MODEL, D_MODEL), D_MODEL),
        "ln_x_g": gain(ks[8], (L, D_MODEL)),
        "ln_mem_g": gain(ks[9], (L, D_MODEL)),
        "w_xq": w(ks[10], (L, D_MODEL, X_WIDTH), D_MODEL),
        "w_xkv": w(ks[11], (L, D_MODEL, 2 * X_WIDTH), D_MODEL),
        "w_xo": w(ks[12], (L, X_WIDTH, D_MODEL), X_WIDTH),
        "ln_ffn_g": gain(ks[13], (L, D_MODEL)),
        "w_up": w(ks[14], (L, D_MODEL, 2 * D_FF), D_MODEL),
        "conv_w": w(ks[15], (L, CONV_WIDTH, 2 * D_FF), CONV_WIDTH),
        "conv_b": 0.02 * jax.random.normal(ks[16], (L, 2 * D_FF), f32),
        "w_down": w(ks[17], (L, D_FF, D_MODEL), D_FF),
        "ln_f_g": gain(ks[18], (D_MODEL,)),
    }


def reference(x, mem, ln_mix_g, w_in, b_gate, w_branch_a, w_branch_b, w_out,
              ln_x_g, ln_mem_g, w_xq, w_xkv, w_xo,
              ln_ffn_g, w_up, conv_w, conv_b, w_down, ln_f_g):
    b, s, _ = x.shape
    pos = jnp.arange(s)
    splits = [A_WIDTH, 2 * A_WIDTH, 3 * A_WIDTH,
              3 * A_WIDTH + SB_WIDTH, 3 * A_WIDTH + 2 * SB_WIDTH, 3 * A_WIDTH + 3 * SB_WIDTH,
              3 * A_WIDTH + 3 * SB_WIDTH + D_MODEL]
    h = x
    for l in range(DEPTH):
        n = rms_norm(h, ln_mix_g[l])
        proj = n @ w_in[l]
        qa, ka, va, qb, kb, vb, ga, gb = jnp.split(proj, splits, axis=-1)
        qa = rope(qa.reshape(b, s, A_HEADS, HEAD_DIM), pos)
        ka = rope(ka.reshape(b, s, A_HEADS, HEAD_DIM), pos)
        va = va.reshape(b, s, A_HEADS, HEAD_DIM)
        ya = dilated_mixture(qa, ka, va).astype(h.dtype) @ w_branch_a[l]
        yb = stick_breaking(qb.reshape(b, s, SB_HEADS, HEAD_DIM),
                            kb.reshape(b, s, SB_HEADS, HEAD_DIM),
                            vb.reshape(b, s, SB_HEADS, HEAD_DIM)).astype(h.dtype) @ w_branch_b[l]
        gate_a = jax.nn.sigmoid(ga + b_gate[l, :D_MODEL])
        gate_b = jax.nn.sigmoid(gb + b_gate[l, D_MODEL:])
        h = h + (gate_a * ya + gate_b * yb) @ w_out[l]
        n = rms_norm(h, ln_x_g[l])
        m = rms_norm(mem, ln_mem_g[l])
        h = h + memory_cross_attn(n, m, w_xq[l], w_xkv[l], w_xo[l])
        n = rms_norm(h, ln_ffn_g[l])
        h = h + conv_gated_mlp(n, w_up[l], conv_w[l], conv_b[l], w_down[l])
    return rms_norm(h, ln_f_g)
```

```python
import math
from contextlib import ExitStack
import numpy as np
import ml_dtypes
import concourse.bass as bass
import concourse.mybir as mybir
from concourse.bass_utils import run_bass_kernel_spmd

F32 = mybir.dt.float32
BF16 = mybir.dt.bfloat16
AF = mybir.ActivationFunctionType
ALU = mybir.AluOpType

D = 1024
NLOC = 4096
OWN0 = 1920
NOWN = NLOC - OWN0
DFF = 2816
NCH = 22
QTILES = [(0, 128)] + [(128 + 512 * j, 512) for j in range(4)]
DILS = (1, 4, 16)


class _Eng:
    def __init__(self, name, eng, sem):
        self.name = name
        self.eng = eng
        self.sem = sem
        self.count = 0
        self.known = {}
        self.maxwait = 0


class Sched:
    def __init__(self, nc, sems, n_dma_slots):
        self.nc = nc
        self.E = {}
        it = iter(sems)
        for name, eng in (("pe", nc.tensor), ("act", nc.scalar), ("dve", nc.vector),
                          ("pool", nc.gpsimd), ("sp", nc.sync)):
            self.E[name] = _Eng(name, eng, next(it))
        self.slots = {"sp": [], "pool": []}
        for i in range(n_dma_slots):
            s = _Eng("dma%d" % i, None, next(it))
            self.E[s.name] = s
            self.slots["sp" if i < n_dma_slots // 2 else "pool"].append(s)
        self.slot_rr = {"sp": 0, "pool": 0}
        self.res = {}

    def _collect(self, reads, writes):
        deps = set()
        for r in reads:
            st = self.res.get(r)
            if st and st[0]:
                deps.add(st[0])
        for w in writes:
            st = self.res.get(w)
            if st:
                if st[0]:
                    deps.add(st[0])
                deps.update(st[1])
        return deps

    def _wait(self, E, deps):
        need = {}
        for (en, idx) in deps:
            if en == E.name and en == "pe":
                continue
            need[en] = max(need.get(en, 0), idx)
        for en, idx in need.items():
            if E.known.get(en, 0) >= idx:
                continue
            Dp = self.E[en]
            E.eng.wait_ge(Dp.sem, idx)
            Dp.maxwait = max(Dp.maxwait, idx)
            E.known[en] = idx

    def _record(self, who, reads, writes):
        for r in reads:
            st = self.res.setdefault(r, [None, []])
            st[1].append(who)
            if len(st[1]) > 24:
                last = {}
                for (en, idx) in st[1]:
                    last[en] = max(last.get(en, 0), idx)
                st[1] = list(last.items())
        for w in writes:
            self.res[w] = [who, []]

    def op(self, engname, fn, reads=(), writes=(), inc=True):
        E = self.E[engname]
        self._wait(E, self._collect(reads, writes))
        ins = fn(E.eng)
        if inc:
            E.count += 1
            ins.then_inc(E.sem, 1)
            who = (engname, E.count)
        else:
            assert engname == "pe"
            who = (engname, E.count + 1)
        self._record(who, reads, writes)
        return ins

    def dma(self, qname, out, in_, reads=(), writes=()):
        Q = self.E[qname]
        S = self.slots[qname][self.slot_rr[qname]]
        self.slot_rr[qname] = (self.slot_rr[qname] + 1) % len(self.slots[qname])
        deps = self._collect(reads, writes)
        if S.count:
            deps.add((S.name, S.count))
        self._wait(Q, deps)
        ins = Q.eng.dma_start(out=out, in_=in_)
        S.count += 16
        ins.then_inc(S.sem, 16)
        self._record((S.name, S.count), reads, writes)
        return ins

    def barrier(self):
        all_deps = set()
        for name, Dp in self.E.items():
            if Dp.count:
                all_deps.add((name, Dp.count))
        for en in ("pe", "act", "dve", "pool", "sp"):
            self._wait(self.E[en], all_deps)
        self.res = {}

    def finish(self, engname="sp"):
        E = self.E[engname]
        deps = {(n, Dp.count) for n, Dp in self.E.items() if Dp.count and n != engname}
        self._wait(E, deps)
        for n, Dp in self.E.items():
            assert Dp.maxwait <= Dp.count, (n, Dp.maxwait, Dp.count)


class Ring:
    def __init__(self, name, tensors):
        self.name = name
        self.t = tensors
        self.i = -1

    def next(self):
        self.i = (self.i + 1) % len(self.t)
        return self.t[self.i], (self.name, self.i)


def build(dbg=False, stop_after=99):
    nc = bass.Bass("TRN2", target_bir_lowering=False)

    def din(name, shape, dt=F32):
        return nc.dram_tensor(name, shape, dt, kind="ExternalInput").ap()

    xl = din("xl", [NLOC, D])
    mem = din("mem", [256, D])
    w_in = din("w_in", [D, 5888])
    w_sw = din("w_sw", [D, 1536])
    g_mix = din("g_mix", [1, D])
    g_x = din("g_x", [1, D])
    g_mem = din("g_mem", [1, D])
    g_ffn = din("g_ffn", [1, D])
    g_f = din("g_f", [1, D])
    b_gate = din("b_gate", [128, 16])
    w_ba = din("w_ba", [256, D])
    w_bb = din("w_bb", [512, D])
    w_out = din("w_out", [D, D])
    w_xq = din("w_xq", [D, 512])
    w_xkv = din("w_xkv", [D, 1024])
    w_xo = din("w_xo", [512, D])
    w_up = din("w_up", [D, 2 * DFF])
    conv_w = din("conv_w", [128, 44, 3])
    conv_b = din("conv_b", [128, 44])
    w_down = din("w_down", [DFF, D])
    rcos = din("rcos", [128, NLOC])
    rsin = din("rsin", [128, NLOC])
    cst_d = din("cst", [128, 8, 128], BF16)
    mask4_d = din("mask4", [128, 512], BF16)
    onesv_d = din("onesv", [128, 2, 64], BF16)
    hval_d = din("hval", [128, 1])
    out = nc.dram_tensor("out", [2048, D], F32, kind="ExternalOutput").ap()
    if dbg:
        d_ya = nc.dram_tensor("d_ya", [128, 2, NOWN], F32, kind="ExternalOutput").ap()
        d_yb = nc.dram_tensor("d_yb", [128, 4, NOWN], F32, kind="ExternalOutput").ap()
        d_h = nc.dram_tensor("d_h", [3, 2048, D], F32, kind="ExternalOutput").ap()
        d_m = nc.dram_tensor("d_m", [128, 8, 512], F32, kind="ExternalOutput").ap()
        d_x = nc.dram_tensor("d_x", [128, 8, 512], F32, kind="ExternalOutput").ap()

    def wview(w):
        return w.rearrange("(kc p) n -> p kc n", p=128)

    SCR = {}
    for nm, src in (("w_in", w_in), ("w_sw", w_sw), ("w_ba", w_ba), ("w_bb", w_bb), ("w_xkv", w_xkv), ("w_out", w_out),
                    ("w_xq", w_xq), ("w_xo", w_xo), ("w_up", w_up), ("w_down", w_down)):
        R_, C_ = src.shape
        SCR[nm] = (src, nc.dram_tensor("scr_" + nm, [R_, C_], BF16, kind="Internal").ap(), [])

    with ExitStack() as es:
        def sb(name, shape, dt):
            return es.enter_context(nc.sbuf_tensor("s_" + name, shape, dt))

        NSLOT = 32
        sems = [es.enter_context(nc.semaphore("s%d" % i)) for i in range(5 + NSLOT)]
        PSALL = es.enter_context(nc.psum_tensor("psall", [128, 4096], F32))
        PB = [PSALL[:, 512 * i:512 * (i + 1)] for i in range(6)]
        pT = Ring("pT", [PSALL[:, 512 * (6 + i):512 * (7 + i)].bitcast(BF16) for i in range(2)])
        ps6 = Ring("pb", PB)
        cst = sb("cst", [128, 8, 128], BF16)
        mask4 = sb("mask4", [128, 512], BF16)
        onesv = sb("onesv", [128, 2, 64], BF16)
        hval = sb("hval", [128, 1], F32)
        gB_mix = sb("gB_mix", [128, D], F32)
        yaT = sb("yaT", [128, 2, NOWN], BF16)
        ybT = sb("ybT", [128, 4, NOWN], BF16)
        junk = sb("junk", [128, D], BF16)
        xn_r = Ring("xn", [sb("xn%d" % i, [128, D], BF16) for i in range(4)])
        ss_r = Ring("ss", [sb("ss%d" % i, [128, 1], F32) for i in range(8)])
        rs_r = Ring("rs", [sb("rs%d" % i, [128, 1], F32) for i in range(8)])
        xin_r = Ring("xin", [sb("xin%d" % i, [128, D], F32) for i in range(2)])

        block = es.enter_context(nc.Block())
        S = Sched(nc, sems, NSLOT)

        ident = cst[:, 0, :]
        m_cur = cst[:, 1, :]
        m_prev = cst[:, 2, :]
        m_strict = cst[:, 3, :]
        tri_incl = cst[:, 4, :]
        tri_fix = cst[:, 5, :]
        zeros_b = cst[:, 6, :]
        ones_b = cst[:, 7, :]

        S.dma("sp", cst[:], cst_d, writes=["cst"])
        S.dma("sp", mask4[:], mask4_d, writes=["cst"])
        S.dma("sp", onesv[:], onesv_d, writes=["cst"])
        S.dma("sp", hval[:], hval_d, writes=["cst"])
        S.dma("sp", gB_mix[:], g_mix.partition_broadcast(128), writes=["gB_mix"])

        def precast(names):
            for nm in names:
                src, dst, keys = SCR[nm]
                R_ = src.shape[0]
                c_lo = 3840 if nm == "w_in" else 0
                step = 256 if R_ >= 1024 else R_
                for r0 in range(0, R_, step):
                    S.dma("pool", dst[r0:r0 + step, c_lo:], src[r0:r0 + step, c_lo:], writes=[("scr", nm, r0)])
                    keys.append(("scr", nm, r0))

        def sw(nm):
            return wview(SCR[nm][1])

        def skeys(nm):
            return list(SCR[nm][2])

        def rms_front(items, gB, gkey):
            st = []
            for (x_ap, x_key, outview, outkey) in items:
                ss, ssk = ss_r.next()
                rs, rsk = rs_r.next()
                S.op("act", lambda e, x_ap=x_ap, ss=ss: e.activation(out=junk[:], in_=x_ap, func=AF.Square, accum_out=ss[:]),
                     reads=[x_key], writes=["junk", ssk])
                st.append([x_ap, x_key, outview, outkey, ss, ssk, rs, rsk])
            for t in st:
                x_ap, x_key, outview, outkey, ss, ssk, rs, rsk = t
                S.op("act", lambda e, ss=ss, rs=rs: e.activation(out=rs[:], in_=ss[:], func=AF.Sqrt, scale=1.0 / D, bias=1e-6),
                     reads=[ssk], writes=[rsk])
            for t in st:
                x_ap, x_key, outview, outkey, ss, ssk, rs, rsk = t
                S.op("dve", lambda e, rs=rs: e.reciprocal(out=rs[:], in_=rs[:]), reads=[rsk], writes=[rsk])
            fr = []
            for t in st:
                x_ap, x_key, outview, outkey, ss, ssk, rs, rsk = t
                xn, xnk = xn_r.next()
                S.op("dve", lambda e, x_ap=x_ap, rs=rs, xn=xn: e.scalar_tensor_tensor(out=xn[:], in0=x_ap, scalar=rs[:], in1=gB[:],
                                                                                  op0=ALU.mult, op1=ALU.mult),
                     reads=[x_key, rsk, gkey], writes=[xnk])
                fr.append((xn, xnk, outview, outkey))
            return fr

        def rms_back(fr):
            for (xn, xnk, outview, outkey) in fr:
                pt, ptk = pT.next()
                for k in range(8):
                    S.op("pe", lambda e, k=k, pt=pt, xn=xn: e.transpose(out=pt[:, k * 128:(k + 1) * 128], in_=xn[:, k * 128:(k + 1) * 128],
                                                                      identity=ident),
                         reads=[xnk, "cst"], writes=[ptk], inc=(k == 7))
                S.op("act", lambda e, outview=outview, pt=pt: e.activation(out=outview, in_=pt[:].rearrange("p (k t) -> p k t", k=8),
                                                                         func=AF.Copy),
                     reads=[ptk], writes=[outkey])

        def rms_to_T_multi(items, gB, gkey):
            rms_back(rms_front(items, gB, gkey))

        def rms_to_T(x_ap, x_key, gB, gkey, outview, outkey):
            rms_to_T_multi([(x_ap, x_key, outview, outkey)], gB, gkey)

        def mm_acc(ps_ap, pskey, pairs, reads, inc_last=True, first_start=True):
            n = len(pairs)
            for i, (l, r) in enumerate(pairs):
                S.op("pe", lambda e, l=l, r=r, i=i: e.matmul(ps_ap, lhsT=l, rhs=r, start=(i == 0 and first_start),
                                                             stop=(i == n - 1)),
                     reads=reads, writes=[pskey], inc=(inc_last and i == n - 1))

        with ExitStack() as es1:
            xnT = es1.enter_context(nc.sbuf_tensor("xnT", [128, 8, NLOC], BF16))

            with ExitStack() as esp1:
                xs_r = Ring("xs", [esp1.enter_context(nc.sbuf_tensor("xs%d" % i, [128, D], F32)) for i in range(8)])
                xn1_r = Ring("xn1", [esp1.enter_context(nc.sbuf_tensor("xn1_%d" % i, [128, D], BF16)) for i in range(4)])
                xn_save = xn_r.t, xn_r.name
                xn_r.t, xn_r.name = xn_r.t + xn1_r.t, "xnw"
                for b0 in range(0, 32, 4):
                    its = []
                    for blk in range(b0, b0 + 4):
                        xin, xk = xs_r.next()
                        S.dma("sp", xin[:], xl[blk * 128:(blk + 1) * 128, :], writes=[xk])
                        its.append((xin[:], xk, xnT[:, :, blk * 128:(blk + 1) * 128], ("xnT", blk // 4)))
                    rms_to_T_multi(its, gB_mix, "gB_mix")
                S.barrier()
                xn_r.t, xn_r.name = xn_save
                xn_r.i = -1

            wbs = Ring("wbs", [es1.enter_context(nc.sbuf_tensor("wbs%d" % i, [128, 8, 128], BF16)) for i in range(8)])
            KT = es1.enter_context(nc.sbuf_tensor("KT", [128, NLOC], BF16))
            VT = es1.enter_context(nc.sbuf_tensor("VT", [128, NLOC], BF16))
            Vtok = es1.enter_context(nc.sbuf_tensor("Vtok", [128, 32, 128], BF16))

            def xkeys(t0, n):
                return [("xnT", t) for t in range(t0 // 512, (t0 + n - 1) // 512 + 1)]

            def load_w(nm, c0):
                wt, wk = wbs.next()
                S.dma("pool", wt[:], wview(SCR[nm][0])[:, :, c0:c0 + 128], writes=[wk])
                return wt, wk

            def proj(ps, psk, wt, wk, t0, n):
                mm_acc(ps[:, 0:n], psk, [(wt[:, k, :], xnT[:, k, t0:t0 + n]) for k in range(8)],
                       reads=[wk] + xkeys(t0, n))

            sbw = {}

            def sb_weights(p):
                sbw[p] = (load_w("w_in", 2304 + 128 * p), load_w("w_in", 2816 + 128 * p), load_w("w_in", 3328 + 128 * p))

            if stop_after >= 2:
                with ExitStack() as es2:
                    def sb2(name, shape, dt):
                        return es2.enter_context(nc.sbuf_tensor(name, shape, dt))
                    QT = sb2("QT", [128, NOWN], BF16)
                    accO = sb2("accO", [128, NOWN], F32)
                    accS = sb2("accS", [128, NOWN], F32)
                    cos_r = Ring("cos", [sb2("cos%d" % i, [128, 512], F32) for i in range(2)])
                    sin_r = Ring("sin", [sb2("sin%d" % i, [128, 512], F32) for i in range(2)])
                    t1_r = Ring("t1", [sb2("t1_%d" % i, [128, 512], F32) for i in range(2)])
                    t2_r = Ring("t2", [sb2("t2_%d" % i, [128, 512], F32) for i in range(2)])
                    PT_r = Ring("PT", [sb2("PT%d" % i, [128, 512], BF16) for i in range(3)])

                    def rope_proj(wa, wak, wb_, wbk, t0, n, outview, outkey):
                        cs, ck = cos_r.next()
                        sn, sk = sin_r.next()
                        S.dma("sp", cs[:, 0:n], rcos[:, t0:t0 + n], writes=[ck])
                        S.dma("sp", sn[:, 0:n], rsin[:, t0:t0 + n], writes=[sk])
                        p1, p1k = ps6.next()
                        p2, p2k = ps6.next()
                        proj(p1, p1k, wa, wak, t0, n)
                        proj(p2, p2k, wb_, wbk, t0, n)
                        t1, t1k = t1_r.next()
                        t2, t2k = t2_r.next()
                        S.op("dve", lambda e: e.tensor_tensor(out=t1[:, 0:n], in0=p1[:, 0:n], in1=cs[:, 0:n], op=ALU.mult),
                             reads=[p1k, ck], writes=[t1k])
                        S.op("dve", lambda e: e.tensor_tensor(out=t2[:, 0:n], in0=p2[:, 0:n], in1=sn[:, 0:n], op=ALU.mult),
                             reads=[p2k, sk], writes=[t2k])
                        S.op("pool", lambda e: e.tensor_tensor(out=outview, in0=t1[:, 0:n], in1=t2[:, 0:n], op=ALU.add),
                             reads=[t1k, t2k], writes=[outkey])

                    for p in range(2):
                        S.op("pool", lambda e: e.memset(accO[:], 0.0), writes=["accO"])
                        S.op("pool", lambda e: e.memset(accS[:], 0.0), writes=["accS"])
                        for g, d in enumerate(DILS):
                            hc = 256 * g + 128 * p
                            nb = 32 // d
                            wq, wqk = load_w("w_in", hc)
                            wqs, wqsk = load_w("w_sw", hc)
                            wk_, wkk = load_w("w_in", 768 + hc)
                            wks, wksk = load_w("w_sw", 768 + hc)
                            wv, wvk = load_w("w_in", 1536 + hc)
                            kv_t0 = {1: 3, 4: 2, 16: 0}[d]
                            for t in range(kv_t0, 8):
                                rope_proj(wk_, wkk, wks, wksk, t * 512, 512, KT[:, t * 512:(t + 1) * 512], ("KT", t))
                                pv, pvk = ps6.next()
                                proj(pv, pvk, wv, wvk, t * 512, 512)
                                S.op("act", lambda e, pv=pv, t=t: e.activation(out=VT[:, t * 512:(t + 1) * 512], in_=pv[:], func=AF.Copy),
                                     reads=[pvk], writes=[("VT", t)])
                            for (o0, n) in QTILES:
                                rope_proj(wq, wqk, wqs, wqsk, OWN0 + o0, n, QT[:, o0:o0 + n], ("QT", o0))
                            allVT = [("VT", t) for t in range(kv_t0, 8)]
                            allKT = [("KT", t) for t in range(kv_t0, 8)]
                            allQT = [("QT", o0) for (o0, n) in QTILES]
                            blist = [(r, B) for r in range(d) for B in range(nb) if (r + d * 128 * B) >= kv_t0 * 512]
                            for i0 in range(0, len(blist), 8):
                                grp = blist[i0:i0 + 8]
                                pt, ptk = pT.next()
                                for j, (r, B) in enumerate(grp):
                                    a0 = r + d * 128 * B
                                    S.op("pe", lambda e, j=j, a0=a0: e.transpose(out=pt[:, j * 128:(j + 1) * 128],
                                                                                 in_=VT[:, a0:a0 + d * 127 + 1:d], identity=ident),
                                         reads=allVT + ["cst"], writes=[ptk], inc=(j == len(grp) - 1))
                                for j, (r, B) in enumerate(grp):
                                    pass
                                j = 0
                                while j < len(grp):
                                    j2 = j
                                    while j2 + 1 < len(grp) and (grp[j2 + 1][0] * nb + grp[j2 + 1][1]) == (grp[j2][0] * nb + grp[j2][1]) + 1:
                                        j2 += 1
                                    i_lo = grp[j][0] * nb + grp[j][1]
                                    cnt = j2 - j + 1
                                    S.op("dve", lambda e, j=j, i_lo=i_lo, cnt=cnt, pt=pt: e.tensor_copy(
                                        out=Vtok[:, i_lo:i_lo + cnt, :],
                                        in_=pt[:, j * 128:(j + cnt) * 128].rearrange("p (c t) -> p c t", c=cnt)),
                                        reads=[ptk], writes=["Vtok"])
                                    j = j2 + 1
                            items = []
                            for r in range(d):
                                for B in range(nb):
                                    lo = r + d * 128 * B
                                    hi = r + d * (128 * B + 127)
                                    if hi < OWN0:
                                        continue
                                    c0 = 0
                                    if lo < OWN0:
                                        c0 = (OWN0 - r + d - 1) // d - 128 * B
                                    items.append((r, B, c0, 128))
                            units = []
                            cur, tot = [], 0
                            for it in items:
                                r, B, c0, c1 = it
                                w = c1 - c0
                                need = w * (2 if B >= 1 else 1)
                                if tot + need > 512:
                                    units.append(cur)
                                    cur, tot = [], 0
                                cur.append(it)
                                tot += need
                            if cur:
                                units.append(cur)
                            if p == 1 and g == 2 and stop_after >= 3:
                                sb_weights(0)
                            work = [(ui, e_) for ui in range(len(units)) for e_ in range(2)]
                            wst = {}
                            ust = {}

                            def stageA(k):
                                ui, e_ = work[k]
                                unit = units[ui]
                                rows = slice(e_ * 64, (e_ + 1) * 64)
                                pz, pzk = ps6.next()
                                tiles = []
                                off = 0
                                calls = []
                                for ii, (r, B, c0, c1) in enumerate(unit):
                                    w = c1 - c0
                                    o0 = r + d * (128 * B + c0) - OWN0
                                    for kb, kind in ((B - 1, "prev"), (B, "cur")):
                                        if kb < 0:
                                            continue
                                        a0 = r + d * 128 * kb
                                        calls.append(lambda e, off=off, w=w, a0=a0, o0=o0, rows=rows, pz=pz: e.matmul(
                                            pz[:, off:off + w], lhsT=KT[rows, a0:a0 + d * 127 + 1:d],
                                            rhs=QT[rows, o0:o0 + d * (w - 1) + 1:d], start=True, stop=True))
                                        tiles.append((off, w, kb, kind, ii, r, c0, c1))
                                        off += w
                                tot = off
                                for ci, fn in enumerate(calls):
                                    S.op("pe", fn, reads=allKT + allQT, writes=[pzk], inc=(ci == len(calls) - 1))
                                PTt, PTk = PT_r.next()
                                S.op("act", lambda e: e.activation(out=PTt[:, 0:tot], in_=pz[:, 0:tot], func=AF.Exp, scale=0.125),
                                     reads=[pzk], writes=[PTk])
                                full = (len(unit) == 2 and all(t[1] == 128 for t in tiles) and len(tiles) == 4)
                                if full:
                                    S.op("pool", lambda e: e.tensor_tensor(out=PTt[:], in0=PTt[:], in1=mask4[:], op=ALU.mult),
                                         reads=[PTk, "cst"], writes=[PTk])
                                else:
                                    for (off, w, kb, kind, ii, r, c0, c1) in tiles:
                                        mk = m_prev if kind == "prev" else m_cur
                                        S.op("pool", lambda e, off=off, w=w, mk=mk, c0=c0, c1=c1: e.tensor_tensor(
                                            out=PTt[:, off:off + w], in0=PTt[:, off:off + w], in1=mk[:, c0:c1], op=ALU.mult),
                                            reads=[PTk, "cst"], writes=[PTk])
                                wst[k] = (tiles, PTt, PTk)

                            def stageB(k):
                                ui, e_ = work[k]
                                unit = units[ui]
                                rows = slice(e_ * 64, (e_ + 1) * 64)
                                tiles, PTt, PTk = wst.pop(k)
                                if e_ == 0:
                                    pO, pOk = ps6.next()
                                    pS, pSk = ps6.next()
                                    ust[ui] = (pO, pOk, pS, pSk)
                                pO, pOk, pS, pSk = ust[ui]
                                ioff = 0
                                for ii, (r, B, c0, c1) in enumerate(unit):
                                    w = c1 - c0
                                    its = [t for t in tiles if t[4] == ii]
                                    for n_i, (off, w_, kb, kind, _, _, _, _) in enumerate(its):
                                        vi = r * nb + kb
                                        S.op("pe", lambda e, off=off, w=w, vi=vi, ioff=ioff, n_i=n_i, its=its: e.matmul(
                                            pO[rows, ioff:ioff + w], lhsT=Vtok[:, vi, rows], rhs=PTt[:, off:off + w],
                                            start=(n_i == 0), stop=(n_i == len(its) - 1)),
                                            reads=[PTk, "Vtok"], writes=[pOk], inc=False)
                                    for n_i, (off, w_, kb, kind, _, _, _, _) in enumerate(its):
                                        fl = 0 if kb < nb // 2 else 1
                                        S.op("pe", lambda e, off=off, w=w, fl=fl, ioff=ioff, n_i=n_i, its=its: e.matmul(
                                            pS[rows, ioff:ioff + w], lhsT=onesv[:, fl, :], rhs=PTt[:, off:off + w],
                                            start=(n_i == 0), stop=(n_i == len(its) - 1)),
                                            reads=[PTk, "cst"], writes=[pSk],
                                            inc=(ii == len(unit) - 1 and n_i == len(its) - 1))
                                    ioff += w
                                if e_ == 1:
                                    ioff = 0
                                    for ii, (r, B, c0, c1) in enumerate(unit):
                                        w = c1 - c0
                                        o0 = r + d * (128 * B + c0) - OWN0
                                        S.op("dve", lambda e, o0=o0, w=w, ioff=ioff: e.tensor_tensor(
                                            out=accO[:, o0:o0 + d * (w - 1) + 1:d], in0=accO[:, o0:o0 + d * (w - 1) + 1:d], in1=pO[:, ioff:ioff + w], op=ALU.add),
                                            reads=[pOk, "accO"], writes=["accO"])
                                        S.op("dve", lambda e, o0=o0, w=w, ioff=ioff: e.tensor_tensor(
                                            out=accS[:, o0:o0 + d * (w - 1) + 1:d], in0=accS[:, o0:o0 + d * (w - 1) + 1:d], in1=pS[:, ioff:ioff + w], op=ALU.add),
                                            reads=[pSk, "accS"], writes=["accS"])
                                        ioff += w
                                    del ust[ui]

                            for k in range(len(work) + 1):
                                if k < len(work):
                                    stageA(k)
                                if k >= 1:
                                    stageB(k - 1)
                        S.op("dve", lambda e: e.tensor_scalar_add(out=accS[:], in0=accS[:], scalar1=1e-30), reads=["accS"], writes=["accS"])
                        S.op("dve", lambda e: e.reciprocal(out=accS[:], in_=accS[:]), reads=["accS"], writes=["accS"])
                        S.op("dve", lambda e, p=p: e.tensor_tensor(out=yaT[:, p, :], in0=accO[:], in1=accS[:], op=ALU.mult),
                             reads=["accO", "accS"], writes=["yaT"])
                    S.barrier()

            if stop_after >= 3:
                with ExitStack() as es3:
                    def sb3(name, shape, dt):
                        return es3.enter_context(nc.sbuf_tensor(name, shape, dt))
                    E_r = Ring("E", [sb3("E%d" % i, [128, 2, 512], F32) for i in range(4)])
                    L_r = Ring("L", [sb3("L%d" % i, [128, 2, 512], BF16) for i in range(4)])
                    G_r = Ring("G", [sb3("G%d" % i, [128, 2, 512], F32) for i in range(2)])
                    A_r = Ring("A", [sb3("A%d" % i, [128, 2, 512], BF16) for i in range(2)])
                    CC = PSALL[:, 0:1024]
                    CC3 = CC.rearrange("p (h n) -> p h n", h=2)
                    CCk = [("pb", 0), ("pb", 1)]
                    psOs = [(PB[2], ("pb", 2)), (PB[3], ("pb", 3))]
                    ZZs = [(PSALL[:, 2048:3072], [("pb", 4), ("pb", 5)]), (PSALL[:, 3072:4096], [("pT", 0), ("pT", 1)])]
                    zz_i = [0]
                    QTz = [sb3("QTz%d" % i, [128, NOWN], BF16) for i in range(2)]
                    S.op("pool", lambda e: e.memset(QTz[0][64:128, :], 0.0), writes=["QTz0"])
                    S.op("pool", lambda e: e.memset(QTz[1][0:64, :], 0.0), writes=["QTz1"])

                    if 0 not in sbw:
                        sb_weights(0)
                    for p in range(4):
                        (wq, wqk), (wk_, wkk), (wv, wvk) = sbw.pop(p)
                        if p + 1 < 4:
                            sb_weights(p + 1)
                        if p == 0:
                            precast(("w_ba", "w_bb", "w_xkv", "w_in", "w_out", "w_xq", "w_xo", "w_up", "w_down"))
                        for t in range(8):
                            pk, pkk = ps6.next()
                            proj(pk, pkk, wk_, wkk, t * 512, 512)
                            S.op("act", lambda e, pk=pk, t=t: e.activation(out=KT[:, t * 512:(t + 1) * 512], in_=pk[:], func=AF.Copy),
                                 reads=[pkk], writes=[("KT", t)])
                            pv, pvk = ps6.next()
                            proj(pv, pvk, wv, wvk, t * 512, 512)
                            S.op("dve", lambda e, pv=pv, t=t: e.tensor_copy(out=VT[:, t * 512:(t + 1) * 512], in_=pv[:]),
                                 reads=[pvk], writes=[("VT", t)])
                        for (o0, n) in QTILES:
                            pq, pqk = ps6.next()
                            proj(pq, pqk, wq, wqk, OWN0 + o0, n)
                            S.op("act", lambda e, pq=pq, o0=o0, n=n: e.activation(out=QTz[0][0:64, o0:o0 + n], in_=pq[0:64, 0:n], func=AF.Copy, scale=0.125),
                                 reads=[pqk, "QTz0"], writes=[("QT", o0)])
                            S.op("act", lambda e, pq=pq, o0=o0, n=n: e.activation(out=QTz[1][64:128, o0:o0 + n], in_=pq[64:128, 0:n], func=AF.Copy, scale=0.125),
                                 reads=[pqk, "QTz1"], writes=[("QT", o0)])
                        allVT = [("VT", t) for t in range(8)]
                        allKT = [("KT", t) for t in range(8)]
                        for i0 in range(0, 32, 8):
                            pt, ptk = pT.next()
                            for j in range(8):
                                kb = i0 + j
                                S.op("pe", lambda e, j=j, kb=kb, pt=pt: e.transpose(out=pt[:, j * 128:(j + 1) * 128],
                                                                                   in_=VT[:, kb * 128:(kb + 1) * 128], identity=ident),
                                     reads=allVT + ["cst"], writes=[ptk], inc=(j == 7))
                            S.op("dve", lambda e, i0=i0, pt=pt: e.tensor_copy(out=Vtok[:, i0:i0 + 8, :],
                                                                            in_=pt[:].rearrange("p (c t) -> p c t", c=8)),
                                 reads=[ptk], writes=["Vtok"])
                        for (o0, n) in QTILES:
                            t0 = OWN0 + o0
                            kb_hi = (t0 + n) // 128 - 1
                            kb_d0 = t0 // 128
                            qk = ("QT", o0)
                            for e_ in range(2):
                                S.op("pe", lambda e, e_=e_, n=n: e.matmul(CC[:, e_ * 512:e_ * 512 + n], lhsT=zeros_b,
                                                                         rhs=cst[:, 0:4, :].rearrange("p a b -> p (a b)")[:, 0:n],
                                                                         start=True, stop=True), reads=["cst"], writes=[CCk[e_]], inc=False)
                            for e_ in range(2):
                                Ot, Ok_ = psOs[e_]
                                S.op("pe", lambda e, n=n, Ot=Ot: e.matmul(Ot[:, 0:n], lhsT=zeros_b, rhs=cst[:, 0:4, :].rearrange("p a b -> p (a b)")[:, 0:n],
                                                                          start=True, stop=True), reads=["cst"], writes=[Ok_], inc=True)
                            steps = list(range(kb_hi, -1, -1))
                            st = {}

                            def st_z(i):
                                kb = steps[i]
                                c0 = max(0, (kb - kb_d0) * 128)
                                w = n - c0
                                zz_i[0] = (zz_i[0] + 1) % 2
                                ZZ, ZZk = ZZs[zz_i[0]]
                                for e_ in range(2):
                                    S.op("pe", lambda e, e_=e_: e.matmul(ZZ[:, e_ * 512:e_ * 512 + w], lhsT=KT[:, kb * 128:(kb + 1) * 128],
                                                                        rhs=QTz[e_][:, o0 + c0:o0 + n], start=True, stop=True),
                                         reads=[("KT", kb // 4), qk], writes=[ZZk[e_]], inc=(e_ == 1))
                                st[i] = dict(kb=kb, c0=c0, w=w, ZZ=ZZ, ZZk=ZZk, diag=(kb >= kb_d0))

                            def st_el(i):
                                s = st[i]
                                w = s["w"]
                                Et, Ek = E_r.next()
                                Lt, Lk = L_r.next()
                                s.update(Et=Et, Ek=Ek, Lt=Lt, Lk=Lk)
                                ZZ3 = s["ZZ"].rearrange("p (h n) -> p h n", h=2)
                                S.op("act", lambda e: e.activation(out=Et[:, :, 0:w], in_=ZZ3[:, :, 0:w], func=AF.Exp),
                                     reads=s["ZZk"], writes=[Ek])

                            def st_l(i):
                                s = st[i]
                                w = s["w"]
                                Et, Ek, Lt, Lk = s["Et"], s["Ek"], s["Lt"], s["Lk"]
                                S.op("act", lambda e: e.activation(out=Lt[:, :, 0:w], in_=Et[:, :, 0:w], func=AF.Ln, bias=1.0),
                                     reads=[Ek], writes=[Lk])
                                if s["diag"]:
                                    for e_ in range(2):
                                        S.op("pool", lambda e, e_=e_: e.tensor_tensor(out=Lt[:, e_, 0:128], in0=Lt[:, e_, 0:128], in1=m_strict, op=ALU.mult),
                                             reads=[Lk, "cst"], writes=[Lk])

                            def st_ctri(i):
                                s = st[i]
                                for e_ in range(2):
                                    S.op("pe", lambda e, e_=e_: e.matmul(CC[:, e_ * 512 + s["c0"]:e_ * 512 + n], lhsT=tri_incl, rhs=s["Lt"][:, e_, 0:s["w"]],
                                                                        start=False, stop=True, skip_group_check=True),
                                         reads=[s["Lk"], "cst"], writes=[CCk[e_]], inc=(e_ == 1))

                            def st_g(i):
                                s = st[i]
                                Gt, Gk = G_r.next()
                                s.update(Gt=Gt, Gk=Gk)
                                S.op("act", lambda e: e.activation(out=Gt[:, :, 0:s["w"]], in_=CC3[:, :, s["c0"]:n], func=AF.Exp, scale=-1.0),
                                     reads=CCk, writes=[Gk])

                            def st_fix(i):
                                s = st[i]
                                w = s["w"]
                                At, Ak = A_r.next()
                                s.update(At=At, Ak=Ak)
                                for e_ in range(2):
                                    S.op("pe", lambda e, e_=e_: e.matmul(CC[:, e_ * 512 + s["c0"]:e_ * 512 + n], lhsT=tri_fix, rhs=s["Lt"][:, e_, 0:w],
                                                                        start=False, stop=True, skip_group_check=True),
                                         reads=[s["Lk"], "cst"], writes=[CCk[e_]], inc=(e_ == 1))
                                S.op("dve", lambda e: e.tensor_tensor(out=At[:, :, 0:w], in0=s["Et"][:, :, 0:w], in1=s["Gt"][:, :, 0:w], op=ALU.mult),
                                     reads=[s["Ek"], s["Gk"]], writes=[Ak])
                                if s["diag"]:
                                    for e_ in range(2):
                                        S.op("pool", lambda e, e_=e_: e.tensor_tensor(out=At[:, e_, 0:128], in0=At[:, e_, 0:128], in1=m_strict, op=ALU.mult),
                                             reads=[Ak, "cst"], writes=[Ak])

                            def st_pv(i):
                                s = st[i]
                                for e_ in range(2):
                                    Ot, Ok_ = psOs[e_]
                                    S.op("pe", lambda e, e_=e_, Ot=Ot: e.matmul(Ot[:, s["c0"]:n], lhsT=Vtok[:, s["kb"], :], rhs=s["At"][:, e_, 0:s["w"]],
                                                                               start=False, stop=True, skip_group_check=True),
                                         reads=[s["Ak"], "Vtok"], writes=[Ok_], inc=True)
                                del st[i]

                            ns = len(steps)
                            ok = lambda j: 0 <= j < ns
                            for it in range(ns + 5):
                                if ok(it):
                                    st_z(it)
                                if ok(it - 1):
                                    st_el(it - 1)
                                if ok(it - 3):
                                    st_g(it - 3)
                                if ok(it - 1):
                                    st_l(it - 1)
                                if ok(it - 2):
                                    st_ctri(it - 2)
                                if ok(it - 4):
                                    st_pv(it - 4)
                                if ok(it - 3):
                                    st_fix(it - 3)
                            for e_ in range(2):
                                Ot, Ok_ = psOs[e_]
                                S.op("act", lambda e, o0=o0, n=n, p=p, Ot=Ot, e_=e_: e.activation(
                                    out=ybT[e_ * 64:(e_ + 1) * 64, p, o0:o0 + n], in_=Ot[e_ * 64:(e_ + 1) * 64, 0:n], func=AF.Copy),
                                    reads=[Ok_], writes=["ybT"])
                    S.barrier()
            S.barrier()

        if dbg:
            with ExitStack() as esd:
                dbf = esd.enter_context(nc.sbuf_tensor("dbf", [128, 4, NOWN], F32))
                S.op("dve", lambda e: e.tensor_copy(out=dbf[:, 0:2, :], in_=yaT[:]), reads=["yaT"], writes=["dbf"])
                S.dma("sp", d_ya, dbf[:, 0:2, :], reads=["dbf"])
                S.barrier()
                S.op("dve", lambda e: e.tensor_copy(out=dbf[:], in_=ybT[:]), reads=["ybT"], writes=["dbf"])
                S.dma("sp", d_yb, dbf[:], reads=["dbf"])
                S.barrier()

        if stop_after >= 4:
            with ExitStack() as es4:
                def sb4(name, shape, dt):
                    return es4.enter_context(nc.sbuf_tensor("q_" + name, shape, dt))
                gB_x = sb4("gB_x", [128, D], F32)
                gB_mem = sb4("gB_mem", [128, D], F32)
                gB_ffn = sb4("gB_ffn", [128, D], F32)
                gB_f = sb4("gB_f", [128, D], F32)
                wb_r = Ring("wb", [sb4("wb%d" % i, [128, 4096], BF16) for i in range(3)])
                H2 = [sb4("H%d" % i, [128, 4, D], F32) for i in range(2)]
                H = H2[1]
                xT = sb4("xT", [128, 8, 512], BF16)
                mT = sb4("mT", [128, 8, 512], BF16)
                QxT = mT[:, 0:4, :]
                oxT = mT[:, 4:8, :]
                ta_r = Ring("ta", [sb4("ta%d" % i, [128, 512], F32) for i in range(2)])
                tb_r = Ring("tb", [sb4("tb%d" % i, [128, 512], F32) for i in range(2)])
                KxT = sb4("KxT", [128, 4, 256], BF16)
                Vx = sb4("Vx", [128, 2, 512], BF16)
                Px_r = Ring("Px", [sb4("Px%d" % i, [128, 2, 512], BF16) for i in range(2)])
                rsum_r = Ring("rsum", [sb4("rsum%d" % i, [128, 512], F32) for i in range(1)])
                aT = sb4("aT", [128, NCH, 512], BF16)
                U_r = Ring("U", [sb4("U%d" % i, [128, 514], F32) for i in range(2)])
                tc_r = Ring("tc", [sb4("tc%d" % i, [128, 512], F32) for i in range(4)])
                sg_r = Ring("sg", [sb4("sg%d" % i, [128, 512], F32) for i in range(1)])
                carry = sb4("carry", [128, 44, 2], F32)
                bg = sb4("bg", [128, 16], F32)
                cw = sb4("cw", [128, 44, 3], F32)
                cb = sb4("cb", [128, 44], F32)
                if dbg:
                    dbuf = sb4("dbuf", [128, 2, 512], F32)

                S.dma("sp", gB_x[:], g_x.partition_broadcast(128), writes=["gB_x"])
                S.dma("sp", gB_mem[:], g_mem.partition_broadcast(128), writes=["gB_mem"])
                S.dma("sp", gB_ffn[:], g_ffn.partition_broadcast(128), writes=["gB_ffn"])
                S.dma("sp", gB_f[:], g_f.partition_broadcast(128), writes=["gB_f"])
                S.dma("sp", bg[:], b_gate, writes=["smallc"])
                S.dma("sp", cw[:], conv_w, writes=["smallc"])
                S.dma("sp", cb[:], conv_b, writes=["smallc"])
                S.op("pool", lambda e: e.memset(carry[:], 0.0), writes=["carry"])

                def load_wb(src_aps, shape3):
                    wt, wk = wb_r.next()
                    kc, nn = shape3
                    v = wt[:, 0:kc * nn].rearrange("p (k n) -> p k n", k=kc)
                    for (c0, c1, nm, src) in src_aps:
                        S.dma("sp", v[:, :, c0:c1], src, reads=skeys(nm), writes=[wk])
                    return v, wk

                wba = sb4("wba", [128, 2, 1024], BF16)
                wbb = sb4("wbb", [128, 4, 1024], BF16)
                wbak, wbbk = "wba", "wbb"
                S.dma("sp", wba[:], sw("w_ba"), reads=skeys("w_ba"), writes=[wbak])
                S.dma("sp", wbb[:], sw("w_bb"), reads=skeys("w_bb"), writes=[wbbk])
                for mb in range(2):
                    S.dma("sp", H[:, mb, :], mem[mb * 128:(mb + 1) * 128, :], writes=[("H", 1, mb)])
                    rms_to_T(H[:, mb, :], ("H", 1, mb), gB_mem, "gB_mem", xT[:, :, mb * 128:(mb + 1) * 128], "xT")
                wkv_k, wkvk_k = load_wb([(0, 512, "w_xkv", sw("w_xkv")[:, :, 0:512])], (8, 512))
                wkv_v, wkvk_v = load_wb([(0, 512, "w_xkv", sw("w_xkv")[:, :, 512:1024])], (8, 512))
                for hx in range(4):
                    ps, psk = ps6.next()
                    mm_acc(ps[:, 0:256], psk, [(wkv_k[:, k, hx * 128:(hx + 1) * 128], xT[:, k, 0:256]) for k in range(8)],
                           reads=[wkvk_k, "xT"])
                    S.op("act", lambda e, ps=ps, hx=hx: e.activation(out=KxT[:, hx, :], in_=ps[:, 0:256], func=AF.Copy),
                         reads=[psk], writes=["KxT"])
                for mb in range(2):
                    ps, psk = ps6.next()
                    mm_acc(ps[:], psk, [(xT[:, k, mb * 128:(mb + 1) * 128], wkv_v[:, k, :]) for k in range(8)],
                           reads=[wkvk_v, "xT"])
                    S.op("act", lambda e, ps=ps, mb=mb: e.activation(out=Vx[:, mb, :], in_=ps[:], func=AF.Copy),
                         reads=[psk], writes=["Vx"])

                XS = 1.0 / math.sqrt(128.0)
                def prefetch_x(ti):
                    o0_, n_ = QTILES[ti]
                    for blk in range(n_ // 128):
                        r0 = OWN0 + o0_ + blk * 128
                        S.dma("pool", H2[ti % 2][:, blk, :], xl[r0:r0 + 128, :], writes=[("H", ti % 2, blk)])

                a_done = {}

                def emit_a_front(tj):
                    o0_, n_ = QTILES[tj]
                    Hj = H2[tj % 2]
                    a_done[tj] = rms_front([(Hj[:, blk, :], ("H", tj % 2, blk), xT[:, :, blk * 128:(blk + 1) * 128], "xT")
                                            for blk in range(n_ // 128)], gB_mix, "gB_mix")

                def emit_a_back(tj):
                    rms_back(a_done[tj])

                prefetch_x(0)
                for ti, (o0, n) in enumerate(QTILES):
                    halo = (o0 == 0)
                    nblk = n // 128
                    H = H2[ti % 2]
                    hp = ti % 2
                    if ti not in a_done:
                        emit_a_front(ti)
                        emit_a_back(ti)
                    if ti + 1 < len(QTILES):
                        prefetch_x(ti + 1)
                    if stop_after < 5:
                        continue
                    for og in range(2):
                        wga, wgak = load_wb([(0, 512, "w_in", sw("w_in")[:, :, 3840 + 512 * og:3840 + 512 * (og + 1)])], (8, 512))
                        for oi in range(4):
                            o = 4 * og + oi
                            ta, tak = ta_r.next()
                            ps, psk = ps6.next()
                            mm_acc(ps[:, 0:n], psk, [(wga[:, k, oi * 128:(oi + 1) * 128], xT[:, k, 0:n]) for k in range(8)],
                                   reads=[wgak, "xT"])
                            S.op("act", lambda e, ps=ps, ta=ta, o=o: e.activation(out=ta[:, 0:n], in_=ps[:, 0:n], func=AF.Sigmoid,
                                                                                bias=bg[:, o:o + 1]),
                                 reads=[psk, "smallc"], writes=[tak])
                            ps2, ps2k = ps6.next()
                            mm_acc(ps2[:, 0:n], ps2k, [(wba[:, k, o * 128:(o + 1) * 128], yaT[:, k, o0:o0 + n]) for k in range(2)],
                                   reads=[wbak, "yaT"])
                            S.op("dve", lambda e, ps2=ps2, ta=ta: e.tensor_tensor(out=ta[:, 0:n], in0=ps2[:, 0:n], in1=ta[:, 0:n], op=ALU.mult),
                                 reads=[ps2k, tak], writes=[tak])
                            S.op("pool", lambda e, ta=ta, o=o: e.tensor_copy(out=mT[:, o, 0:n], in_=ta[:, 0:n]),
                                 reads=[tak], writes=[("mTa", o), "QxT" if o < 4 else "oxT"])
                    pend_add = []

                    def emit_add(tb, tbk, o):
                        S.op("dve", lambda e: e.tensor_tensor(out=mT[:, o, 0:n], in0=mT[:, o, 0:n], in1=tb[:, 0:n], op=ALU.add),
                             reads=[tbk, ("mTa", o)], writes=[("mTa", o)])

                    for og in range(2):
                        wgb, wgbk = load_wb([(0, 512, "w_in", sw("w_in")[:, :, 4864 + 512 * og:4864 + 512 * (og + 1)])], (8, 512))
                        for oi in range(4):
                            o = 4 * og + oi
                            tb, tbk = tb_r.next()
                            ps, psk = ps6.next()
                            mm_acc(ps[:, 0:n], psk, [(wgb[:, k, oi * 128:(oi + 1) * 128], xT[:, k, 0:n]) for k in range(8)],
                                   reads=[wgbk, "xT"])
                            S.op("act", lambda e, ps=ps, tb=tb, o=o: e.activation(out=tb[:, 0:n], in_=ps[:, 0:n], func=AF.Sigmoid,
                                                                                bias=bg[:, 8 + o:9 + o]),
                                 reads=[psk, "smallc"], writes=[tbk])
                            ps2, ps2k = ps6.next()
                            mm_acc(ps2[:, 0:n], ps2k, [(wbb[:, k, o * 128:(o + 1) * 128], ybT[:, k, o0:o0 + n]) for k in range(4)],
                                   reads=[wbbk, "ybT"])
                            S.op("dve", lambda e, ps2=ps2, tb=tb: e.tensor_tensor(out=tb[:, 0:n], in0=ps2[:, 0:n], in1=tb[:, 0:n], op=ALU.mult),
                                 reads=[ps2k, tbk], writes=[tbk])
                            if pend_add:
                                emit_add(*pend_add.pop())
                            pend_add.append((tb, tbk, o))
                    while pend_add:
                        emit_add(*pend_add.pop())
                    allm = [("mTa", o) for o in range(8)]
                    if stop_after < 6:
                        continue
                    if dbg and o0 == 128:
                        for q4 in range(4):
                            S.op("dve", lambda e, q4=q4: e.tensor_copy(out=dbuf[:], in_=mT[:, 2 * q4:2 * q4 + 2, :]), reads=allm, writes=["dbuf"])
                            S.dma("sp", d_m[:, 2 * q4:2 * q4 + 2, :], dbuf[:], reads=["dbuf"])
                        for q4 in range(4):
                            S.op("dve", lambda e, q4=q4: e.tensor_copy(out=dbuf[:], in_=xT[:, 2 * q4:2 * q4 + 2, :]), reads=["xT"], writes=["dbuf"])
                            S.dma("sp", d_x[:, 2 * q4:2 * q4 + 2, :], dbuf[:], reads=["dbuf"])
                    for half in range(2):
                        wo, wok = load_wb([(0, 512, "w_out", sw("w_out")[:, :, half * 512:(half + 1) * 512])], (8, 512))
                        for blk in range(nblk):
                            ps, psk = ps6.next()
                            mm_acc(ps[:], psk, [(mT[:, k, blk * 128:(blk + 1) * 128], wo[:, k, :]) for k in range(8)],
                                   reads=[wok] + allm)
                            S.op("dve", lambda e, ps=ps, blk=blk, half=half: e.tensor_tensor(
                                out=H[:, blk, half * 512:(half + 1) * 512], in0=H[:, blk, half * 512:(half + 1) * 512], in1=ps[:], op=ALU.add),
                                reads=[psk, ("H", hp, blk)], writes=[("H", hp, blk)])
                    if dbg and not halo:
                        for blk in range(4):
                            r0 = o0 - 128 + blk * 128
                            S.dma("sp", d_h[0, r0:r0 + 128, :], H[:, blk, :], reads=[("H", hp, blk)])
                    rms_to_T_multi([(H[:, blk, :], ("H", hp, blk), xT[:, :, blk * 128:(blk + 1) * 128], "xT") for blk in range(nblk)],
                                   gB_x, "gB_x")
                    if stop_after < 7:
                        continue
                    wxq, wxqk = load_wb([(0, 512, "w_xq", sw("w_xq"))], (8, 512))
                    for c in range(4):
                        ps, psk = ps6.next()
                        mm_acc(ps[:, 0:n], psk, [(wxq[:, k, c * 128:(c + 1) * 128], xT[:, k, 0:n]) for k in range(8)],
                               reads=[wxqk, "xT"])
                        S.op("act", lambda e, ps=ps, c=c: e.activation(out=QxT[:, c, 0:n], in_=ps[:, 0:n], func=AF.Copy),
                             reads=[psk], writes=["QxT", ("mTa", c)])
                    def xa_scores(hx):
                        Px, Pxk = Px_r.next()
                        for mb in range(2):
                            ps, psk = ps6.next()
                            S.op("pe", lambda e, ps=ps, mb=mb: e.matmul(ps[:, 0:n], lhsT=KxT[:, hx, mb * 128:(mb + 1) * 128],
                                                                       rhs=QxT[:, hx, 0:n], start=True, stop=True),
                                 reads=["KxT", "QxT"], writes=[psk])
                            S.op("act", lambda e, ps=ps, mb=mb: e.activation(out=Px[:, mb, 0:n], in_=ps[:, 0:n], func=AF.Exp, scale=XS),
                                 reads=[psk], writes=[Pxk])
                        return Px, Pxk

                    def xa_pv(hx, Px, Pxk):
                        pso, psok = ps6.next()
                        mm_acc(pso[:, 0:n], psok, [(Vx[:, mb, hx * 128:(hx + 1) * 128], Px[:, mb, 0:n]) for mb in range(2)],
                               reads=["Vx", Pxk])
                        psr, psrk = ps6.next()
                        mm_acc(psr[:, 0:n], psrk, [(ones_b, Px[:, mb, 0:n]) for mb in range(2)], reads=["cst", Pxk])
                        rsum, rsumk = rsum_r.next()
                        S.op("dve", lambda e: e.reciprocal(out=rsum[:, 0:n], in_=psr[:, 0:n]),
                             reads=[psrk], writes=[rsumk])
                        S.op("dve", lambda e: e.tensor_tensor(out=oxT[:, hx, 0:n], in0=pso[:, 0:n], in1=rsum[:, 0:n], op=ALU.mult),
                             reads=[psok, rsumk], writes=["oxT", ("mTa", 4 + hx)])

                    cur = xa_scores(0)
                    for hx in range(4):
                        nxt = xa_scores(hx + 1) if hx + 1 < 4 else None
                        xa_pv(hx, *cur)
                        cur = nxt
                    wxo, wxok = load_wb([(0, 1024, "w_xo", sw("w_xo"))], (4, 1024))
                    for half in range(2):
                        for blk in range(nblk):
                            ps, psk = ps6.next()
                            mm_acc(ps[:], psk, [(oxT[:, k, blk * 128:(blk + 1) * 128], wxo[:, k, half * 512:(half + 1) * 512]) for k in range(4)],
                                   reads=[wxok, "oxT"])
                            S.op("dve", lambda e, ps=ps, blk=blk, half=half: e.tensor_tensor(
                                out=H[:, blk, half * 512:(half + 1) * 512], in0=H[:, blk, half * 512:(half + 1) * 512], in1=ps[:], op=ALU.add),
                                reads=[psk, ("H", hp, blk)], writes=[("H", hp, blk)])
                    if dbg and not halo:
                        for blk in range(4):
                            r0 = o0 - 128 + blk * 128
                            S.dma("sp", d_h[1, r0:r0 + 128, :], H[:, blk, :], reads=[("H", hp, blk)])
                    rms_to_T_multi([(H[:, blk, :], ("H", hp, blk), xT[:, :, blk * 128:(blk + 1) * 128], "xT") for blk in range(nblk)],
                                   gB_ffn, "gB_ffn")
                    if stop_after < 8:
                        continue
                    pend = []

                    def emit_back(c, res_t):
                        (tg, tgk), (tv, tvk) = res_t
                        sg, sgk = sg_r.next()
                        S.op("act", lambda e: e.activation(out=sg[:], in_=tg[:], func=AF.Silu), reads=[tgk], writes=[sgk])
                        S.op("pool", lambda e: e.tensor_tensor(out=aT[:, c, :], in0=sg[:], in1=tv[:], op=ALU.mult),
                             reads=[sgk, tvk], writes=[("aT", c)])

                    for cg in range(11):
                        if cg == 8 and ti + 1 < len(QTILES):
                            emit_a_front(ti + 1)
                        wu, wuk = load_wb([(0, 256, "w_up", sw("w_up")[:, :, 256 * cg:256 * (cg + 1)]),
                                           (256, 512, "w_up", sw("w_up")[:, :, DFF + 256 * cg:DFF + 256 * (cg + 1)])], (8, 512))
                        for ci in range(2):
                            c = 2 * cg + ci
                            res_t = []
                            parts = []
                            for part in range(2):
                                ch = c + NCH * part
                                ps, psk = ps6.next()
                                mm_acc(ps[:, 0:n], psk, [(wu[:, k, 256 * part + ci * 128:256 * part + (ci + 1) * 128], xT[:, k, 0:n]) for k in range(8)],
                                       reads=[wuk, "xT"])
                                if halo:
                                    S.op("dve", lambda e, ps=ps, ch=ch: e.tensor_scalar_mul(out=carry[:, ch, :], in0=ps[:, n - 2:n], scalar1=hval[:, 0:1]),
                                         reads=[psk, "cst"], writes=[("carry", ch)])
                                    continue
                                U, Uk = U_r.next()
                                tcb, tck = tc_r.next()
                                parts.append((ch, ps, psk, U, Uk, tcb, tck))
                            for (ch, ps, psk, U, Uk, tcb, tck) in parts:
                                S.op("pool", lambda e, U=U, ch=ch: e.tensor_copy(out=U[:, 0:2], in_=carry[:, ch, :]),
                                     reads=[("carry", ch), "carry"], writes=[(Uk, "c")])
                                S.op("act", lambda e, U=U, ps=ps: e.activation(out=U[:, 2:514], in_=ps[:], func=AF.Copy),
                                     reads=[psk], writes=[(Uk, "d")])
                                S.op("pool", lambda e, U=U, ch=ch: e.tensor_copy(out=carry[:, ch, :], in_=U[:, 512:514]),
                                     reads=[(Uk, "d")], writes=[("carry", ch)])
                            for (ch, ps, psk, U, Uk, tcb, tck) in parts:
                                S.op("act", lambda e, U=U, tcb=tcb, ch=ch: e.activation(out=tcb[:], in_=U[:, 0:512], func=AF.Identity,
                                                                                        scale=cw[:, ch, 0:1], bias=cb[:, ch:ch + 1]),
                                     reads=[(Uk, "c"), (Uk, "d"), "smallc"], writes=[tck])
                            for (ch, ps, psk, U, Uk, tcb, tck) in parts:
                                S.op("dve", lambda e, U=U, tcb=tcb, ch=ch: e.scalar_tensor_tensor(out=tcb[:], in0=U[:, 1:513], scalar=cw[:, ch, 1:2],
                                                                                                in1=tcb[:], op0=ALU.mult, op1=ALU.add),
                                     reads=[(Uk, "c"), (Uk, "d"), tck, "smallc"], writes=[tck])
                            for (ch, ps, psk, U, Uk, tcb, tck) in parts:
                                S.op("dve", lambda e, ps=ps, tcb=tcb, ch=ch: e.scalar_tensor_tensor(out=tcb[:], in0=ps[:], scalar=cw[:, ch, 2:3],
                                                                                                 in1=tcb[:], op0=ALU.mult, op1=ALU.add),
                                     reads=[psk, tck, "smallc"], writes=[tck])
                                res_t.append((tcb, tck))
                            if halo:
                                continue
                            if pend:
                                emit_back(*pend.pop())
                            pend.append((c, res_t))
                    while pend:
                        emit_back(*pend.pop())
                    if ti + 1 < len(QTILES):
                        emit_a_back(ti + 1)
                    if halo:
                        continue
                    if stop_after < 9:
                        continue
                    alla = [("aT", c) for c in range(NCH)]
                    for half in range(2):
                        accs = [(PB[i], ("pb", i)) for i in range(4)]
                        for cgp, (c_lo, c_hi) in enumerate(((0, 8), (8, 16), (16, 22))):
                            wd, wdk = load_wb([(0, 512, "w_down", sw("w_down")[:, c_lo:c_hi, half * 512:(half + 1) * 512])], (c_hi - c_lo, 512))
                            for c in range(c_lo, c_hi):
                                for blk in range(4):
                                    ps, psk = accs[blk]
                                    S.op("pe", lambda e, ps=ps, c=c, blk=blk, wd=wd, c_lo=c_lo: e.matmul(
                                        ps[:], lhsT=aT[:, c, blk * 128:(blk + 1) * 128], rhs=wd[:, c - c_lo, :],
                                        start=(c == 0), stop=(c == NCH - 1)),
                                        reads=[wdk, ("aT", c)], writes=[psk], inc=(c == NCH - 1 or (c == c_hi - 1 and blk == 3)))
                        for blk in range(4):
                            ps, psk = accs[blk]
                            S.op("dve", lambda e, ps=ps, blk=blk, half=half: e.tensor_tensor(
                                out=H[:, blk, half * 512:(half + 1) * 512], in0=H[:, blk, half * 512:(half + 1) * 512], in1=ps[:], op=ALU.add),
                                reads=[psk, ("H", hp, blk)], writes=[("H", hp, blk)])
                    if dbg and not halo:
                        for blk in range(4):
                            r0 = o0 - 128 + blk * 128
                            S.dma("sp", d_h[2, r0:r0 + 128, :], H[:, blk, :], reads=[("H", hp, blk)])
                    fin = []
                    for blk in range(4):
                        ss, ssk = ss_r.next()
                        rs, rsk = rs_r.next()
                        S.op("act", lambda e, blk=blk, ss=ss: e.activation(out=junk[:], in_=H[:, blk, :], func=AF.Square, accum_out=ss[:]),
                             reads=[("H", hp, blk)], writes=["junk", ssk])
                        fin.append((blk, ss, ssk, rs, rsk))
                    for (blk, ss, ssk, rs, rsk) in fin:
                        S.op("act", lambda e, ss=ss, rs=rs: e.activation(out=rs[:], in_=ss[:], func=AF.Sqrt, scale=1.0 / D, bias=1e-6),
                             reads=[ssk], writes=[rsk])
                    for (blk, ss, ssk, rs, rsk) in fin:
                        S.op("dve", lambda e, rs=rs: e.reciprocal(out=rs[:], in_=rs[:]), reads=[rsk], writes=[rsk])
                    for (blk, ss, ssk, rs, rsk) in fin:
                        ob, obk = xin_r.next()
                        S.op("dve", lambda e, blk=blk, rs=rs, ob=ob: e.scalar_tensor_tensor(out=ob[:], in0=H[:, blk, :], scalar=rs[:], in1=gB_f[:],
                                                                                          op0=ALU.mult, op1=ALU.mult),
                             reads=[("H", hp, blk), rsk, "gB_f"], writes=[obk])
                        r0 = o0 - 128 + blk * 128
                        S.dma("pool", out[r0:r0 + 128, :], ob[:], reads=[obk])
                S.barrier()
        S.finish("sp")
    return nc


def _consts():
    bf = ml_dtypes.bfloat16
    k = np.arange(128)[:, None]
    q = np.arange(128)[None, :]
    cst = np.zeros((128, 8, 128), np.float32)
    cst[:, 0] = (k == q)
    cst[:, 1] = (k <= q)
    cst[:, 2] = (k >= q)
    cst[:, 3] = (k < q)
    cst[:, 4] = (k >= q)
    cst[:, 5] = (k < q)
    cst[:, 6] = 0.0
    cst[:, 7] = 1.0
    mask4 = np.concatenate([cst[:, 2], cst[:, 1], cst[:, 2], cst[:, 1]], axis=1)
    return cst.astype(bf), mask4.astype(bf)


def _rope_tables(h):
    gpos = (np.arange(NLOC) - (2048 if h == 0 else 0)).astype(np.float32)
    half = 32
    inv_freq = (1.0 / (np.float32(10000.0) ** (np.arange(half, dtype=np.float32) * np.float32(2.0 / 64)))).astype(np.float32)
    ang = (gpos[None, :] * inv_freq[:, None]).astype(np.float32)
    c = np.cos(ang).astype(np.float32)
    s = np.sin(ang).astype(np.float32)
    rc = np.concatenate([c, c, c, c], axis=0)
    rs = np.concatenate([-s, s, -s, s], axis=0)
    return np.ascontiguousarray(rc), np.ascontiguousarray(rs)


def make_in_maps(inp):
    bf = ml_dtypes.bfloat16
    f = lambda a: np.ascontiguousarray(np.asarray(a, dtype=np.float32))
    x = f(inp["x"])
    memv = f(inp["mem"])
    w_in = f(inp["w_in"][0])
    j = np.arange(1536)
    partner = np.where((j % 64) < 32, j + 32, j - 32)
    w_sw = np.ascontiguousarray(w_in[:, partner])
    cst, mask4 = _consts()
    common = {
        "w_in": w_in, "w_sw": w_sw,
        "g_mix": f(inp["ln_mix_g"][0]).reshape(1, D), "g_x": f(inp["ln_x_g"][0]).reshape(1, D),
        "g_mem": f(inp["ln_mem_g"][0]).reshape(1, D), "g_ffn": f(inp["ln_ffn_g"][0]).reshape(1, D),
        "g_f": f(inp["ln_f_g"]).reshape(1, D),
        "b_gate": np.ascontiguousarray(f(inp["b_gate"][0]).reshape(16, 128).T),
        "w_ba": f(inp["w_branch_a"][0]), "w_bb": f(inp["w_branch_b"][0]), "w_out": f(inp["w_out"][0]),
        "w_xq": f(inp["w_xq"][0]), "w_xkv": f(inp["w_xkv"][0]), "w_xo": f(inp["w_xo"][0]),
        "w_up": f(inp["w_up"][0]),
        "conv_w": np.ascontiguousarray(f(inp["conv_w"][0]).reshape(3, 44, 128).transpose(2, 1, 0)),
        "conv_b": np.ascontiguousarray(f(inp["conv_b"][0]).reshape(44, 128).T),
        "w_down": f(inp["w_down"][0]),
        "cst": cst, "mask4": mask4,
    }
    ropes = [_rope_tables(0), _rope_tables(1)]
    maps = []
    for c in range(8):
        b, h = c // 2, c % 2
        if h == 0:
            xl = np.zeros((NLOC, D), np.float32)
            xl[2048:] = x[b, :2048]
        else:
            xl = np.ascontiguousarray(x[b])
        onesv = np.ones((128, 2, 64), np.float32)
        onesv[:, 0, :] = float(h)
        m = dict(common)
        m.update({"xl": xl, "mem": np.ascontiguousarray(memv[b]), "rcos": ropes[h][0], "rsin": ropes[h][1],
                  "onesv": onesv.astype(bf), "hval": np.full((128, 1), float(h), np.float32)})
        maps.append(m)
    return maps


_NC_CACHE = {}


def kernel(**inputs):
    if "nc" not in _NC_CACHE:
        _NC_CACHE["nc"] = build()
    nc = _NC_CACHE["nc"]
    maps = make_in_maps(inputs)
    res = run_bass_kernel_spmd(nc, maps, core_ids=list(range(8)))
    outp = np.zeros((4, 4096, D), np.float32)
    for c in range(8):
        b, h = c // 2, c % 2
        outp[b, 2048 * h:2048 * (h + 1)] = np.asarray(res.results[c]["out"], dtype=np.float32)
    return outp
```

```python
import math
from contextlib import ExitStack
import numpy as np
import ml_dtypes
import concourse.bass as bass
import concourse.mybir as mybir
from concourse.bass_utils import run_bass_kernel_spmd

F32 = mybir.dt.float32
BF16 = mybir.dt.bfloat16
AF = mybir.ActivationFunctionType
ALU = mybir.AluOpType

D = 1024
NLOC = 4096
OWN0 = 1920
NOWN = NLOC - OWN0
DFF = 2816
NCH = 22
QTILES = [(0, 128)] + [(128 + 512 * j, 512) for j in range(4)]
DILS = (1, 4, 16)


class _Eng:
    def __init__(self, name, eng, sem):
        self.name = name
        self.eng = eng
        self.sem = sem
        self.count = 0
        self.known = {}
        self.maxwait = 0


class Sched:
    def __init__(self, nc, sems, n_dma_slots):
        self.nc = nc
        self.E = {}
        it = iter(sems)
        for name, eng in (("pe", nc.tensor), ("act", nc.scalar), ("dve", nc.vector),
                          ("pool", nc.gpsimd), ("sp", nc.sync)):
            self.E[name] = _Eng(name, eng, next(it))
        self.slots = {"sp": [], "pool": []}
        for i in range(n_dma_slots):
            s = _Eng("dma%d" % i, None, next(it))
            self.E[s.name] = s
            self.slots["sp" if i < n_dma_slots // 2 else "pool"].append(s)
        self.slot_rr = {"sp": 0, "pool": 0}
        self.res = {}

    def _collect(self, reads, writes):
        deps = set()
        for r in reads:
            st = self.res.get(r)
            if st and st[0]:
                deps.add(st[0])
        for w in writes:
            st = self.res.get(w)
            if st:
                if st[0]:
                    deps.add(st[0])
                deps.update(st[1])
        return deps

    def _wait(self, E, deps):
        need = {}
        for (en, idx) in deps:
            if en == E.name and en == "pe":
                continue
            need[en] = max(need.get(en, 0), idx)
        for en, idx in need.items():
            if E.known.get(en, 0) >= idx:
                continue
            Dp = self.E[en]
            E.eng.wait_ge(Dp.sem, idx)
            Dp.maxwait = max(Dp.maxwait, idx)
            E.known[en] = idx

    def _record(self, who, reads, writes):
        for r in reads:
            st = self.res.setdefault(r, [None, []])
            st[1].append(who)
            if len(st[1]) > 24:
                last = {}
                for (en, idx) in st[1]:
                    last[en] = max(last.get(en, 0), idx)
                st[1] = list(last.items())
        for w in writes:
            self.res[w] = [who, []]

    def op(self, engname, fn, reads=(), writes=(), inc=True):
        E = self.E[engname]
        self._wait(E, self._collect(reads, writes))
        ins = fn(E.eng)
        if inc:
            E.count += 1
            ins.then_inc(E.sem, 1)
            who = (engname, E.count)
        else:
            assert engname == "pe"
            who = (engname, E.count + 1)
        self._record(who, reads, writes)
        return ins

    def dma(self, qname, out, in_, reads=(), writes=()):
        Q = self.E[qname]
        S = self.slots[qname][self.slot_rr[qname]]
        self.slot_rr[qname] = (self.slot_rr[qname] + 1) % len(self.slots[qname])
        deps = self._collect(reads, writes)
        if S.count:
            deps.add((S.name, S.count))
        self._wait(Q, deps)
        ins = Q.eng.dma_start(out=out, in_=in_)
        S.count += 16
        ins.then_inc(S.sem, 16)
        self._record((S.name, S.count), reads, writes)
        return ins

    def barrier(self):
        all_deps = set()
        for name, Dp in self.E.items():
            if Dp.count:
                all_deps.add((name, Dp.count))
        for en in ("pe", "act", "dve", "pool", "sp"):
            self._wait(self.E[en], all_deps)
        self.res = {}

    def finish(self, engname="sp"):
        E = self.E[engname]
        deps = {(n, Dp.count) for n, Dp in self.E.items() if Dp.count and n != engname}
        self._wait(E, deps)
        for n, Dp in self.E.items():
            assert Dp.maxwait <= Dp.count, (n, Dp.maxwait, Dp.count)


class Ring:
    def __init__(self, name, tensors):
        self.name = name
        self.t = tensors
        self.i = -1

    def next(self):
        self.i = (self.i + 1) % len(self.t)
        return self.t[self.i], (self.name, self.i)


def build(dbg=False, stop_after=99):
    nc = bass.Bass("TRN2", target_bir_lowering=False)

    def din(name, shape, dt=F32):
        return nc.dram_tensor(name, shape, dt, kind="ExternalInput").ap()

    xl = din("xl", [NLOC, D])
    mem = din("mem", [256, D])
    w_in = din("w_in", [D, 5888])
    w_sw = din("w_sw", [D, 1536])
    g_mix = din("g_mix", [1, D])
    g_x = din("g_x", [1, D])
    g_mem = din("g_mem", [1, D])
    g_ffn = din("g_ffn", [1, D])
    g_f = din("g_f", [1, D])
    b_gate = din("b_gate", [128, 16])
    w_ba = din("w_ba", [256, D])
    w_bb = din("w_bb", [512, D])
    w_out = din("w_out", [D, D])
    w_xq = din("w_xq", [D, 512])
    w_xkv = din("w_xkv", [D, 1024])
    w_xo = din("w_xo", [512, D])
    w_up = din("w_up", [D, 2 * DFF])
    conv_w = din("conv_w", [128, 44, 3])
    conv_b = din("conv_b", [128, 44])
    w_down = din("w_down", [DFF, D])
    rcos = din("rcos", [128, NLOC])
    rsin = din("rsin", [128, NLOC])
    cst_d = din("cst", [128, 8, 128], BF16)
    mask4_d = din("mask4", [128, 512], BF16)
    onesv_d = din("onesv", [128, 2, 64], BF16)
    hval_d = din("hval", [128, 1])
    out = nc.dram_tensor("out", [2048, D], F32, kind="ExternalOutput").ap()
    if dbg:
        d_ya = nc.dram_tensor("d_ya", [128, 2, NOWN], F32, kind="ExternalOutput").ap()
        d_yb = nc.dram_tensor("d_yb", [128, 4, NOWN], F32, kind="ExternalOutput").ap()
        d_h = nc.dram_tensor("d_h", [3, 2048, D], F32, kind="ExternalOutput").ap()
        d_m = nc.dram_tensor("d_m", [128, 8, 512], F32, kind="ExternalOutput").ap()
        d_x = nc.dram_tensor("d_x", [128, 8, 512], F32, kind="ExternalOutput").ap()

    def wview(w):
        return w.rearrange("(kc p) n -> p kc n", p=128)

    SCR = {}
    for nm, src in (("w_in", w_in), ("w_sw", w_sw), ("w_ba", w_ba), ("w_bb", w_bb), ("w_xkv", w_xkv), ("w_out", w_out),
                    ("w_xq", w_xq), ("w_xo", w_xo), ("w_up", w_up), ("w_down", w_down)):
        R_, C_ = src.shape
        SCR[nm] = (src, nc.dram_tensor("scr_" + nm, [R_, C_], BF16, kind="Internal").ap(), [])

    with ExitStack() as es:
        def sb(name, shape, dt):
            return es.enter_context(nc.sbuf_tensor("s_" + name, shape, dt))

        NSLOT = 32
        sems = [es.enter_context(nc.semaphore("s%d" % i)) for i in range(5 + NSLOT)]
        PSALL = es.enter_context(nc.psum_tensor("psall", [128, 4096], F32))
        PB = [PSALL[:, 512 * i:512 * (i + 1)] for i in range(6)]
        pT = Ring("pT", [PSALL[:, 512 * (6 + i):512 * (7 + i)].bitcast(BF16) for i in range(2)])
        ps6 = Ring("pb", PB)
        cst = sb("cst", [128, 8, 128], BF16)
        mask4 = sb("mask4", [128, 512], BF16)
        onesv = sb("onesv", [128, 2, 64], BF16)
        hval = sb("hval", [128, 1], F32)
        gB_mix = sb("gB_mix", [128, D], F32)
        yaT = sb("yaT", [128, 2, NOWN], BF16)
        ybT = sb("ybT", [128, 4, NOWN], BF16)
        junk = sb("junk", [128, D], BF16)
        xn_r = Ring("xn", [sb("xn%d" % i, [128, D], BF16) for i in range(4)])
        ss_r = Ring("ss", [sb("ss%d" % i, [128, 1], F32) for i in range(8)])
        rs_r = Ring("rs", [sb("rs%d" % i, [128, 1], F32) for i in range(8)])
        xin_r = Ring("xin", [sb("xin%d" % i, [128, D], F32) for i in range(2)])

        block = es.enter_context(nc.Block())
        S = Sched(nc, sems, NSLOT)

        ident = cst[:, 0, :]
        m_cur = cst[:, 1, :]
        m_prev = cst[:, 2, :]
        m_strict = cst[:, 3, :]
        tri_incl = cst[:, 4, :]
        tri_fix = cst[:, 5, :]
        zeros_b = cst[:, 6, :]
        ones_b = cst[:, 7, :]

        S.dma("sp", cst[:], cst_d, writes=["cst"])
        S.dma("sp", mask4[:], mask4_d, writes=["cst"])
        S.dma("sp", onesv[:], onesv_d, writes=["cst"])
        S.dma("sp", hval[:], hval_d, writes=["cst"])
        S.dma("sp", gB_mix[:], g_mix.partition_broadcast(128), writes=["gB_mix"])

        def precast(names):
            for nm in names:
                src, dst, keys = SCR[nm]
                R_ = src.shape[0]
                c_lo = 3840 if nm == "w_in" else 0
                step = 256 if R_ >= 1024 else R_
                for r0 in range(0, R_, step):
                    S.dma("pool", dst[r0:r0 + step, c_lo:], src[r0:r0 + step, c_lo:], writes=[("scr", nm, r0)])
                    keys.append(("scr", nm, r0))

        def sw(nm):
            return wview(SCR[nm][1])

        def skeys(nm):
            return list(SCR[nm][2])

        def rms_front(items, gB, gkey):
            st = []
            for (x_ap, x_key, outview, outkey) in items:
                ss, ssk = ss_r.next()
                rs, rsk = rs_r.next()
                S.op("act", lambda e, x_ap=x_ap, ss=ss: e.activation(out=junk[:], in_=x_ap, func=AF.Square, accum_out=ss[:]),
                     reads=[x_key], writes=["junk", ssk])
                st.append([x_ap, x_key, outview, outkey, ss, ssk, rs, rsk])
            for t in st:
                x_ap, x_key, outview, outkey, ss, ssk, rs, rsk = t
                S.op("act", lambda e, ss=ss, rs=rs: e.activation(out=rs[:], in_=ss[:], func=AF.Sqrt, scale=1.0 / D, bias=1e-6),
                     reads=[ssk], writes=[rsk])
            for t in st:
                x_ap, x_key, outview, outkey, ss, ssk, rs, rsk = t
                S.op("dve", lambda e, rs=rs: e.reciprocal(out=rs[:], in_=rs[:]), reads=[rsk], writes=[rsk])
            fr = []
            for t in st:
                x_ap, x_key, outview, outkey, ss, ssk, rs, rsk = t
                xn, xnk = xn_r.next()
                S.op("dve", lambda e, x_ap=x_ap, rs=rs, xn=xn: e.scalar_tensor_tensor(out=xn[:], in0=x_ap, scalar=rs[:], in1=gB[:],
                                                                                  op0=ALU.mult, op1=ALU.mult),
                     reads=[x_key, rsk, gkey], writes=[xnk])
                fr.append((xn, xnk, outview, outkey))
            return fr

        def rms_back(fr):
            for (xn, xnk, outview, outkey) in fr:
                pt, ptk = pT.next()
                for k in range(8):
                    S.op("pe", lambda e, k=k, pt=pt, xn=xn: e.transpose(out=pt[:, k * 128:(k + 1) * 128], in_=xn[:, k * 128:(k + 1) * 128],
                                                                      identity=ident),
                         reads=[xnk, "cst"], writes=[ptk], inc=(k == 7))
                S.op("act", lambda e, outview=outview, pt=pt: e.activation(out=outview, in_=pt[:].rearrange("p (k t) -> p k t", k=8),
                                                                         func=AF.Copy),
                     reads=[ptk], writes=[outkey])

        def rms_to_T_multi(items, gB, gkey):
            rms_back(rms_front(items, gB, gkey))

        def rms_to_T(x_ap, x_key, gB, gkey, outview, outkey):
            rms_to_T_multi([(x_ap, x_key, outview, outkey)], gB, gkey)

        def mm_acc(ps_ap, pskey, pairs, reads, inc_last=True, first_start=True):
            n = len(pairs)
            for i, (l, r) in enumerate(pairs):
                S.op("pe", lambda e, l=l, r=r, i=i: e.matmul(ps_ap, lhsT=l, rhs=r, start=(i == 0 and first_start),
                                                             stop=(i == n - 1)),
                     reads=reads, writes=[pskey], inc=(inc_last and i == n - 1))

        with ExitStack() as es1:
            xnT = es1.enter_context(nc.sbuf_tensor("xnT", [128, 8, NLOC], BF16))

            with ExitStack() as esp1:
                xs_r = Ring("xs", [esp1.enter_context(nc.sbuf_tensor("xs%d" % i, [128, D], F32)) for i in range(8)])
                xn1_r = Ring("xn1", [esp1.enter_context(nc.sbuf_tensor("xn1_%d" % i, [128, D], BF16)) for i in range(4)])
                xn_save = xn_r.t, xn_r.name
                xn_r.t, xn_r.name = xn_r.t + xn1_r.t, "xnw"
                for b0 in range(0, 32, 4):
                    its = []
                    for blk in range(b0, b0 + 4):
                        xin, xk = xs_r.next()
                        S.dma("sp", xin[:], xl[blk * 128:(blk + 1) * 128, :], writes=[xk])
                        its.append((xin[:], xk, xnT[:, :, blk * 128:(blk + 1) * 128], ("xnT", blk // 4)))
                    rms_to_T_multi(its, gB_mix, "gB_mix")
                S.barrier()
                xn_r.t, xn_r.name = xn_save
                xn_r.i = -1

            wbs = Ring("wbs", [es1.enter_context(nc.sbuf_tensor("wbs%d" % i, [128, 8, 128], BF16)) for i in range(8)])
            KT = es1.enter_context(nc.sbuf_tensor("KT", [128, NLOC], BF16))
            VT = es1.enter_context(nc.sbuf_tensor("VT", [128, NLOC], BF16))
            Vtok = es1.enter_context(nc.sbuf_tensor("Vtok", [128, 32, 128], BF16))

            def xkeys(t0, n):
                return [("xnT", t) for t in range(t0 // 512, (t0 + n - 1) // 512 + 1)]

            def load_w(nm, c0):
                wt, wk = wbs.next()
                S.dma("pool", wt[:], wview(SCR[nm][0])[:, :, c0:c0 + 128], writes=[wk])
                return wt, wk

            def proj(ps, psk, wt, wk, t0, n):
                mm_acc(ps[:, 0:n], psk, [(wt[:, k, :], xnT[:, k, t0:t0 + n]) for k in range(8)],
                       reads=[wk] + xkeys(t0, n))

            sbw = {}

            def sb_weights(p):
                sbw[p] = (load_w("w_in", 2304 + 128 * p), load_w("w_in", 2816 + 128 * p), load_w("w_in", 3328 + 128 * p))

            if stop_after >= 2:
                with ExitStack() as es2:
                    def sb2(name, shape, dt):
                        return es2.enter_context(nc.sbuf_tensor(name, shape, dt))
                    QT = sb2("QT", [128, NOWN], BF16)
                    accO = sb2("accO", [128, NOWN], F32)
                    accS = sb2("accS", [128, NOWN], F32)
                    cos_r = Ring("cos", [sb2("cos%d" % i, [128, 512], F32) for i in range(2)])
                    sin_r = Ring("sin", [sb2("sin%d" % i, [128, 512], F32) for i in range(2)])
                    t1_r = Ring("t1", [sb2("t1_%d" % i, [128, 512], F32) for i in range(2)])
                    t2_r = Ring("t2", [sb2("t2_%d" % i, [128, 512], F32) for i in range(2)])
                    PT_r = Ring("PT", [sb2("PT%d" % i, [128, 512], BF16) for i in range(3)])

                    def rope_proj(wa, wak, wb_, wbk, t0, n, outview, outkey):
                        cs, ck = cos_r.next()
                        sn, sk = sin_r.next()
                        S.dma("sp", cs[:, 0:n], rcos[:, t0:t0 + n], writes=[ck])
                        S.dma("sp", sn[:, 0:n], rsin[:, t0:t0 + n], writes=[sk])
                        p1, p1k = ps6.next()
                        p2, p2k = ps6.next()
                        proj(p1, p1k, wa, wak, t0, n)
                        proj(p2, p2k, wb_, wbk, t0, n)
                        t1, t1k = t1_r.next()
                        t2, t2k = t2_r.next()
                        S.op("dve", lambda e: e.tensor_tensor(out=t1[:, 0:n], in0=p1[:, 0:n], in1=cs[:, 0:n], op=ALU.mult),
                             reads=[p1k, ck], writes=[t1k])
                        S.op("dve", lambda e: e.tensor_tensor(out=t2[:, 0:n], in0=p2[:, 0:n], in1=sn[:, 0:n], op=ALU.mult),
                             reads=[p2k, sk], writes=[t2k])
                        S.op("pool", lambda e: e.tensor_tensor(out=outview, in0=t1[:, 0:n], in1=t2[:, 0:n], op=ALU.add),
                             reads=[t1k, t2k], writes=[outkey])

                    for p in range(2):
                        S.op("pool", lambda e: e.memset(accO[:], 0.0), writes=["accO"])
                        S.op("pool", lambda e: e.memset(accS[:], 0.0), writes=["accS"])
                        for g, d in enumerate(DILS):
                            hc = 256 * g + 128 * p
                            nb = 32 // d
                            wq, wqk = load_w("w_in", hc)
                            wqs, wqsk = load_w("w_sw", hc)
                            wk_, wkk = load_w("w_in", 768 + hc)
                            wks, wksk = load_w("w_sw", 768 + hc)
                            wv, wvk = load_w("w_in", 1536 + hc)
                            kv_t0 = {1: 3, 4: 2, 16: 0}[d]
                            for t in range(kv_t0, 8):
                                rope_proj(wk_, wkk, wks, wksk, t * 512, 512, KT[:, t * 512:(t + 1) * 512], ("KT", t))
                                pv, pvk = ps6.next()
                                proj(pv, pvk, wv, wvk, t * 512, 512)
                                S.op("act", lambda e, pv=pv, t=t: e.activation(out=VT[:, t * 512:(t + 1) * 512], in_=pv[:], func=AF.Copy),
                                     reads=[pvk], writes=[("VT", t)])
                            for (o0, n) in QTILES:
                                rope_proj(wq, wqk, wqs, wqsk, OWN0 + o0, n, QT[:, o0:o0 + n], ("QT", o0))
                            allVT = [("VT", t) for t in range(kv_t0, 8)]
                            allKT = [("KT", t) for t in range(kv_t0, 8)]
                            allQT = [("QT", o0) for (o0, n) in QTILES]
                            blist = [(r, B) for r in range(d) for B in range(nb) if (r + d * 128 * B) >= kv_t0 * 512]
                            for i0 in range(0, len(blist), 8):
                                grp = blist[i0:i0 + 8]
                                pt, ptk = pT.next()
                                for j, (r, B) in enumerate(grp):
                                    a0 = r + d * 128 * B
                                    S.op("pe", lambda e, j=j, a0=a0: e.transpose(out=pt[:, j * 128:(j + 1) * 128],
                                                                                 in_=VT[:, a0:a0 + d * 127 + 1:d], identity=ident),
                                         reads=allVT + ["cst"], writes=[ptk], inc=(j == len(grp) - 1))
                                for j, (r, B) in enumerate(grp):
                                    pass
                                j = 0
                                while j < len(grp):
                                    j2 = j
                                    while j2 + 1 < len(grp) and (grp[j2 + 1][0] * nb + grp[j2 + 1][1]) == (grp[j2][0] * nb + grp[j2][1]) + 1:
                                        j2 += 1
                                    i_lo = grp[j][0] * nb + grp[j][1]
                                    cnt = j2 - j + 1
                                    S.op("dve", lambda e, j=j, i_lo=i_lo, cnt=cnt, pt=pt: e.tensor_copy(
                                        out=Vtok[:, i_lo:i_lo + cnt, :],
                                        in_=pt[:, j * 128:(j + cnt) * 128].rearrange("p (c t) -> p c t", c=cnt)),
                                        reads=[ptk], writes=["Vtok"])
                                    j = j2 + 1
                            items = []
                            for r in range(d):
                                for B in range(nb):
                                    lo = r + d * 128 * B
                                    hi = r + d * (128 * B + 127)
                                    if hi < OWN0:
                                        continue
                                    c0 = 0
                                    if lo < OWN0:
                                        c0 = (OWN0 - r + d - 1) // d - 128 * B
                                    items.append((r, B, c0, 128))
                            units = []
                            cur, tot = [], 0
                            for it in items:
                                r, B, c0, c1 = it
                                w = c1 - c0
                                need = w * (2 if B >= 1 else 1)
                                if tot + need > 512:
                                    units.append(cur)
                                    cur, tot = [], 0
                                cur.append(it)
                                tot += need
                            if cur:
                                units.append(cur)
                            if p == 1 and g == 2 and stop_after >= 3:
                                sb_weights(0)
                            work = [(ui, e_) for ui in range(len(units)) for e_ in range(2)]
                            wst = {}
                            ust = {}

                            def stageA(k):
                                ui, e_ = work[k]
                                unit = units[ui]
                                rows = slice(e_ * 64, (e_ + 1) * 64)
                                pz, pzk = ps6.next()
                                tiles = []
                                off = 0
                                calls = []
                                for ii, (r, B, c0, c1) in enumerate(unit):
                                    w = c1 - c0
                                    o0 = r + d * (128 * B + c0) - OWN0
                                    for kb, kind in ((B - 1, "prev"), (B, "cur")):
                                        if kb < 0:
                                            continue
                                        a0 = r + d * 128 * kb
                                        calls.append(lambda e, off=off, w=w, a0=a0, o0=o0, rows=rows, pz=pz: e.matmul(
                                            pz[:, off:off + w], lhsT=KT[rows, a0:a0 + d * 127 + 1:d],
                                            rhs=QT[rows, o0:o0 + d * (w - 1) + 1:d], start=True, stop=True))
                                        tiles.append((off, w, kb, kind, ii, r, c0, c1))
                                        off += w
                                tot = off
                                for ci, fn in enumerate(calls):
                                    S.op("pe", fn, reads=allKT + allQT, writes=[pzk], inc=(ci == len(calls) - 1))
                                PTt, PTk = PT_r.next()
                                S.op("act", lambda e: e.activation(out=PTt[:, 0:tot], in_=pz[:, 0:tot], func=AF.Exp, scale=0.125),
                                     reads=[pzk], writes=[PTk])
                                full = (len(unit) == 2 and all(t[1] == 128 for t in tiles) and len(tiles) == 4)
                                if full:
                                    S.op("pool", lambda e: e.tensor_tensor(out=PTt[:], in0=PTt[:], in1=mask4[:], op=ALU.mult),
                                         reads=[PTk, "cst"], writes=[PTk])
                                else:
                                    for (off, w, kb, kind, ii, r, c0, c1) in tiles:
                                        mk = m_prev if kind == "prev" else m_cur
                                        S.op("pool", lambda e, off=off, w=w, mk=mk, c0=c0, c1=c1: e.tensor_tensor(
                                            out=PTt[:, off:off + w], in0=PTt[:, off:off + w], in1=mk[:, c0:c1], op=ALU.mult),
                                            reads=[PTk, "cst"], writes=[PTk])
                                wst[k] = (tiles, PTt, PTk)

                            def stageB(k):
                                ui, e_ = work[k]
                                unit = units[ui]
                                rows = slice(e_ * 64, (e_ + 1) * 64)
                                tiles, PTt, PTk = wst.pop(k)
                                if e_ == 0:
                                    pO, pOk = ps6.next()
                                    pS, pSk = ps6.next()
                                    ust[ui] = (pO, pOk, pS, pSk)
                                pO, pOk, pS, pSk = ust[ui]
                                ioff = 0
                                for ii, (r, B, c0, c1) in enumerate(unit):
                                    w = c1 - c0
                                    its = [t for t in tiles if t[4] == ii]
                                    for n_i, (off, w_, kb, kind, _, _, _, _) in enumerate(its):
                                        vi = r * nb + kb
                                        S.op("pe", lambda e, off=off, w=w, vi=vi, ioff=ioff, n_i=n_i, its=its: e.matmul(
                                            pO[rows, ioff:ioff + w], lhsT=Vtok[:, vi, rows], rhs=PTt[:, off:off + w],
                                            start=(n_i == 0), stop=(n_i == len(its) - 1)),
                                            reads=[PTk, "Vtok"], writes=[pOk], inc=False)
                                    for n_i, (off, w_, kb, kind, _, _, _, _) in enumerate(its):
                                        fl = 0 if kb < nb // 2 else 1
                                        S.op("pe", lambda e, off=off, w=w, fl=fl, ioff=ioff, n_i=n_i, its=its: e.matmul(
                                            pS[rows, ioff:ioff + w], lhsT=onesv[:, fl, :], rhs=PTt[:, off:off + w],
                                            start=(n_i == 0), stop=(n_i == len(its) - 1)),
                                            reads=[PTk, "cst"], writes=[pSk],
                                            inc=(ii == len(unit) - 1 and n_i == len(its) - 1))
                                    ioff += w
                                if e_ == 1:
                                    ioff = 0
                                    for ii, (r, B, c0, c1) in enumerate(unit):
                                        w = c1 - c0
                                        o0 = r + d * (128 * B + c0) - OWN0
                                        S.op("dve", lambda e, o0=o0, w=w, ioff=ioff: e.tensor_tensor(
                                            out=accO[:, o0:o0 + d * (w - 1) + 1:d], in0=accO[:, o0:o0 + d * (w - 1) + 1:d], in1=pO[:, ioff:ioff + w], op=ALU.add),
                                            reads=[pOk, "accO"], writes=["accO"])
                                        S.op("dve", lambda e, o0=o0, w=w, ioff=ioff: e.tensor_tensor(
                                            out=accS[:, o0:o0 + d * (w - 1) + 1:d], in0=accS[:, o0:o0 + d * (w - 1) + 1:d], in1=pS[:, ioff:ioff + w], op=ALU.add),
                                            reads=[pSk, "accS"], writes=["accS"])
                                        ioff += w
                                    del ust[ui]

                            for k in range(len(work) + 1):
                                if k < len(work):
                                    stageA(k)
                                if k >= 1:
                                    stageB(k - 1)
                        S.op("dve", lambda e: e.tensor_scalar_add(out=accS[:], in0=accS[:], scalar1=1e-30), reads=["accS"], writes=["accS"])
                        S.op("dve", lambda e: e.reciprocal(out=accS[:], in_=accS[:]), reads=["accS"], writes=["accS"])
                        S.op("dve", lambda e, p=p: e.tensor_tensor(out=yaT[:, p, :], in0=accO[:], in1=accS[:], op=ALU.mult),
                             reads=["accO", "accS"], writes=["yaT"])
                    S.barrier()

            if stop_after >= 3:
                with ExitStack() as es3:
                    def sb3(name, shape, dt):
                        return es3.enter_context(nc.sbuf_tensor(name, shape, dt))
                    E_r = Ring("E", [sb3("E%d" % i, [128, 2, 512], F32) for i in range(4)])
                    L_r = Ring("L", [sb3("L%d" % i, [128, 2, 512], BF16) for i in range(4)])
                    G_r = Ring("G", [sb3("G%d" % i, [128, 2, 512], F32) for i in range(2)])
                    A_r = Ring("A", [sb3("A%d" % i, [128, 2, 512], BF16) for i in range(2)])
                    CC = PSALL[:, 0:1024]
                    CC3 = CC.rearrange("p (h n) -> p h n", h=2)
                    CCk = [("pb", 0), ("pb", 1)]
                    psOs = [(PB[2], ("pb", 2)), (PB[3], ("pb", 3))]
                    ZZs = [(PSALL[:, 2048:3072], [("pb", 4), ("pb", 5)]), (PSALL[:, 3072:4096], [("pT", 0), ("pT", 1)])]
                    zz_i = [0]
                    QTz = [sb3("QTz%d" % i, [128, NOWN], BF16) for i in range(2)]
                    S.op("pool", lambda e: e.memset(QTz[0][64:128, :], 0.0), writes=["QTz0"])
                    S.op("pool", lambda e: e.memset(QTz[1][0:64, :], 0.0), writes=["QTz1"])

                    if 0 not in sbw:
                        sb_weights(0)
                    for p in range(4):
                        (wq, wqk), (wk_, wkk), (wv, wvk) = sbw.pop(p)
                        if p + 1 < 4:
                            sb_weights(p + 1)
                        if p == 0:
                            precast(("w_ba", "w_bb", "w_xkv", "w_in", "w_out", "w_xq", "w_xo", "w_up", "w_down"))
                        for t in range(8):
                            pk, pkk = ps6.next()
                            proj(pk, pkk, wk_, wkk, t * 512, 512)
                            S.op("act", lambda e, pk=pk, t=t: e.activation(out=KT[:, t * 512:(t + 1) * 512], in_=pk[:], func=AF.Copy),
                                 reads=[pkk], writes=[("KT", t)])
                            pv, pvk = ps6.next()
                            proj(pv, pvk, wv, wvk, t * 512, 512)
                            S.op("dve", lambda e, pv=pv, t=t: e.tensor_copy(out=VT[:, t * 512:(t + 1) * 512], in_=pv[:]),
                                 reads=[pvk], writes=[("VT", t)])
                        for (o0, n) in QTILES:
                            pq, pqk = ps6.next()
                            proj(pq, pqk, wq, wqk, OWN0 + o0, n)
                            S.op("act", lambda e, pq=pq, o0=o0, n=n: e.activation(out=QTz[0][0:64, o0:o0 + n], in_=pq[0:64, 0:n], func=AF.Copy, scale=0.125),
                                 reads=[pqk, "QTz0"], writes=[("QT", o0)])
                            S.op("act", lambda e, pq=pq, o0=o0, n=n: e.activation(out=QTz[1][64:128, o0:o0 + n], in_=pq[64:128, 0:n], func=AF.Copy, scale=0.125),
                                 reads=[pqk, "QTz1"], writes=[("QT", o0)])
                        allVT = [("VT", t) for t in range(8)]
                        allKT = [("KT", t) for t in range(8)]
                        for i0 in range(0, 32, 8):
                            pt, ptk = pT.next()
                            for j in range(8):
                                kb = i0 + j
                                S.op("pe", lambda e, j=j, kb=kb, pt=pt: e.transpose(out=pt[:, j * 128:(j + 1) * 128],
                                                                                   in_=VT[:, kb * 128:(kb + 1) * 128], identity=ident),
                                     reads=allVT + ["cst"], writes=[ptk], inc=(j == 7))
                            S.op("dve", lambda e, i0=i0, pt=pt: e.tensor_copy(out=Vtok[:, i0:i0 + 8, :],
                                                                            in_=pt[:].rearrange("p (c t) -> p c t", c=8)),
                                 reads=[ptk], writes=["Vtok"])
                        for (o0, n) in QTILES:
                            t0 = OWN0 + o0
                            kb_hi = (t0 + n) // 128 - 1
                            kb_d0 = t0 // 128
                            qk = ("QT", o0)
                            for e_ in range(2):
                                S.op("pe", lambda e, e_=e_, n=n: e.matmul(CC[:, e_ * 512:e_ * 512 + n], lhsT=zeros_b,
                                                                         rhs=cst[:, 0:4, :].rearrange("p a b -> p (a b)")[:, 0:n],
                                                                         start=True, stop=True), reads=["cst"], writes=[CCk[e_]], inc=False)
                            for e_ in range(2):
                                Ot, Ok_ = psOs[e_]
                                S.op("pe", lambda e, n=n, Ot=Ot: e.matmul(Ot[:, 0:n], lhsT=zeros_b, rhs=cst[:, 0:4, :].rearrange("p a b -> p (a b)")[:, 0:n],
                                                                          start=True, stop=True), reads=["cst"], writes=[Ok_], inc=True)
                            steps = list(range(kb_hi, -1, -1))
                            st = {}

                            def st_z(i):
                                kb = steps[i]
                                c0 = max(0, (kb - kb_d0) * 128)
                                w = n - c0
                                zz_i[0] = (zz_i[0] + 1) % 2
                                ZZ, ZZk = ZZs[zz_i[0]]
                                for e_ in range(2):
                                    S.op("pe", lambda e, e_=e_: e.matmul(ZZ[:, e_ * 512:e_ * 512 + w], lhsT=KT[:, kb * 128:(kb + 1) * 128],
                                                                        rhs=QTz[e_][:, o0 + c0:o0 + n], start=True, stop=True),
                                         reads=[("KT", kb // 4), qk], writes=[ZZk[e_]], inc=(e_ == 1))
                                st[i] = dict(kb=kb, c0=c0, w=w, ZZ=ZZ, ZZk=ZZk, diag=(kb >= kb_d0))

                            def st_el(i):
                                s = st[i]
                                w = s["w"]
                                Et, Ek = E_r.next()
                                Lt, Lk = L_r.next()
                                s.update(Et=Et, Ek=Ek, Lt=Lt, Lk=Lk)
                                ZZ3 = s["ZZ"].rearrange("p (h n) -> p h n", h=2)
                                S.op("act", lambda e: e.activation(out=Et[:, :, 0:w], in_=ZZ3[:, :, 0:w], func=AF.Exp),
                                     reads=s["ZZk"], writes=[Ek])

                            def st_l(i):
                                s = st[i]
                                w = s["w"]
                                Et, Ek, Lt, Lk = s["Et"], s["Ek"], s["Lt"], s["Lk"]
                                S.op("act", lambda e: e.activation(out=Lt[:, :, 0:w], in_=Et[:, :, 0:w], func=AF.Ln, bias=1.0),
                                     reads=[Ek], writes=[Lk])
                                if s["diag"]:
                                    for e_ in range(2):
                                        S.op("pool", lambda e, e_=e_: e.tensor_tensor(out=Lt[:, e_, 0:128], in0=Lt[:, e_, 0:128], in1=m_strict, op=ALU.mult),
                                             reads=[Lk, "cst"], writes=[Lk])

                            def st_ctri(i):
                                s = st[i]
                                for e_ in range(2):
                                    S.op("pe", lambda e, e_=e_: e.matmul(CC[:, e_ * 512 + s["c0"]:e_ * 512 + n], lhsT=tri_incl, rhs=s["Lt"][:, e_, 0:s["w"]],
                                                                        start=False, stop=True, skip_group_check=True),
                                         reads=[s["Lk"], "cst"], writes=[CCk[e_]], inc=(e_ == 1))

                            def st_g(i):
                                s = st[i]
                                Gt, Gk = G_r.next()
                                s.update(Gt=Gt, Gk=Gk)
                                S.op("act", lambda e: e.activation(out=Gt[:, :, 0:s["w"]], in_=CC3[:, :, s["c0"]:n], func=AF.Exp, scale=-1.0),
                                     reads=CCk, writes=[Gk])

                            def st_fix(i):
                                s = st[i]
                                w = s["w"]
                                At, Ak = A_r.next()
                                s.update(At=At, Ak=Ak)
                                for e_ in range(2):
                                    S.op("pe", lambda e, e_=e_: e.matmul(CC[:, e_ * 512 + s["c0"]:e_ * 512 + n], lhsT=tri_fix, rhs=s["Lt"][:, e_, 0:w],
                                                                        start=False, stop=True, skip_group_check=True),
                                         reads=[s["Lk"], "cst"], writes=[CCk[e_]], inc=(e_ == 1))
                                S.op("dve", lambda e: e.tensor_tensor(out=At[:, :, 0:w], in0=s["Et"][:, :, 0:w], in1=s["Gt"][:, :, 0:w], op=ALU.mult),
                                     reads=[s["Ek"], s["Gk"]], writes=[Ak])
                                if s["diag"]:
                                    for e_ in range(2):
                                        S.op("pool", lambda e, e_=e_: e.tensor_tensor(out=At[:, e_, 0:128], in0=At[:, e_, 0:128], in1=m_strict, op=ALU.mult),
                                             reads=[Ak, "cst"], writes=[Ak])

                            def st_pv(i):
                                s = st[i]
                                for e_ in range(2):
                                    Ot, Ok_ = psOs[e_]
                                    S.op("pe", lambda e, e_=e_, Ot=Ot: e.matmul(Ot[:, s["c0"]:n], lhsT=Vtok[:, s["kb"], :], rhs=s["At"][:, e_, 0:s["w"]],
                                                                               start=False, stop=True, skip_group_check=True),
                                         reads=[s["Ak"], "Vtok"], writes=[Ok_], inc=(e_ == 1))
                                del st[i]

                            ns = len(steps)
                            ok = lambda j: 0 <= j < ns
                            for it in range(ns + 5):
                                if ok(it):
                                    st_z(it)
                                if ok(it - 1):
                                    st_el(it - 1)
                                if ok(it - 3):
                                    st_g(it - 3)
                                if ok(it - 1):
                                    st_l(it - 1)
                                if ok(it - 2):
                                    st_ctri(it - 2)
                                if ok(it - 4):
                                    st_pv(it - 4)
                                if ok(it - 3):
                                    st_fix(it - 3)
                            for e_ in range(2):
                                Ot, Ok_ = psOs[e_]
                                S.op("act", lambda e, o0=o0, n=n, p=p, Ot=Ot, e_=e_: e.activation(
                                    out=ybT[e_ * 64:(e_ + 1) * 64, p, o0:o0 + n], in_=Ot[e_ * 64:(e_ + 1) * 64, 0:n], func=AF.Copy),
                                    reads=[Ok_], writes=["ybT"])
                    S.barrier()
            S.barrier()

        if dbg:
            with ExitStack() as esd:
                dbf = esd.enter_context(nc.sbuf_tensor("dbf", [128, 4, NOWN], F32))
                S.op("dve", lambda e: e.tensor_copy(out=dbf[:, 0:2, :], in_=yaT[:]), reads=["yaT"], writes=["dbf"])
                S.dma("sp", d_ya, dbf[:, 0:2, :], reads=["dbf"])
                S.barrier()
                S.op("dve", lambda e: e.tensor_copy(out=dbf[:], in_=ybT[:]), reads=["ybT"], writes=["dbf"])
                S.dma("sp", d_yb, dbf[:], reads=["dbf"])
                S.barrier()

        if stop_after >= 4:
            with ExitStack() as es4:
                def sb4(name, shape, dt):
                    return es4.enter_context(nc.sbuf_tensor("q_" + name, shape, dt))
                gB_x = sb4("gB_x", [128, D], F32)
                gB_mem = sb4("gB_mem", [128, D], F32)
                gB_ffn = sb4("gB_ffn", [128, D], F32)
                gB_f = sb4("gB_f", [128, D], F32)
                wb_r = Ring("wb", [sb4("wb%d" % i, [128, 4096], BF16) for i in range(3)])
                H2 = [sb4("H%d" % i, [128, 4, D], F32) for i in range(2)]
                H = H2[1]
                xT = sb4("xT", [128, 8, 512], BF16)
                mT = sb4("mT", [128, 8, 512], BF16)
                QxT = mT[:, 0:4, :]
                oxT = mT[:, 4:8, :]
                ta_r = Ring("ta", [sb4("ta%d" % i, [128, 512], F32) for i in range(2)])
                tb_r = Ring("tb", [sb4("tb%d" % i, [128, 512], F32) for i in range(2)])
                KxT = sb4("KxT", [128, 4, 256], BF16)
                Vx = sb4("Vx", [128, 2, 512], BF16)
                Px_r = Ring("Px", [sb4("Px%d" % i, [128, 2, 512], BF16) for i in range(2)])
                rsum_r = Ring("rsum", [sb4("rsum%d" % i, [128, 512], F32) for i in range(1)])
                aT = sb4("aT", [128, NCH, 512], BF16)
                U_r = Ring("U", [sb4("U%d" % i, [128, 514], F32) for i in range(2)])
                tc_r = Ring("tc", [sb4("tc%d" % i, [128, 512], F32) for i in range(4)])
                sg_r = Ring("sg", [sb4("sg%d" % i, [128, 512], F32) for i in range(1)])
                carry = sb4("carry", [128, 44, 2], F32)
                bg = sb4("bg", [128, 16], F32)
                cw = sb4("cw", [128, 44, 3], F32)
                cb = sb4("cb", [128, 44], F32)
                if dbg:
                    dbuf = sb4("dbuf", [128, 2, 512], F32)

                S.dma("sp", gB_x[:], g_x.partition_broadcast(128), writes=["gB_x"])
                S.dma("sp", gB_mem[:], g_mem.partition_broadcast(128), writes=["gB_mem"])
                S.dma("sp", gB_ffn[:], g_ffn.partition_broadcast(128), writes=["gB_ffn"])
                S.dma("sp", gB_f[:], g_f.partition_broadcast(128), writes=["gB_f"])
                S.dma("sp", bg[:], b_gate, writes=["smallc"])
                S.dma("sp", cw[:], conv_w, writes=["smallc"])
                S.dma("sp", cb[:], conv_b, writes=["smallc"])
                S.op("pool", lambda e: e.memset(carry[:], 0.0), writes=["carry"])

                def load_wb(src_aps, shape3):
                    wt, wk = wb_r.next()
                    kc, nn = shape3
                    v = wt[:, 0:kc * nn].rearrange("p (k n) -> p k n", k=kc)
                    for (c0, c1, nm, src) in src_aps:
                        S.dma("sp", v[:, :, c0:c1], src, reads=skeys(nm), writes=[wk])
                    return v, wk

                wba = sb4("wba", [128, 2, 1024], BF16)
                wbb = sb4("wbb", [128, 4, 1024], BF16)
                wbak, wbbk = "wba", "wbb"
                S.dma("sp", wba[:], sw("w_ba"), reads=skeys("w_ba"), writes=[wbak])
                S.dma("sp", wbb[:], sw("w_bb"), reads=skeys("w_bb"), writes=[wbbk])
                for mb in range(2):
                    S.dma("sp", H[:, mb, :], mem[mb * 128:(mb + 1) * 128, :], writes=[("H", 1, mb)])
                    rms_to_T(H[:, mb, :], ("H", 1, mb), gB_mem, "gB_mem", xT[:, :, mb * 128:(mb + 1) * 128], "xT")
                wkv_k, wkvk_k = load_wb([(0, 512, "w_xkv", sw("w_xkv")[:, :, 0:512])], (8, 512))
                wkv_v, wkvk_v = load_wb([(0, 512, "w_xkv", sw("w_xkv")[:, :, 512:1024])], (8, 512))
                for hx in range(4):
                    ps, psk = ps6.next()
                    mm_acc(ps[:, 0:256], psk, [(wkv_k[:, k, hx * 128:(hx + 1) * 128], xT[:, k, 0:256]) for k in range(8)],
                           reads=[wkvk_k, "xT"])
                    S.op("act", lambda e, ps=ps, hx=hx: e.activation(out=KxT[:, hx, :], in_=ps[:, 0:256], func=AF.Copy),
                         reads=[psk], writes=["KxT"])
                for mb in range(2):
                    ps, psk = ps6.next()
                    mm_acc(ps[:], psk, [(xT[:, k, mb * 128:(mb + 1) * 128], wkv_v[:, k, :]) for k in range(8)],
                           reads=[wkvk_v, "xT"])
                    S.op("act", lambda e, ps=ps, mb=mb: e.activation(out=Vx[:, mb, :], in_=ps[:], func=AF.Copy),
                         reads=[psk], writes=["Vx"])

                XS = 1.0 / math.sqrt(128.0)
                def prefetch_x(ti):
                    o0_, n_ = QTILES[ti]
                    for blk in range(n_ // 128):
                        r0 = OWN0 + o0_ + blk * 128
                        S.dma("pool", H2[ti % 2][:, blk, :], xl[r0:r0 + 128, :], writes=[("H", ti % 2, blk)])

                a_done = {}

                def emit_a_front(tj):
                    o0_, n_ = QTILES[tj]
                    Hj = H2[tj % 2]
                    a_done[tj] = rms_front([(Hj[:, blk, :], ("H", tj % 2, blk), xT[:, :, blk * 128:(blk + 1) * 128], "xT")
                                            for blk in range(n_ // 128)], gB_mix, "gB_mix")

                def emit_a_back(tj):
                    rms_back(a_done[tj])

                prefetch_x(0)
                for ti, (o0, n) in enumerate(QTILES):
                    halo = (o0 == 0)
                    nblk = n // 128
                    H = H2[ti % 2]
                    hp = ti % 2
                    if ti not in a_done:
                        emit_a_front(ti)
                        emit_a_back(ti)
                    if ti + 1 < len(QTILES):
                        prefetch_x(ti + 1)
                    if stop_after < 5:
                        continue
                    for og in range(2):
                        wga, wgak = load_wb([(0, 512, "w_in", sw("w_in")[:, :, 3840 + 512 * og:3840 + 512 * (og + 1)])], (8, 512))
                        for oi in range(4):
                            o = 4 * og + oi
                            ta, tak = ta_r.next()
                            ps, psk = ps6.next()
                            mm_acc(ps[:, 0:n], psk, [(wga[:, k, oi * 128:(oi + 1) * 128], xT[:, k, 0:n]) for k in range(8)],
                                   reads=[wgak, "xT"])
                            S.op("act", lambda e, ps=ps, ta=ta, o=o: e.activation(out=ta[:, 0:n], in_=ps[:, 0:n], func=AF.Sigmoid,
                                                                                bias=bg[:, o:o + 1]),
                                 reads=[psk, "smallc"], writes=[tak])
                            ps2, ps2k = ps6.next()
                            mm_acc(ps2[:, 0:n], ps2k, [(wba[:, k, o * 128:(o + 1) * 128], yaT[:, k, o0:o0 + n]) for k in range(2)],
                                   reads=[wbak, "yaT"])
                            S.op("dve", lambda e, ps2=ps2, ta=ta: e.tensor_tensor(out=ta[:, 0:n], in0=ps2[:, 0:n], in1=ta[:, 0:n], op=ALU.mult),
                                 reads=[ps2k, tak], writes=[tak])
                            S.op("pool", lambda e, ta=ta, o=o: e.tensor_copy(out=mT[:, o, 0:n], in_=ta[:, 0:n]),
                                 reads=[tak], writes=[("mTa", o), "QxT" if o < 4 else "oxT"])
                    pend_add = []

                    def emit_add(tb, tbk, o):
                        S.op("dve", lambda e: e.tensor_tensor(out=mT[:, o, 0:n], in0=mT[:, o, 0:n], in1=tb[:, 0:n], op=ALU.add),
                             reads=[tbk, ("mTa", o)], writes=[("mTa", o)])

                    for og in range(2):
                        wgb, wgbk = load_wb([(0, 512, "w_in", sw("w_in")[:, :, 4864 + 512 * og:4864 + 512 * (og + 1)])], (8, 512))
                        for oi in range(4):
                            o = 4 * og + oi
                            tb, tbk = tb_r.next()
                            ps, psk = ps6.next()
                            mm_acc(ps[:, 0:n], psk, [(wgb[:, k, oi * 128:(oi + 1) * 128], xT[:, k, 0:n]) for k in range(8)],
                                   reads=[wgbk, "xT"])
                            S.op("act", lambda e, ps=ps, tb=tb, o=o: e.activation(out=tb[:, 0:n], in_=ps[:, 0:n], func=AF.Sigmoid,
                                                                                bias=bg[:, 8 + o:9 + o]),
                                 reads=[psk, "smallc"], writes=[tbk])
                            ps2, ps2k = ps6.next()
                            mm_acc(ps2[:, 0:n], ps2k, [(wbb[:, k, o * 128:(o + 1) * 128], ybT[:, k, o0:o0 + n]) for k in range(4)],
                                   reads=[wbbk, "ybT"])
                            S.op("dve", lambda e, ps2=ps2, tb=tb: e.tensor_tensor(out=tb[:, 0:n], in0=ps2[:, 0:n], in1=tb[:, 0:n], op=ALU.mult),
                                 reads=[ps2k, tbk], writes=[tbk])
                            if pend_add:
                                emit_add(*pend_add.pop())
                            pend_add.append((tb, tbk, o))
                    while pend_add:
                        emit_add(*pend_add.pop())
                    allm = [("mTa", o) for o in range(8)]
                    if stop_after < 6:
                        continue
                    if dbg and o0 == 128:
                        for q4 in range(4):
                            S.op("dve", lambda e, q4=q4: e.tensor_copy(out=dbuf[:], in_=mT[:, 2 * q4:2 * q4 + 2, :]), reads=allm, writes=["dbuf"])
                            S.dma("sp", d_m[:, 2 * q4:2 * q4 + 2, :], dbuf[:], reads=["dbuf"])
                        for q4 in range(4):
                            S.op("dve", lambda e, q4=q4: e.tensor_copy(out=dbuf[:], in_=xT[:, 2 * q4:2 * q4 + 2, :]), reads=["xT"], writes=["dbuf"])
                            S.dma("sp", d_x[:, 2 * q4:2 * q4 + 2, :], dbuf[:], reads=["dbuf"])
                    for half in range(2):
                        wo, wok = load_wb([(0, 512, "w_out", sw("w_out")[:, :, half * 512:(half + 1) * 512])], (8, 512))
                        for blk in range(nblk):
                            ps, psk = ps6.next()
                            mm_acc(ps[:], psk, [(mT[:, k, blk * 128:(blk + 1) * 128], wo[:, k, :]) for k in range(8)],
                                   reads=[wok] + allm)
                            S.op("dve", lambda e, ps=ps, blk=blk, half=half: e.tensor_tensor(
                                out=H[:, blk, half * 512:(half + 1) * 512], in0=H[:, blk, half * 512:(half + 1) * 512], in1=ps[:], op=ALU.add),
                                reads=[psk, ("H", hp, blk)], writes=[("H", hp, blk)])
                    if dbg and not halo:
                        for blk in range(4):
                            r0 = o0 - 128 + blk * 128
                            S.dma("sp", d_h[0, r0:r0 + 128, :], H[:, blk, :], reads=[("H", hp, blk)])
                    rms_to_T_multi([(H[:, blk, :], ("H", hp, blk), xT[:, :, blk * 128:(blk + 1) * 128], "xT") for blk in range(nblk)],
                                   gB_x, "gB_x")
                    if stop_after < 7:
                        continue
                    wxq, wxqk = load_wb([(0, 512, "w_xq", sw("w_xq"))], (8, 512))
                    for c in range(4):
                        ps, psk = ps6.next()
                        mm_acc(ps[:, 0:n], psk, [(wxq[:, k, c * 128:(c + 1) * 128], xT[:, k, 0:n]) for k in range(8)],
                               reads=[wxqk, "xT"])
                        S.op("act", lambda e, ps=ps, c=c: e.activation(out=QxT[:, c, 0:n], in_=ps[:, 0:n], func=AF.Copy),
                             reads=[psk], writes=["QxT", ("mTa", c)])
                    def xa_scores(hx):
                        Px, Pxk = Px_r.next()
                        for mb in range(2):
                            ps, psk = ps6.next()
                            S.op("pe", lambda e, ps=ps, mb=mb: e.matmul(ps[:, 0:n], lhsT=KxT[:, hx, mb * 128:(mb + 1) * 128],
                                                                       rhs=QxT[:, hx, 0:n], start=True, stop=True),
                                 reads=["KxT", "QxT"], writes=[psk])
                            S.op("act", lambda e, ps=ps, mb=mb: e.activation(out=Px[:, mb, 0:n], in_=ps[:, 0:n], func=AF.Exp, scale=XS),
                                 reads=[psk], writes=[Pxk])
                        return Px, Pxk

                    def xa_pv(hx, Px, Pxk):
                        pso, psok = ps6.next()
                        mm_acc(pso[:, 0:n], psok, [(Vx[:, mb, hx * 128:(hx + 1) * 128], Px[:, mb, 0:n]) for mb in range(2)],
                               reads=["Vx", Pxk])
                        psr, psrk = ps6.next()
                        mm_acc(psr[:, 0:n], psrk, [(ones_b, Px[:, mb, 0:n]) for mb in range(2)], reads=["cst", Pxk])
                        rsum, rsumk = rsum_r.next()
                        S.op("dve", lambda e: e.reciprocal(out=rsum[:, 0:n], in_=psr[:, 0:n]),
                             reads=[psrk], writes=[rsumk])
                        S.op("dve", lambda e: e.tensor_tensor(out=oxT[:, hx, 0:n], in0=pso[:, 0:n], in1=rsum[:, 0:n], op=ALU.mult),
                             reads=[psok, rsumk], writes=["oxT", ("mTa", 4 + hx)])

                    cur = xa_scores(0)
                    for hx in range(4):
                        nxt = xa_scores(hx + 1) if hx + 1 < 4 else None
                        xa_pv(hx, *cur)
                        cur = nxt
                    wxo, wxok = load_wb([(0, 1024, "w_xo", sw("w_xo"))], (4, 1024))
                    for half in range(2):
                        for blk in range(nblk):
                            ps, psk = ps6.next()
                            mm_acc(ps[:], psk, [(oxT[:, k, blk * 128:(blk + 1) * 128], wxo[:, k, half * 512:(half + 1) * 512]) for k in range(4)],
                                   reads=[wxok, "oxT"])
                            S.op("dve", lambda e, ps=ps, blk=blk, half=half: e.tensor_tensor(
                                out=H[:, blk, half * 512:(half + 1) * 512], in0=H[:, blk, half * 512:(half + 1) * 512], in1=ps[:], op=ALU.add),
                                reads=[psk, ("H", hp, blk)], writes=[("H", hp, blk)])
                    if dbg and not halo:
                        for blk in range(4):
                            r0 = o0 - 128 + blk * 128
                            S.dma("sp", d_h[1, r0:r0 + 128, :], H[:, blk, :], reads=[("H", hp, blk)])
                    rms_to_T_multi([(H[:, blk, :], ("H", hp, blk), xT[:, :, blk * 128:(blk + 1) * 128], "xT") for blk in range(nblk)],
                                   gB_ffn, "gB_ffn")
                    if stop_after < 8:
                        continue
                    pend = []

                    def emit_back(c, res_t):
                        (tg, tgk), (tv, tvk) = res_t
                        sg, sgk = sg_r.next()
                        S.op("act", lambda e: e.activation(out=sg[:], in_=tg[:], func=AF.Silu), reads=[tgk], writes=[sgk])
                        S.op("pool", lambda e: e.tensor_tensor(out=aT[:, c, :], in0=sg[:], in1=tv[:], op=ALU.mult),
                             reads=[sgk, tvk], writes=[("aT", c)])

                    for cg in range(11):
                        if cg == 8 and ti + 1 < len(QTILES):
                            emit_a_front(ti + 1)
                        wu, wuk = load_wb([(0, 256, "w_up", sw("w_up")[:, :, 256 * cg:256 * (cg + 1)]),
                                           (256, 512, "w_up", sw("w_up")[:, :, DFF + 256 * cg:DFF + 256 * (cg + 1)])], (8, 512))
                        for ci in range(2):
                            c = 2 * cg + ci
                            res_t = []
                            parts = []
                            for part in range(2):
                                ch = c + NCH * part
                                ps, psk = ps6.next()
                                mm_acc(ps[:, 0:n], psk, [(wu[:, k, 256 * part + ci * 128:256 * part + (ci + 1) * 128], xT[:, k, 0:n]) for k in range(8)],
                                       reads=[wuk, "xT"])
                                if halo:
                                    S.op("dve", lambda e, ps=ps, ch=ch: e.tensor_scalar_mul(out=carry[:, ch, :], in0=ps[:, n - 2:n], scalar1=hval[:, 0:1]),
                                         reads=[psk, "cst"], writes=[("carry", ch)])
                                    continue
                                U, Uk = U_r.next()
                                tcb, tck = tc_r.next()
                                parts.append((ch, ps, psk, U, Uk, tcb, tck))
                            for (ch, ps, psk, U, Uk, tcb, tck) in parts:
                                S.op("pool", lambda e, U=U, ch=ch: e.tensor_copy(out=U[:, 0:2], in_=carry[:, ch, :]),
                                     reads=[("carry", ch), "carry"], writes=[(Uk, "c")])
                                S.op("act", lambda e, U=U, ps=ps: e.activation(out=U[:, 2:514], in_=ps[:], func=AF.Copy),
                                     reads=[psk], writes=[(Uk, "d")])
                                S.op("pool", lambda e, U=U, ch=ch: e.tensor_copy(out=carry[:, ch, :], in_=U[:, 512:514]),
                                     reads=[(Uk, "d")], writes=[("carry", ch)])
                            for (ch, ps, psk, U, Uk, tcb, tck) in parts:
                                S.op("act", lambda e, U=U, tcb=tcb, ch=ch: e.activation(out=tcb[:], in_=U[:, 0:512], func=AF.Identity,
                                                                                        scale=cw[:, ch, 0:1], bias=cb[:, ch:ch + 1]),
                                     reads=[(Uk, "c"), (Uk, "d"), "smallc"], writes=[tck])
                            for (ch, ps, psk, U, Uk, tcb, tck) in parts:
                                S.op("dve", lambda e, U=U, tcb=tcb, ch=ch: e.scalar_tensor_tensor(out=tcb[:], in0=U[:, 1:513], scalar=cw[:, ch, 1:2],
                                                                                                in1=tcb[:], op0=ALU.mult, op1=ALU.add),
                                     reads=[(Uk, "c"), (Uk, "d"), tck, "smallc"], writes=[tck])
                            for (ch, ps, psk, U, Uk, tcb, tck) in parts:
                                S.op("dve", lambda e, ps=ps, tcb=tcb, ch=ch: e.scalar_tensor_tensor(out=tcb[:], in0=ps[:], scalar=cw[:, ch, 2:3],
                                                                                                 in1=tcb[:], op0=ALU.mult, op1=ALU.add),
                                     reads=[psk, tck, "smallc"], writes=[tck])
                                res_t.append((tcb, tck))
                            if halo:
                                continue
                            if pend:
                                emit_back(*pend.pop())
                            pend.append((c, res_t))
                    while pend:
                        emit_back(*pend.pop())
                    if halo and ti + 1 < len(QTILES):
                        emit_a_back(ti + 1)
                    if halo:
                        continue
                    if stop_after < 9:
                        continue
                    alla = [("aT", c) for c in range(NCH)]
                    for half in range(2):
                        accs = [(PB[i], ("pb", i)) for i in range(4)]
                        for cgp, (c_lo, c_hi) in enumerate(((0, 8), (8, 16), (16, 22))):
                            wd, wdk = load_wb([(0, 512, "w_down", sw("w_down")[:, c_lo:c_hi, half * 512:(half + 1) * 512])], (c_hi - c_lo, 512))
                            for c in range(c_lo, c_hi):
                                for blk in range(4):
                                    ps, psk = accs[blk]
                                    S.op("pe", lambda e, ps=ps, c=c, blk=blk, wd=wd, c_lo=c_lo: e.matmul(
                                        ps[:], lhsT=aT[:, c, blk * 128:(blk + 1) * 128], rhs=wd[:, c - c_lo, :],
                                        start=(c == 0), stop=(c == NCH - 1)),
                                        reads=[wdk, ("aT", c)], writes=[psk], inc=(c == NCH - 1 or (c == c_hi - 1 and blk == 3)))
                        for blk in range(4):
                            ps, psk = accs[blk]
                            S.op("dve", lambda e, ps=ps, blk=blk, half=half: e.tensor_tensor(
                                out=H[:, blk, half * 512:(half + 1) * 512], in0=H[:, blk, half * 512:(half + 1) * 512], in1=ps[:], op=ALU.add),
                                reads=[psk, ("H", hp, blk)], writes=[("H", hp, blk)])
                    if dbg and not halo:
                        for blk in range(4):
                            r0 = o0 - 128 + blk * 128
                            S.dma("sp", d_h[2, r0:r0 + 128, :], H[:, blk, :], reads=[("H", hp, blk)])
                    if ti + 1 < len(QTILES):
                        emit_a_back(ti + 1)
                    fin = []
                    for blk in range(4):
                        ss, ssk = ss_r.next()
                        rs, rsk = rs_r.next()
                        S.op("act", lambda e, blk=blk, ss=ss: e.activation(out=junk[:], in_=H[:, blk, :], func=AF.Square, accum_out=ss[:]),
                             reads=[("H", hp, blk)], writes=["junk", ssk])
                        fin.append((blk, ss, ssk, rs, rsk))
                    for (blk, ss, ssk, rs, rsk) in fin:
                        S.op("act", lambda e, ss=ss, rs=rs: e.activation(out=rs[:], in_=ss[:], func=AF.Sqrt, scale=1.0 / D, bias=1e-6),
                             reads=[ssk], writes=[rsk])
                    for (blk, ss, ssk, rs, rsk) in fin:
                        S.op("dve", lambda e, rs=rs: e.reciprocal(out=rs[:], in_=rs[:]), reads=[rsk], writes=[rsk])
                    for (blk, ss, ssk, rs, rsk) in fin:
                        ob, obk = xin_r.next()
                        S.op("dve", lambda e, blk=blk, rs=rs, ob=ob: e.scalar_tensor_tensor(out=ob[:], in0=H[:, blk, :], scalar=rs[:], in1=gB_f[:],
                                                                                          op0=ALU.mult, op1=ALU.mult),
                             reads=[("H", hp, blk), rsk, "gB_f"], writes=[obk])
                        r0 = o0 - 128 + blk * 128
                        S.dma("pool", out[r0:r0 + 128, :], ob[:], reads=[obk])
                S.barrier()
        S.finish("sp")
    return nc


def _consts():
    bf = ml_dtypes.bfloat16
    k = np.arange(128)[:, None]
    q = np.arange(128)[None, :]
    cst = np.zeros((128, 8, 128), np.float32)
    cst[:, 0] = (k == q)
    cst[:, 1] = (k <= q)
    cst[:, 2] = (k >= q)
    cst[:, 3] = (k < q)
    cst[:, 4] = (k >= q)
    cst[:, 5] = (k < q)
    cst[:, 6] = 0.0
    cst[:, 7] = 1.0
    mask4 = np.concatenate([cst[:, 2], cst[:, 1], cst[:, 2], cst[:, 1]], axis=1)
    return cst.astype(bf), mask4.astype(bf)


def _rope_tables(h):
    gpos = (np.arange(NLOC) - (2048 if h == 0 else 0)).astype(np.float32)
    half = 32
    inv_freq = (1.0 / (np.float32(10000.0) ** (np.arange(half, dtype=np.float32) * np.float32(2.0 / 64)))).astype(np.float32)
    ang = (gpos[None, :] * inv_freq[:, None]).astype(np.float32)
    c = np.cos(ang).astype(np.float32)
    s = np.sin(ang).astype(np.float32)
    rc = np.concatenate([c, c, c, c], axis=0)
    rs = np.concatenate([-s, s, -s, s], axis=0)
    return np.ascontiguousarray(rc), np.ascontiguousarray(rs)


def make_in_maps(inp):
    bf = ml_dtypes.bfloat16
    f = lambda a: np.ascontiguousarray(np.asarray(a, dtype=np.float32))
    x = f(inp["x"])
    memv = f(inp["mem"])
    w_in = f(inp["w_in"][0])
    j = np.arange(1536)
    partner = np.where((j % 64) < 32, j + 32, j - 32)
    w_sw = np.ascontiguousarray(w_in[:, partner])
    cst, mask4 = _consts()
    common = {
        "w_in": w_in, "w_sw": w_sw,
        "g_mix": f(inp["ln_mix_g"][0]).reshape(1, D), "g_x": f(inp["ln_x_g"][0]).reshape(1, D),
        "g_mem": f(inp["ln_mem_g"][0]).reshape(1, D), "g_ffn": f(inp["ln_ffn_g"][0]).reshape(1, D),
        "g_f": f(inp["ln_f_g"]).reshape(1, D),
        "b_gate": np.ascontiguousarray(f(inp["b_gate"][0]).reshape(16, 128).T),
        "w_ba": f(inp["w_branch_a"][0]), "w_bb": f(inp["w_branch_b"][0]), "w_out": f(inp["w_out"][0]),
        "w_xq": f(inp["w_xq"][0]), "w_xkv": f(inp["w_xkv"][0]), "w_xo": f(inp["w_xo"][0]),
        "w_up": f(inp["w_up"][0]),
        "conv_w": np.ascontiguousarray(f(inp["conv_w"][0]).reshape(3, 44, 128).transpose(2, 1, 0)),
        "conv_b": np.ascontiguousarray(f(inp["conv_b"][0]).reshape(44, 128).T),
        "w_down": f(inp["w_down"][0]),
        "cst": cst, "mask4": mask4,
    }
    ropes = [_rope_tables(0), _rope_tables(1)]
    maps = []
    for c in range(8):
        b, h = c // 2, c % 2
        if h == 0:
            xl = np.zeros((NLOC, D), np.float32)
            xl[2048:] = x[b, :2048]
        else:
            xl = np.ascontiguousarray(x[b])
        onesv = np.ones((128, 2, 64), np.float32)
        onesv[:, 0, :] = float(h)
        m = dict(common)
        m.update({"xl": xl, "mem": np.ascontiguousarray(memv[b]), "rcos": ropes[h][0], "rsin": ropes[h][1],
                  "onesv": onesv.astype(bf), "hval": np.full((128, 1), float(h), np.float32)})
        maps.append(m)
    return maps


_NC_CACHE = {}


def kernel(**inputs):
    if "nc" not in _NC_CACHE:
        _NC_CACHE["nc"] = build()
    nc = _NC_CACHE["nc"]
    maps = make_in_maps(inputs)
    res = run_bass_kernel_spmd(nc, maps, core_ids=list(range(8)))
    outp = np.zeros((4, 4096, D), np.float32)
    for c in range(8):
        b, h = c // 2, c % 2
        outp[b, 2048 * h:2048 * (h + 1)] = np.asarray(res.results[c]["out"], dtype=np.float32)
    return outp
```

```python
import math
from contextlib import ExitStack
import numpy as np
import ml_dtypes
import concourse.bass as bass
import concourse.mybir as mybir
from concourse.bass_utils import run_bass_kernel_spmd

F32 = mybir.dt.float32
BF16 = mybir.dt.bfloat16
AF = mybir.ActivationFunctionType
ALU = mybir.AluOpType

D = 1024
NLOC = 4096
OWN0 = 1920
NOWN = NLOC - OWN0
DFF = 2816
NCH = 22
QTILES = [(0, 128)] + [(128 + 512 * j, 512) for j in range(4)]
DILS = (1, 4, 16)


class _Eng:
    def __init__(self, name, eng, sem):
        self.name = name
        self.eng = eng
        self.sem = sem
        self.count = 0
        self.known = {}
        self.maxwait = 0


class Sched:
    def __init__(self, nc, sems, n_dma_slots):
        self.nc = nc
        self.E = {}
        it = iter(sems)
        for name, eng in (("pe", nc.tensor), ("act", nc.scalar), ("dve", nc.vector),
                          ("pool", nc.gpsimd), ("sp", nc.sync)):
            self.E[name] = _Eng(name, eng, next(it))
        self.slots = {"sp": [], "pool": []}
        for i in range(n_dma_slots):
            s = _Eng("dma%d" % i, None, next(it))
            self.E[s.name] = s
            self.slots["sp" if i < n_dma_slots // 2 else "pool"].append(s)
        self.slot_rr = {"sp": 0, "pool": 0}
        self.res = {}

    def _collect(self, reads, writes):
        deps = set()
        for r in reads:
            st = self.res.get(r)
            if st and st[0]:
                deps.add(st[0])
        for w in writes:
            st = self.res.get(w)
            if st:
                if st[0]:
                    deps.add(st[0])
                deps.update(st[1])
        return deps

    def _wait(self, E, deps):
        need = {}
        for (en, idx) in deps:
            if en == E.name and en == "pe":
                continue
            need[en] = max(need.get(en, 0), idx)
        for en, idx in need.items():
            if E.known.get(en, 0) >= idx:
                continue
            Dp = self.E[en]
            E.eng.wait_ge(Dp.sem, idx)
            Dp.maxwait = max(Dp.maxwait, idx)
            E.known[en] = idx

    def _record(self, who, reads, writes):
        for r in reads:
            st = self.res.setdefault(r, [None, []])
            st[1].append(who)
            if len(st[1]) > 24:
                last = {}
                for (en, idx) in st[1]:
                    last[en] = max(last.get(en, 0), idx)
                st[1] = list(last.items())
        for w in writes:
            self.res[w] = [who, []]

    def op(self, engname, fn, reads=(), writes=(), inc=True):
        E = self.E[engname]
        self._wait(E, self._collect(reads, writes))
        ins = fn(E.eng)
        if inc:
            E.count += 1
            ins.then_inc(E.sem, 1)
            who = (engname, E.count)
        else:
            assert engname == "pe"
            who = (engname, E.count + 1)
        self._record(who, reads, writes)
        return ins

    def dma(self, qname, out, in_, reads=(), writes=()):
        Q = self.E[qname]
        S = self.slots[qname][self.slot_rr[qname]]
        self.slot_rr[qname] = (self.slot_rr[qname] + 1) % len(self.slots[qname])
        deps = self._collect(reads, writes)
        if S.count:
            deps.add((S.name, S.count))
        self._wait(Q, deps)
        ins = Q.eng.dma_start(out=out, in_=in_)
        S.count += 16
        ins.then_inc(S.sem, 16)
        self._record((S.name, S.count), reads, writes)
        return ins

    def barrier(self):
        all_deps = set()
        for name, Dp in self.E.items():
            if Dp.count:
                all_deps.add((name, Dp.count))
        for en in ("pe", "act", "dve", "pool", "sp"):
            self._wait(self.E[en], all_deps)
        self.res = {}

    def finish(self, engname="sp"):
        E = self.E[engname]
        deps = {(n, Dp.count) for n, Dp in self.E.items() if Dp.count and n != engname}
        self._wait(E, deps)
        for n, Dp in self.E.items():
            assert Dp.maxwait <= Dp.count, (n, Dp.maxwait, Dp.count)


class Ring:
    def __init__(self, name, tensors):
        self.name = name
        self.t = tensors
        self.i = -1

    def next(self):
        self.i = (self.i + 1) % len(self.t)
        return self.t[self.i], (self.name, self.i)


def build(dbg=False, stop_after=99):
    nc = bass.Bass("TRN2", target_bir_lowering=False)

    def din(name, shape, dt=F32):
        return nc.dram_tensor(name, shape, dt, kind="ExternalInput").ap()

    xl = din("xl", [NLOC, D])
    mem = din("mem", [256, D])
    w_in = din("w_in", [D, 5888])
    w_sw = din("w_sw", [D, 1536])
    g_mix = din("g_mix", [1, D])
    g_x = din("g_x", [1, D])
    g_mem = din("g_mem", [1, D])
    g_ffn = din("g_ffn", [1, D])
    g_f = din("g_f", [1, D])
    b_gate = din("b_gate", [128, 16])
    w_ba = din("w_ba", [256, D])
    w_bb = din("w_bb", [512, D])
    w_out = din("w_out", [D, D])
    w_xq = din("w_xq", [D, 512])
    w_xkv = din("w_xkv", [D, 1024])
    w_xo = din("w_xo", [512, D])
    w_up = din("w_up", [D, 2 * DFF])
    conv_w = din("conv_w", [128, 44, 3])
    conv_b = din("conv_b", [128, 44])
    w_down = din("w_down", [DFF, D])
    rcos = din("rcos", [128, NLOC])
    rsin = din("rsin", [128, NLOC])
    cst_d = din("cst", [128, 8, 128], BF16)
    mask4_d = din("mask4", [128, 512], BF16)
    onesv_d = din("onesv", [128, 2, 64], BF16)
    hval_d = din("hval", [128, 1])
    out = nc.dram_tensor("out", [2048, D], F32, kind="ExternalOutput").ap()
    if dbg:
        d_ya = nc.dram_tensor("d_ya", [128, 2, NOWN], F32, kind="ExternalOutput").ap()
        d_yb = nc.dram_tensor("d_yb", [128, 4, NOWN], F32, kind="ExternalOutput").ap()
        d_h = nc.dram_tensor("d_h", [3, 2048, D], F32, kind="ExternalOutput").ap()
        d_m = nc.dram_tensor("d_m", [128, 8, 512], F32, kind="ExternalOutput").ap()
        d_x = nc.dram_tensor("d_x", [128, 8, 512], F32, kind="ExternalOutput").ap()

    def wview(w):
        return w.rearrange("(kc p) n -> p kc n", p=128)

    SCR = {}
    for nm, src in (("w_in", w_in), ("w_sw", w_sw), ("w_ba", w_ba), ("w_bb", w_bb), ("w_xkv", w_xkv), ("w_out", w_out),
                    ("w_xq", w_xq), ("w_xo", w_xo), ("w_up", w_up), ("w_down", w_down)):
        R_, C_ = src.shape
        SCR[nm] = (src, nc.dram_tensor("scr_" + nm, [R_, C_], BF16, kind="Internal").ap(), [])

    with ExitStack() as es:
        def sb(name, shape, dt):
            return es.enter_context(nc.sbuf_tensor("s_" + name, shape, dt))

        NSLOT = 32
        sems = [es.enter_context(nc.semaphore("s%d" % i)) for i in range(5 + NSLOT)]
        PSALL = es.enter_context(nc.psum_tensor("psall", [128, 4096], F32))
        PB = [PSALL[:, 512 * i:512 * (i + 1)] for i in range(6)]
        pT = Ring("pT", [PSALL[:, 512 * (6 + i):512 * (7 + i)].bitcast(BF16) for i in range(2)])
        ps6 = Ring("pb", PB)
        cst = sb("cst", [128, 8, 128], BF16)
        mask4 = sb("mask4", [128, 512], BF16)
        onesv = sb("onesv", [128, 2, 64], BF16)
        hval = sb("hval", [128, 1], F32)
        gB_mix = sb("gB_mix", [128, D], F32)
        yaT = sb("yaT", [128, 2, NOWN], BF16)
        ybT = sb("ybT", [128, 4, NOWN], BF16)
        junk = sb("junk", [128, D], BF16)
        xn_r = Ring("xn", [sb("xn%d" % i, [128, D], BF16) for i in range(4)])
        ss_r = Ring("ss", [sb("ss%d" % i, [128, 1], F32) for i in range(8)])
        rs_r = Ring("rs", [sb("rs%d" % i, [128, 1], F32) for i in range(8)])
        xin_r = Ring("xin", [sb("xin%d" % i, [128, D], F32) for i in range(2)])

        block = es.enter_context(nc.Block())
        S = Sched(nc, sems, NSLOT)

        ident = cst[:, 0, :]
        m_cur = cst[:, 1, :]
        m_prev = cst[:, 2, :]
        m_strict = cst[:, 3, :]
        tri_incl = cst[:, 4, :]
        tri_fix = cst[:, 5, :]
        zeros_b = cst[:, 6, :]
        ones_b = cst[:, 7, :]

        S.dma("sp", cst[:], cst_d, writes=["cst"])
        S.dma("sp", mask4[:], mask4_d, writes=["cst"])
        S.dma("sp", onesv[:], onesv_d, writes=["cst"])
        S.dma("sp", hval[:], hval_d, writes=["cst"])
        S.dma("sp", gB_mix[:], g_mix.partition_broadcast(128), writes=["gB_mix"])

        def precast(names):
            for nm in names:
                src, dst, keys = SCR[nm]
                R_ = src.shape[0]
                c_lo = 3840 if nm == "w_in" else 0
                step = 256 if R_ >= 1024 else R_
                for r0 in range(0, R_, step):
                    S.dma("pool", dst[r0:r0 + step, c_lo:], src[r0:r0 + step, c_lo:], writes=[("scr", nm, r0)])
                    keys.append(("scr", nm, r0))

        def sw(nm):
            return wview(SCR[nm][1])

        def skeys(nm):
            return list(SCR[nm][2])

        def rms_front(items, gB, gkey):
            st = []
            for (x_ap, x_key, outview, outkey) in items:
                ss, ssk = ss_r.next()
                rs, rsk = rs_r.next()
                S.op("act", lambda e, x_ap=x_ap, ss=ss: e.activation(out=junk[:], in_=x_ap, func=AF.Square, accum_out=ss[:]),
                     reads=[x_key], writes=["junk", ssk])
                st.append([x_ap, x_key, outview, outkey, ss, ssk, rs, rsk])
            for t in st:
                x_ap, x_key, outview, outkey, ss, ssk, rs, rsk = t
                S.op("act", lambda e, ss=ss, rs=rs: e.activation(out=rs[:], in_=ss[:], func=AF.Ln, scale=1.0 / D, bias=1e-6),
                     reads=[ssk], writes=[rsk])
            for t in st:
                x_ap, x_key, outview, outkey, ss, ssk, rs, rsk = t
                S.op("act", lambda e, rs=rs: e.activation(out=rs[:], in_=rs[:], func=AF.Exp, scale=-0.5), reads=[rsk], writes=[rsk])
            fr = []
            for t in st:
                x_ap, x_key, outview, outkey, ss, ssk, rs, rsk = t
                xn, xnk = xn_r.next()
                S.op("dve", lambda e, x_ap=x_ap, rs=rs, xn=xn: e.scalar_tensor_tensor(out=xn[:], in0=x_ap, scalar=rs[:], in1=gB[:],
                                                                                  op0=ALU.mult, op1=ALU.mult),
                     reads=[x_key, rsk, gkey], writes=[xnk])
                fr.append((xn, xnk, outview, outkey))
            return fr

        def rms_back(fr):
            for (xn, xnk, outview, outkey) in fr:
                pt, ptk = pT.next()
                for k in range(8):
                    S.op("pe", lambda e, k=k, pt=pt, xn=xn: e.transpose(out=pt[:, k * 128:(k + 1) * 128], in_=xn[:, k * 128:(k + 1) * 128],
                                                                      identity=ident),
                         reads=[xnk, "cst"], writes=[ptk], inc=(k == 7))
                S.op("act", lambda e, outview=outview, pt=pt: e.activation(out=outview, in_=pt[:].rearrange("p (k t) -> p k t", k=8),
                                                                         func=AF.Copy),
                     reads=[ptk], writes=[outkey])

        def rms_to_T_multi(items, gB, gkey):
            rms_back(rms_front(items, gB, gkey))

        def rms_to_T(x_ap, x_key, gB, gkey, outview, outkey):
            rms_to_T_multi([(x_ap, x_key, outview, outkey)], gB, gkey)

        def mm_acc(ps_ap, pskey, pairs, reads, inc_last=True, first_start=True):
            n = len(pairs)
            for i, (l, r) in enumerate(pairs):
                S.op("pe", lambda e, l=l, r=r, i=i: e.matmul(ps_ap, lhsT=l, rhs=r, start=(i == 0 and first_start),
                                                             stop=(i == n - 1)),
                     reads=reads, writes=[pskey], inc=(inc_last and i == n - 1))

        with ExitStack() as es1:
            xnT = es1.enter_context(nc.sbuf_tensor("xnT", [128, 8, NLOC], BF16))

            with ExitStack() as esp1:
                xs_r = Ring("xs", [esp1.enter_context(nc.sbuf_tensor("xs%d" % i, [128, D], F32)) for i in range(8)])
                xn1_r = Ring("xn1", [esp1.enter_context(nc.sbuf_tensor("xn1_%d" % i, [128, D], BF16)) for i in range(4)])
                xn_save = xn_r.t, xn_r.name
                xn_r.t, xn_r.name = xn_r.t + xn1_r.t, "xnw"
                for b0 in range(0, 32, 4):
                    its = []
                    for blk in range(b0, b0 + 4):
                        xin, xk = xs_r.next()
                        S.dma("sp", xin[:], xl[blk * 128:(blk + 1) * 128, :], writes=[xk])
                        its.append((xin[:], xk, xnT[:, :, blk * 128:(blk + 1) * 128], ("xnT", blk // 4)))
                    rms_to_T_multi(its, gB_mix, "gB_mix")
                S.barrier()
                xn_r.t, xn_r.name = xn_save
                xn_r.i = -1

            wbs = Ring("wbs", [es1.enter_context(nc.sbuf_tensor("wbs%d" % i, [128, 8, 128], BF16)) for i in range(8)])
            KT = es1.enter_context(nc.sbuf_tensor("KT", [128, NLOC], BF16))
            VT = es1.enter_context(nc.sbuf_tensor("VT", [128, NLOC], BF16))
            Vtok = es1.enter_context(nc.sbuf_tensor("Vtok", [128, 32, 128], BF16))

            def xkeys(t0, n):
                return [("xnT", t) for t in range(t0 // 512, (t0 + n - 1) // 512 + 1)]

            def load_w(nm, c0):
                wt, wk = wbs.next()
                S.dma("pool", wt[:], wview(SCR[nm][0])[:, :, c0:c0 + 128], writes=[wk])
                return wt, wk

            def proj(ps, psk, wt, wk, t0, n):
                mm_acc(ps[:, 0:n], psk, [(wt[:, k, :], xnT[:, k, t0:t0 + n]) for k in range(8)],
                       reads=[wk] + xkeys(t0, n))

            sbw = {}

            def sb_weights(p):
                sbw[p] = (load_w("w_in", 2304 + 128 * p), load_w("w_in", 2816 + 128 * p), load_w("w_in", 3328 + 128 * p))

            if stop_after >= 2:
                with ExitStack() as es2:
                    def sb2(name, shape, dt):
                        return es2.enter_context(nc.sbuf_tensor(name, shape, dt))
                    QT = sb2("QT", [128, NOWN], BF16)
                    accO = sb2("accO", [128, NOWN], F32)
                    accS = sb2("accS", [128, NOWN], F32)
                    cos_r = Ring("cos", [sb2("cos%d" % i, [128, 512], F32) for i in range(2)])
                    sin_r = Ring("sin", [sb2("sin%d" % i, [128, 512], F32) for i in range(2)])
                    t1_r = Ring("t1", [sb2("t1_%d" % i, [128, 512], F32) for i in range(2)])
                    t2_r = Ring("t2", [sb2("t2_%d" % i, [128, 512], F32) for i in range(2)])
                    PT_r = Ring("PT", [sb2("PT%d" % i, [128, 512], BF16) for i in range(3)])

                    def rope_proj(wa, wak, wb_, wbk, t0, n, outview, outkey):
                        cs, ck = cos_r.next()
                        sn, sk = sin_r.next()
                        S.dma("sp", cs[:, 0:n], rcos[:, t0:t0 + n], writes=[ck])
                        S.dma("sp", sn[:, 0:n], rsin[:, t0:t0 + n], writes=[sk])
                        p1, p1k = ps6.next()
                        p2, p2k = ps6.next()
                        proj(p1, p1k, wa, wak, t0, n)
                        proj(p2, p2k, wb_, wbk, t0, n)
                        t1, t1k = t1_r.next()
                        t2, t2k = t2_r.next()
                        S.op("dve", lambda e: e.tensor_tensor(out=t1[:, 0:n], in0=p1[:, 0:n], in1=cs[:, 0:n], op=ALU.mult),
                             reads=[p1k, ck], writes=[t1k])
                        S.op("dve", lambda e: e.tensor_tensor(out=t2[:, 0:n], in0=p2[:, 0:n], in1=sn[:, 0:n], op=ALU.mult),
                             reads=[p2k, sk], writes=[t2k])
                        S.op("pool", lambda e: e.tensor_tensor(out=outview, in0=t1[:, 0:n], in1=t2[:, 0:n], op=ALU.add),
                             reads=[t1k, t2k], writes=[outkey])

                    for p in range(2):
                        S.op("pool", lambda e: e.memset(accO[:], 0.0), writes=["accO"])
                        S.op("pool", lambda e: e.memset(accS[:], 0.0), writes=["accS"])
                        for g, d in enumerate(DILS):
                            hc = 256 * g + 128 * p
                            nb = 32 // d
                            wq, wqk = load_w("w_in", hc)
                            wqs, wqsk = load_w("w_sw", hc)
                            wk_, wkk = load_w("w_in", 768 + hc)
                            wks, wksk = load_w("w_sw", 768 + hc)
                            wv, wvk = load_w("w_in", 1536 + hc)
                            kv_t0 = {1: 3, 4: 2, 16: 0}[d]
                            for t in range(kv_t0, 8):
                                rope_proj(wk_, wkk, wks, wksk, t * 512, 512, KT[:, t * 512:(t + 1) * 512], ("KT", t))
                                pv, pvk = ps6.next()
                                proj(pv, pvk, wv, wvk, t * 512, 512)
                                S.op("act", lambda e, pv=pv, t=t: e.activation(out=VT[:, t * 512:(t + 1) * 512], in_=pv[:], func=AF.Copy),
                                     reads=[pvk], writes=[("VT", t)])
                            for (o0, n) in QTILES:
                                rope_proj(wq, wqk, wqs, wqsk, OWN0 + o0, n, QT[:, o0:o0 + n], ("QT", o0))
                            allVT = [("VT", t) for t in range(kv_t0, 8)]
                            allKT = [("KT", t) for t in range(kv_t0, 8)]
                            allQT = [("QT", o0) for (o0, n) in QTILES]
                            blist = [(r, B) for r in range(d) for B in range(nb) if (r + d * 128 * B) >= kv_t0 * 512]
                            for i0 in range(0, len(blist), 8):
                                grp = blist[i0:i0 + 8]
                                pt, ptk = pT.next()
                                for j, (r, B) in enumerate(grp):
                                    a0 = r + d * 128 * B
                                    S.op("pe", lambda e, j=j, a0=a0: e.transpose(out=pt[:, j * 128:(j + 1) * 128],
                                                                                 in_=VT[:, a0:a0 + d * 127 + 1:d], identity=ident),
                                         reads=allVT + ["cst"], writes=[ptk], inc=(j == len(grp) - 1))
                                for j, (r, B) in enumerate(grp):
                                    pass
                                j = 0
                                while j < len(grp):
                                    j2 = j
                                    while j2 + 1 < len(grp) and (grp[j2 + 1][0] * nb + grp[j2 + 1][1]) == (grp[j2][0] * nb + grp[j2][1]) + 1:
                                        j2 += 1
                                    i_lo = grp[j][0] * nb + grp[j][1]
                                    cnt = j2 - j + 1
                                    S.op("dve", lambda e, j=j, i_lo=i_lo, cnt=cnt, pt=pt: e.tensor_copy(
                                        out=Vtok[:, i_lo:i_lo + cnt, :],
                                        in_=pt[:, j * 128:(j + cnt) * 128].rearrange("p (c t) -> p c t", c=cnt)),
                                        reads=[ptk], writes=["Vtok"])
                                    j = j2 + 1
                            items = []
                            for r in range(d):
                                for B in range(nb):
                                    lo = r + d * 128 * B
                                    hi = r + d * (128 * B + 127)
                                    if hi < OWN0:
                                        continue
                                    c0 = 0
                                    if lo < OWN0:
                                        c0 = (OWN0 - r + d - 1) // d - 128 * B
                                    items.append((r, B, c0, 128))
                            units = []
                            cur, tot = [], 0
                            for it in items:
                                r, B, c0, c1 = it
                                w = c1 - c0
                                need = w * (2 if B >= 1 else 1)
                                if tot + need > 512:
                                    units.append(cur)
                                    cur, tot = [], 0
                                cur.append(it)
                                tot += need
                            if cur:
                                units.append(cur)
                            if p == 1 and g == 2 and stop_after >= 3:
                                sb_weights(0)
                            work = [(ui, e_) for ui in range(len(units)) for e_ in range(2)]
                            wst = {}
                            ust = {}

                            def stageA(k):
                                ui, e_ = work[k]
                                unit = units[ui]
                                rows = slice(e_ * 64, (e_ + 1) * 64)
                                pz, pzk = ps6.next()
                                tiles = []
                                off = 0
                                calls = []
                                for ii, (r, B, c0, c1) in enumerate(unit):
                                    w = c1 - c0
                                    o0 = r + d * (128 * B + c0) - OWN0
                                    for kb, kind in ((B - 1, "prev"), (B, "cur")):
                                        if kb < 0:
                                            continue
                                        a0 = r + d * 128 * kb
                                        calls.append(lambda e, off=off, w=w, a0=a0, o0=o0, rows=rows, pz=pz: e.matmul(
                                            pz[:, off:off + w], lhsT=KT[rows, a0:a0 + d * 127 + 1:d],
                                            rhs=QT[rows, o0:o0 + d * (w - 1) + 1:d], start=True, stop=True))
                                        tiles.append((off, w, kb, kind, ii, r, c0, c1))
                                        off += w
                                tot = off
                                for ci, fn in enumerate(calls):
                                    S.op("pe", fn, reads=allKT + allQT, writes=[pzk], inc=(ci == len(calls) - 1))
                                PTt, PTk = PT_r.next()
                                S.op("act", lambda e: e.activation(out=PTt[:, 0:tot], in_=pz[:, 0:tot], func=AF.Exp, scale=0.125),
                                     reads=[pzk], writes=[PTk])
                                full = (len(unit) == 2 and all(t[1] == 128 for t in tiles) and len(tiles) == 4)
                                if full:
                                    S.op("pool", lambda e: e.tensor_tensor(out=PTt[:], in0=PTt[:], in1=mask4[:], op=ALU.mult),
                                         reads=[PTk, "cst"], writes=[PTk])
                                else:
                                    for (off, w, kb, kind, ii, r, c0, c1) in tiles:
                                        mk = m_prev if kind == "prev" else m_cur
                                        S.op("pool", lambda e, off=off, w=w, mk=mk, c0=c0, c1=c1: e.tensor_tensor(
                                            out=PTt[:, off:off + w], in0=PTt[:, off:off + w], in1=mk[:, c0:c1], op=ALU.mult),
                                            reads=[PTk, "cst"], writes=[PTk])
                                wst[k] = (tiles, PTt, PTk)

                            def stageB(k):
                                ui, e_ = work[k]
                                unit = units[ui]
                                rows = slice(e_ * 64, (e_ + 1) * 64)
                                tiles, PTt, PTk = wst.pop(k)
                                if e_ == 0:
                                    pO, pOk = ps6.next()
                                    pS, pSk = ps6.next()
                                    ust[ui] = (pO, pOk, pS, pSk)
                                pO, pOk, pS, pSk = ust[ui]
                                ioff = 0
                                for ii, (r, B, c0, c1) in enumerate(unit):
                                    w = c1 - c0
                                    its = [t for t in tiles if t[4] == ii]
                                    for n_i, (off, w_, kb, kind, _, _, _, _) in enumerate(its):
                                        vi = r * nb + kb
                                        S.op("pe", lambda e, off=off, w=w, vi=vi, ioff=ioff, n_i=n_i, its=its: e.matmul(
                                            pO[rows, ioff:ioff + w], lhsT=Vtok[:, vi, rows], rhs=PTt[:, off:off + w],
                                            start=(n_i == 0), stop=(n_i == len(its) - 1)),
                                            reads=[PTk, "Vtok"], writes=[pOk], inc=False)
                                    for n_i, (off, w_, kb, kind, _, _, _, _) in enumerate(its):
                                        fl = 0 if kb < nb // 2 else 1
                                        S.op("pe", lambda e, off=off, w=w, fl=fl, ioff=ioff, n_i=n_i, its=its: e.matmul(
                                            pS[rows, ioff:ioff + w], lhsT=onesv[:, fl, :], rhs=PTt[:, off:off + w],
                                            start=(n_i == 0), stop=(n_i == len(its) - 1)),
                                            reads=[PTk, "cst"], writes=[pSk],
                                            inc=(ii == len(unit) - 1 and n_i == len(its) - 1))
                                    ioff += w
                                if e_ == 1:
                                    ioff = 0
                                    for ii, (r, B, c0, c1) in enumerate(unit):
                                        w = c1 - c0
                                        o0 = r + d * (128 * B + c0) - OWN0
                                        S.op("dve", lambda e, o0=o0, w=w, ioff=ioff: e.tensor_tensor(
                                            out=accO[:, o0:o0 + d * (w - 1) + 1:d], in0=accO[:, o0:o0 + d * (w - 1) + 1:d], in1=pO[:, ioff:ioff + w], op=ALU.add),
                                            reads=[pOk, "accO"], writes=["accO"])
                                        S.op("dve", lambda e, o0=o0, w=w, ioff=ioff: e.tensor_tensor(
                                            out=accS[:, o0:o0 + d * (w - 1) + 1:d], in0=accS[:, o0:o0 + d * (w - 1) + 1:d], in1=pS[:, ioff:ioff + w], op=ALU.add),
                                            reads=[pSk, "accS"], writes=["accS"])
                                        ioff += w
                                    del ust[ui]

                            for k in range(len(work) + 1):
                                if k < len(work):
                                    stageA(k)
                                if k >= 1:
                                    stageB(k - 1)
                        S.op("dve", lambda e: e.tensor_scalar_add(out=accS[:], in0=accS[:], scalar1=1e-30), reads=["accS"], writes=["accS"])
                        S.op("dve", lambda e: e.reciprocal(out=accS[:], in_=accS[:]), reads=["accS"], writes=["accS"])
                        S.op("dve", lambda e, p=p: e.tensor_tensor(out=yaT[:, p, :], in0=accO[:], in1=accS[:], op=ALU.mult),
                             reads=["accO", "accS"], writes=["yaT"])
                    S.barrier()

            if stop_after >= 3:
                with ExitStack() as es3:
                    def sb3(name, shape, dt):
                        return es3.enter_context(nc.sbuf_tensor(name, shape, dt))
                    E_r = Ring("E", [sb3("E%d" % i, [128, 2, 512], F32) for i in range(4)])
                    L_r = Ring("L", [sb3("L%d" % i, [128, 2, 512], BF16) for i in range(4)])
                    G_r = Ring("G", [sb3("G%d" % i, [128, 2, 512], F32) for i in range(2)])
                    A_r = Ring("A", [sb3("A%d" % i, [128, 2, 512], BF16) for i in range(2)])
                    CC = PSALL[:, 0:1024]
                    CC3 = CC.rearrange("p (h n) -> p h n", h=2)
                    CCk = [("pb", 0), ("pb", 1)]
                    psOs = [(PB[2], ("pb", 2)), (PB[3], ("pb", 3))]
                    ZZs = [(PSALL[:, 2048:3072], [("pb", 4), ("pb", 5)]), (PSALL[:, 3072:4096], [("pT", 0), ("pT", 1)])]
                    zz_i = [0]
                    QTz = [sb3("QTz%d" % i, [128, NOWN], BF16) for i in range(2)]
                    S.op("pool", lambda e: e.memset(QTz[0][64:128, :], 0.0), writes=["QTz0"])
                    S.op("pool", lambda e: e.memset(QTz[1][0:64, :], 0.0), writes=["QTz1"])

                    if 0 not in sbw:
                        sb_weights(0)
                    for p in range(4):
                        (wq, wqk), (wk_, wkk), (wv, wvk) = sbw.pop(p)
                        if p + 1 < 4:
                            sb_weights(p + 1)
                        if p == 0:
                            precast(("w_ba", "w_bb", "w_xkv", "w_in", "w_out", "w_xq", "w_xo", "w_up", "w_down"))
                        for t in range(8):
                            pk, pkk = ps6.next()
                            proj(pk, pkk, wk_, wkk, t * 512, 512)
                            S.op("act", lambda e, pk=pk, t=t: e.activation(out=KT[:, t * 512:(t + 1) * 512], in_=pk[:], func=AF.Copy),
                                 reads=[pkk], writes=[("KT", t)])
                            pv, pvk = ps6.next()
                            proj(pv, pvk, wv, wvk, t * 512, 512)
                            S.op("dve", lambda e, pv=pv, t=t: e.tensor_copy(out=VT[:, t * 512:(t + 1) * 512], in_=pv[:]),
                                 reads=[pvk], writes=[("VT", t)])
                        for (o0, n) in QTILES:
                            pq, pqk = ps6.next()
                            proj(pq, pqk, wq, wqk, OWN0 + o0, n)
                            S.op("act", lambda e, pq=pq, o0=o0, n=n: e.activation(out=QTz[0][0:64, o0:o0 + n], in_=pq[0:64, 0:n], func=AF.Copy, scale=0.125),
                                 reads=[pqk, "QTz0"], writes=[("QT", o0)])
                            S.op("act", lambda e, pq=pq, o0=o0, n=n: e.activation(out=QTz[1][64:128, o0:o0 + n], in_=pq[64:128, 0:n], func=AF.Copy, scale=0.125),
                                 reads=[pqk, "QTz1"], writes=[("QT", o0)])
                        allVT = [("VT", t) for t in range(8)]
                        allKT = [("KT", t) for t in range(8)]
                        for i0 in range(0, 32, 8):
                            pt, ptk = pT.next()
                            for j in range(8):
                                kb = i0 + j
                                S.op("pe", lambda e, j=j, kb=kb, pt=pt: e.transpose(out=pt[:, j * 128:(j + 1) * 128],
                                                                                   in_=VT[:, kb * 128:(kb + 1) * 128], identity=ident),
                                     reads=allVT + ["cst"], writes=[ptk], inc=(j == 7))
                            S.op("dve", lambda e, i0=i0, pt=pt: e.tensor_copy(out=Vtok[:, i0:i0 + 8, :],
                                                                            in_=pt[:].rearrange("p (c t) -> p c t", c=8)),
                                 reads=[ptk], writes=["Vtok"])
                        for (o0, n) in QTILES:
                            t0 = OWN0 + o0
                            kb_hi = (t0 + n) // 128 - 1
                            kb_d0 = t0 // 128
                            qk = ("QT", o0)
                            for e_ in range(2):
                                S.op("pe", lambda e, e_=e_, n=n: e.matmul(CC[:, e_ * 512:e_ * 512 + n], lhsT=zeros_b,
                                                                         rhs=cst[:, 0:4, :].rearrange("p a b -> p (a b)")[:, 0:n],
                                                                         start=True, stop=True), reads=["cst"], writes=[CCk[e_]], inc=False)
                            for e_ in range(2):
                                Ot, Ok_ = psOs[e_]
                                S.op("pe", lambda e, n=n, Ot=Ot: e.matmul(Ot[:, 0:n], lhsT=zeros_b, rhs=cst[:, 0:4, :].rearrange("p a b -> p (a b)")[:, 0:n],
                                                                          start=True, stop=True), reads=["cst"], writes=[Ok_], inc=True)
                            steps = list(range(kb_hi, -1, -1))
                            st = {}

                            def st_z(i):
                                kb = steps[i]
                                c0 = max(0, (kb - kb_d0) * 128)
                                w = n - c0
                                zz_i[0] = (zz_i[0] + 1) % 2
                                ZZ, ZZk = ZZs[zz_i[0]]
                                for e_ in range(2):
                                    S.op("pe", lambda e, e_=e_: e.matmul(ZZ[:, e_ * 512:e_ * 512 + w], lhsT=KT[:, kb * 128:(kb + 1) * 128],
                                                                        rhs=QTz[e_][:, o0 + c0:o0 + n], start=True, stop=True),
                                         reads=[("KT", kb // 4), qk], writes=[ZZk[e_]], inc=(e_ == 1))
                                st[i] = dict(kb=kb, c0=c0, w=w, ZZ=ZZ, ZZk=ZZk, diag=(kb >= kb_d0))

                            def st_el(i):
                                s = st[i]
                                w = s["w"]
                                Et, Ek = E_r.next()
                                Lt, Lk = L_r.next()
                                s.update(Et=Et, Ek=Ek, Lt=Lt, Lk=Lk)
                                ZZ3 = s["ZZ"].rearrange("p (h n) -> p h n", h=2)
                                S.op("act", lambda e: e.activation(out=Et[:, :, 0:w], in_=ZZ3[:, :, 0:w], func=AF.Exp),
                                     reads=s["ZZk"], writes=[Ek])

                            def st_l(i):
                                s = st[i]
                                w = s["w"]
                                Et, Ek, Lt, Lk = s["Et"], s["Ek"], s["Lt"], s["Lk"]
                                S.op("act", lambda e: e.activation(out=Lt[:, :, 0:w], in_=Et[:, :, 0:w], func=AF.Ln, bias=1.0),
                                     reads=[Ek], writes=[Lk])
                                if s["diag"]:
                                    for e_ in range(2):
                                        S.op("pool", lambda e, e_=e_: e.tensor_tensor(out=Lt[:, e_, 0:128], in0=Lt[:, e_, 0:128], in1=m_strict, op=ALU.mult),
                                             reads=[Lk, "cst"], writes=[Lk])

                            def st_ctri(i):
                                s = st[i]
                                for e_ in range(2):
                                    S.op("pe", lambda e, e_=e_: e.matmul(CC[:, e_ * 512 + s["c0"]:e_ * 512 + n], lhsT=tri_incl, rhs=s["Lt"][:, e_, 0:s["w"]],
                                                                        start=False, stop=True, skip_group_check=True),
                                         reads=[s["Lk"], "cst"], writes=[CCk[e_]], inc=(e_ == 1))

                            def st_g(i):
                                s = st[i]
                                Gt, Gk = G_r.next()
                                s.update(Gt=Gt, Gk=Gk)
                                S.op("act", lambda e: e.activation(out=Gt[:, :, 0:s["w"]], in_=CC3[:, :, s["c0"]:n], func=AF.Exp, scale=-1.0),
                                     reads=CCk, writes=[Gk])

                            def st_fix(i):
                                s = st[i]
                                w = s["w"]
                                At, Ak = A_r.next()
                                s.update(At=At, Ak=Ak)
                                for e_ in range(2):
                                    S.op("pe", lambda e, e_=e_: e.matmul(CC[:, e_ * 512 + s["c0"]:e_ * 512 + n], lhsT=tri_fix, rhs=s["Lt"][:, e_, 0:w],
                                                                        start=False, stop=True, skip_group_check=True),
                                         reads=[s["Lk"], "cst"], writes=[CCk[e_]], inc=(e_ == 1))
                                S.op("dve", lambda e: e.tensor_tensor(out=At[:, :, 0:w], in0=s["Et"][:, :, 0:w], in1=s["Gt"][:, :, 0:w], op=ALU.mult),
                                     reads=[s["Ek"], s["Gk"]], writes=[Ak])
                                if s["diag"]:
                                    for e_ in range(2):
                                        S.op("pool", lambda e, e_=e_: e.tensor_tensor(out=At[:, e_, 0:128], in0=At[:, e_, 0:128], in1=m_strict, op=ALU.mult),
                                             reads=[Ak, "cst"], writes=[Ak])

                            def st_pv(i):
                                s = st[i]
                                for e_ in range(2):
                                    Ot, Ok_ = psOs[e_]
                                    S.op("pe", lambda e, e_=e_, Ot=Ot: e.matmul(Ot[:, s["c0"]:n], lhsT=Vtok[:, s["kb"], :], rhs=s["At"][:, e_, 0:s["w"]],
                                                                               start=False, stop=True, skip_group_check=True),
                                         reads=[s["Ak"], "Vtok"], writes=[Ok_], inc=True)
                                del st[i]

                            ns = len(steps)
                            ok = lambda j: 0 <= j < ns
                            for it in range(ns + 5):
                                if ok(it):
                                    st_z(it)
                                if ok(it - 1):
                                    st_el(it - 1)
                                if ok(it - 3):
                                    st_g(it - 3)
                                if ok(it - 1):
                                    st_l(it - 1)
                                if ok(it - 2):
                                    st_ctri(it - 2)
                                if ok(it - 4):
                                    st_pv(it - 4)
                                if ok(it - 3):
                                    st_fix(it - 3)
                            for e_ in range(2):
                                Ot, Ok_ = psOs[e_]
                                S.op("act", lambda e, o0=o0, n=n, p=p, Ot=Ot, e_=e_: e.activation(
                                    out=ybT[e_ * 64:(e_ + 1) * 64, p, o0:o0 + n], in_=Ot[e_ * 64:(e_ + 1) * 64, 0:n], func=AF.Copy),
                                    reads=[Ok_], writes=["ybT"])
                    S.barrier()
            S.barrier()

        if dbg:
            with ExitStack() as esd:
                dbf = esd.enter_context(nc.sbuf_tensor("dbf", [128, 4, NOWN], F32))
                S.op("dve", lambda e: e.tensor_copy(out=dbf[:, 0:2, :], in_=yaT[:]), reads=["yaT"], writes=["dbf"])
                S.dma("sp", d_ya, dbf[:, 0:2, :], reads=["dbf"])
                S.barrier()
                S.op("dve", lambda e: e.tensor_copy(out=dbf[:], in_=ybT[:]), reads=["ybT"], writes=["dbf"])
                S.dma("sp", d_yb, dbf[:], reads=["dbf"])
                S.barrier()

        if stop_after >= 4:
            with ExitStack() as es4:
                def sb4(name, shape, dt):
                    return es4.enter_context(nc.sbuf_tensor("q_" + name, shape, dt))
                gB_x = sb4("gB_x", [128, D], F32)
                gB_mem = sb4("gB_mem", [128, D], F32)
                gB_ffn = sb4("gB_ffn", [128, D], F32)
                gB_f = sb4("gB_f", [128, D], F32)
                wb_r = Ring("wb", [sb4("wb%d" % i, [128, 4096], BF16) for i in range(3)])
                H2 = [sb4("H%d" % i, [128, 4, D], F32) for i in range(2)]
                H = H2[1]
                xT = sb4("xT", [128, 8, 512], BF16)
                mT = sb4("mT", [128, 8, 512], BF16)
                QxT = mT[:, 0:4, :]
                oxT = mT[:, 4:8, :]
                ta_r = Ring("ta", [sb4("ta%d" % i, [128, 512], F32) for i in range(2)])
                tb_r = Ring("tb", [sb4("tb%d" % i, [128, 512], F32) for i in range(2)])
                KxT = sb4("KxT", [128, 4, 256], BF16)
                Vx = sb4("Vx", [128, 2, 512], BF16)
                Px_r = Ring("Px", [sb4("Px%d" % i, [128, 2, 512], BF16) for i in range(2)])
                rsum_r = Ring("rsum", [sb4("rsum%d" % i, [128, 512], F32) for i in range(1)])
                aT = sb4("aT", [128, NCH, 512], BF16)
                U_r = Ring("U", [sb4("U%d" % i, [128, 514], F32) for i in range(2)])
                tc_r = Ring("tc", [sb4("tc%d" % i, [128, 512], F32) for i in range(4)])
                sg_r = Ring("sg", [sb4("sg%d" % i, [128, 512], F32) for i in range(1)])
                carry = sb4("carry", [128, 44, 2], F32)
                bg = sb4("bg", [128, 16], F32)
                cw = sb4("cw", [128, 44, 3], F32)
                cb = sb4("cb", [128, 44], F32)
                if dbg:
                    dbuf = sb4("dbuf", [128, 2, 512], F32)

                S.dma("sp", gB_x[:], g_x.partition_broadcast(128), writes=["gB_x"])
                S.dma("sp", gB_mem[:], g_mem.partition_broadcast(128), writes=["gB_mem"])
                S.dma("sp", gB_ffn[:], g_ffn.partition_broadcast(128), writes=["gB_ffn"])
                S.dma("sp", gB_f[:], g_f.partition_broadcast(128), writes=["gB_f"])
                S.dma("sp", bg[:], b_gate, writes=["smallc"])
                S.dma("sp", cw[:], conv_w, writes=["smallc"])
                S.dma("sp", cb[:], conv_b, writes=["smallc"])
                S.op("pool", lambda e: e.memset(carry[:], 0.0), writes=["carry"])

                def load_wb(src_aps, shape3):
                    wt, wk = wb_r.next()
                    kc, nn = shape3
                    v = wt[:, 0:kc * nn].rearrange("p (k n) -> p k n", k=kc)
                    for (c0, c1, nm, src) in src_aps:
                        S.dma("sp", v[:, :, c0:c1], src, reads=skeys(nm), writes=[wk])
                    return v, wk

                wba = sb4("wba", [128, 2, 1024], BF16)
                wbb = sb4("wbb", [128, 4, 1024], BF16)
                wbak, wbbk = "wba", "wbb"
                S.dma("sp", wba[:], sw("w_ba"), reads=skeys("w_ba"), writes=[wbak])
                S.dma("sp", wbb[:], sw("w_bb"), reads=skeys("w_bb"), writes=[wbbk])
                for mb in range(2):
                    S.dma("sp", H[:, mb, :], mem[mb * 128:(mb + 1) * 128, :], writes=[("H", 1, mb)])
                    rms_to_T(H[:, mb, :], ("H", 1, mb), gB_mem, "gB_mem", xT[:, :, mb * 128:(mb + 1) * 128], "xT")
                wkv_k, wkvk_k = load_wb([(0, 512, "w_xkv", sw("w_xkv")[:, :, 0:512])], (8, 512))
                wkv_v, wkvk_v = load_wb([(0, 512, "w_xkv", sw("w_xkv")[:, :, 512:1024])], (8, 512))
                for hx in range(4):
                    ps, psk = ps6.next()
                    mm_acc(ps[:, 0:256], psk, [(wkv_k[:, k, hx * 128:(hx + 1) * 128], xT[:, k, 0:256]) for k in range(8)],
                           reads=[wkvk_k, "xT"])
                    S.op("act", lambda e, ps=ps, hx=hx: e.activation(out=KxT[:, hx, :], in_=ps[:, 0:256], func=AF.Copy),
                         reads=[psk], writes=["KxT"])
                for mb in range(2):
                    ps, psk = ps6.next()
                    mm_acc(ps[:], psk, [(xT[:, k, mb * 128:(mb + 1) * 128], wkv_v[:, k, :]) for k in range(8)],
                           reads=[wkvk_v, "xT"])
                    S.op("act", lambda e, ps=ps, mb=mb: e.activation(out=Vx[:, mb, :], in_=ps[:], func=AF.Copy),
                         reads=[psk], writes=["Vx"])

                XS = 1.0 / math.sqrt(128.0)
                def prefetch_x(ti):
                    o0_, n_ = QTILES[ti]
                    for blk in range(n_ // 128):
                        r0 = OWN0 + o0_ + blk * 128
                        S.dma("pool", H2[ti % 2][:, blk, :], xl[r0:r0 + 128, :], writes=[("H", ti % 2, blk)])

                a_done = {}

                def emit_a_front(tj):
                    o0_, n_ = QTILES[tj]
                    Hj = H2[tj % 2]
                    a_done[tj] = rms_front([(Hj[:, blk, :], ("H", tj % 2, blk), xT[:, :, blk * 128:(blk + 1) * 128], "xT")
                                            for blk in range(n_ // 128)], gB_mix, "gB_mix")

                def emit_a_back(tj):
                    rms_back(a_done[tj])

                prefetch_x(0)
                for ti, (o0, n) in enumerate(QTILES):
                    halo = (o0 == 0)
                    nblk = n // 128
                    H = H2[ti % 2]
                    hp = ti % 2
                    if ti not in a_done:
                        emit_a_front(ti)
                        emit_a_back(ti)
                    if ti + 1 < len(QTILES):
                        prefetch_x(ti + 1)
                    if stop_after < 5:
                        continue
                    for og in range(2):
                        wga, wgak = load_wb([(0, 512, "w_in", sw("w_in")[:, :, 3840 + 512 * og:3840 + 512 * (og + 1)])], (8, 512))
                        for oi in range(4):
                            o = 4 * og + oi
                            ta, tak = ta_r.next()
                            ps, psk = ps6.next()
                            mm_acc(ps[:, 0:n], psk, [(wga[:, k, oi * 128:(oi + 1) * 128], xT[:, k, 0:n]) for k in range(8)],
                                   reads=[wgak, "xT"])
                            S.op("act", lambda e, ps=ps, ta=ta, o=o: e.activation(out=ta[:, 0:n], in_=ps[:, 0:n], func=AF.Sigmoid,
                                                                                bias=bg[:, o:o + 1]),
                                 reads=[psk, "smallc"], writes=[tak])
                            ps2, ps2k = ps6.next()
                            mm_acc(ps2[:, 0:n], ps2k, [(wba[:, k, o * 128:(o + 1) * 128], yaT[:, k, o0:o0 + n]) for k in range(2)],
                                   reads=[wbak, "yaT"])
                            S.op("dve", lambda e, ps2=ps2, ta=ta: e.tensor_tensor(out=ta[:, 0:n], in0=ps2[:, 0:n], in1=ta[:, 0:n], op=ALU.mult),
                                 reads=[ps2k, tak], writes=[tak])
                            S.op("pool", lambda e, ta=ta, o=o: e.tensor_copy(out=mT[:, o, 0:n], in_=ta[:, 0:n]),
                                 reads=[tak], writes=[("mTa", o), "QxT" if o < 4 else "oxT"])
                    pend_add = []

                    def emit_add(tb, tbk, o):
                        S.op("dve", lambda e: e.tensor_tensor(out=mT[:, o, 0:n], in0=mT[:, o, 0:n], in1=tb[:, 0:n], op=ALU.add),
                             reads=[tbk, ("mTa", o)], writes=[("mTa", o)])

                    for og in range(2):
                        wgb, wgbk = load_wb([(0, 512, "w_in", sw("w_in")[:, :, 4864 + 512 * og:4864 + 512 * (og + 1)])], (8, 512))
                        for oi in range(4):
                            o = 4 * og + oi
                            tb, tbk = tb_r.next()
                            ps, psk = ps6.next()
                            mm_acc(ps[:, 0:n], psk, [(wgb[:, k, oi * 128:(oi + 1) * 128], xT[:, k, 0:n]) for k in range(8)],
                                   reads=[wgbk, "xT"])
                            S.op("act", lambda e, ps=ps, tb=tb, o=o: e.activation(out=tb[:, 0:n], in_=ps[:, 0:n], func=AF.Sigmoid,
                                                                                bias=bg[:, 8 + o:9 + o]),
                                 reads=[psk, "smallc"], writes=[tbk])
                            ps2, ps2k = ps6.next()
                            mm_acc(ps2[:, 0:n], ps2k, [(wbb[:, k, o * 128:(o + 1) * 128], ybT[:, k, o0:o0 + n]) for k in range(4)],
                                   reads=[wbbk, "ybT"])
                            S.op("dve", lambda e, ps2=ps2, tb=tb: e.tensor_tensor(out=tb[:, 0:n], in0=ps2[:, 0:n], in1=tb[:, 0:n], op=ALU.mult),
                                 reads=[ps2k, tbk], writes=[tbk])
                            if pend_add:
                                emit_add(*pend_add.pop())
                            pend_add.append((tb, tbk, o))
                    while pend_add:
                        emit_add(*pend_add.pop())
                    allm = [("mTa", o) for o in range(8)]
                    if stop_after < 6:
                        continue
                    if dbg and o0 == 128:
                        for q4 in range(4):
                            S.op("dve", lambda e, q4=q4: e.tensor_copy(out=dbuf[:], in_=mT[:, 2 * q4:2 * q4 + 2, :]), reads=allm, writes=["dbuf"])
                            S.dma("sp", d_m[:, 2 * q4:2 * q4 + 2, :], dbuf[:], reads=["dbuf"])
                        for q4 in range(4):
                            S.op("dve", lambda e, q4=q4: e.tensor_copy(out=dbuf[:], in_=xT[:, 2 * q4:2 * q4 + 2, :]), reads=["xT"], writes=["dbuf"])
                            S.dma("sp", d_x[:, 2 * q4:2 * q4 + 2, :], dbuf[:], reads=["dbuf"])
                    for half in range(2):
                        wo, wok = load_wb([(0, 512, "w_out", sw("w_out")[:, :, half * 512:(half + 1) * 512])], (8, 512))
                        for blk in range(nblk):
                            ps, psk = ps6.next()
                            mm_acc(ps[:], psk, [(mT[:, k, blk * 128:(blk + 1) * 128], wo[:, k, :]) for k in range(8)],
                                   reads=[wok] + allm)
                            S.op("dve", lambda e, ps=ps, blk=blk, half=half: e.tensor_tensor(
                                out=H[:, blk, half * 512:(half + 1) * 512], in0=H[:, blk, half * 512:(half + 1) * 512], in1=ps[:], op=ALU.add),
                                reads=[psk, ("H", hp, blk)], writes=[("H", hp, blk)])
                    if dbg and not halo:
                        for blk in range(4):
                            r0 = o0 - 128 + blk * 128
                            S.dma("sp", d_h[0, r0:r0 + 128, :], H[:, blk, :], reads=[("H", hp, blk)])
                    rms_to_T_multi([(H[:, blk, :], ("H", hp, blk), xT[:, :, blk * 128:(blk + 1) * 128], "xT") for blk in range(nblk)],
                                   gB_x, "gB_x")
                    if stop_after < 7:
                        continue
                    wxq, wxqk = load_wb([(0, 512, "w_xq", sw("w_xq"))], (8, 512))
                    for c in range(4):
                        ps, psk = ps6.next()
                        mm_acc(ps[:, 0:n], psk, [(wxq[:, k, c * 128:(c + 1) * 128], xT[:, k, 0:n]) for k in range(8)],
                               reads=[wxqk, "xT"])
                        S.op("act", lambda e, ps=ps, c=c: e.activation(out=QxT[:, c, 0:n], in_=ps[:, 0:n], func=AF.Copy),
                             reads=[psk], writes=["QxT", ("mTa", c)])
                    for hx in range(4):
                        Px, Pxk = Px_r.next()
                        for mb in range(2):
                            ps, psk = ps6.next()
                            S.op("pe", lambda e, ps=ps, hx=hx, mb=mb: e.matmul(ps[:, 0:n], lhsT=KxT[:, hx, mb * 128:(mb + 1) * 128],
                                                                             rhs=QxT[:, hx, 0:n], start=True, stop=True),
                                 reads=["KxT", "QxT"], writes=[psk])
                            S.op("act", lambda e, ps=ps, Px=Px, mb=mb: e.activation(out=Px[:, mb, 0:n], in_=ps[:, 0:n], func=AF.Exp, scale=XS),
                                 reads=[psk], writes=[Pxk])
                        pso, psok = ps6.next()
                        mm_acc(pso[:, 0:n], psok, [(Vx[:, mb, hx * 128:(hx + 1) * 128], Px[:, mb, 0:n]) for mb in range(2)],
                               reads=["Vx", Pxk])
                        psr, psrk = ps6.next()
                        mm_acc(psr[:, 0:n], psrk, [(ones_b, Px[:, mb, 0:n]) for mb in range(2)], reads=["cst", Pxk])
                        rsum, rsumk = rsum_r.next()
                        S.op("dve", lambda e, psr=psr, rsum=rsum: e.reciprocal(out=rsum[:, 0:n], in_=psr[:, 0:n]),
                             reads=[psrk], writes=[rsumk])
                        S.op("dve", lambda e, pso=pso, rsum=rsum, hx=hx: e.tensor_tensor(out=oxT[:, hx, 0:n], in0=pso[:, 0:n], in1=rsum[:, 0:n], op=ALU.mult),
                             reads=[psok, rsumk], writes=["oxT", ("mTa", 4 + hx)])
                    wxo, wxok = load_wb([(0, 1024, "w_xo", sw("w_xo"))], (4, 1024))
                    for half in range(2):
                        for blk in range(nblk):
                            ps, psk = ps6.next()
                            mm_acc(ps[:], psk, [(oxT[:, k, blk * 128:(blk + 1) * 128], wxo[:, k, half * 512:(half + 1) * 512]) for k in range(4)],
                                   reads=[wxok, "oxT"])
                            S.op("dve", lambda e, ps=ps, blk=blk, half=half: e.tensor_tensor(
                                out=H[:, blk, half * 512:(half + 1) * 512], in0=H[:, blk, half * 512:(half + 1) * 512], in1=ps[:], op=ALU.add),
                                reads=[psk, ("H", hp, blk)], writes=[("H", hp, blk)])
                    if dbg and not halo:
                        for blk in range(4):
                            r0 = o0 - 128 + blk * 128
                            S.dma("sp", d_h[1, r0:r0 + 128, :], H[:, blk, :], reads=[("H", hp, blk)])
                    rms_to_T_multi([(H[:, blk, :], ("H", hp, blk), xT[:, :, blk * 128:(blk + 1) * 128], "xT") for blk in range(nblk)],
                                   gB_ffn, "gB_ffn")
                    if stop_after < 8:
                        continue
                    pend = []

                    def emit_back(c, res_t):
                        (tg, tgk), (tv, tvk) = res_t
                        sg, sgk = sg_r.next()
                        S.op("act", lambda e: e.activation(out=sg[:], in_=tg[:], func=AF.Silu), reads=[tgk], writes=[sgk])
                        S.op("pool", lambda e: e.tensor_tensor(out=aT[:, c, :], in0=sg[:], in1=tv[:], op=ALU.mult),
                             reads=[sgk, tvk], writes=[("aT", c)])

                    for cg in range(11):
                        if cg == 8 and ti + 1 < len(QTILES):
                            emit_a_front(ti + 1)
                        wu, wuk = load_wb([(0, 256, "w_up", sw("w_up")[:, :, 256 * cg:256 * (cg + 1)]),
                                           (256, 512, "w_up", sw("w_up")[:, :, DFF + 256 * cg:DFF + 256 * (cg + 1)])], (8, 512))
                        for ci in range(2):
                            c = 2 * cg + ci
                            res_t = []
                            parts = []
                            for part in range(2):
                                ch = c + NCH * part
                                ps, psk = ps6.next()
                                mm_acc(ps[:, 0:n], psk, [(wu[:, k, 256 * part + ci * 128:256 * part + (ci + 1) * 128], xT[:, k, 0:n]) for k in range(8)],
                                       reads=[wuk, "xT"])
                                if halo:
                                    S.op("dve", lambda e, ps=ps, ch=ch: e.tensor_scalar_mul(out=carry[:, ch, :], in0=ps[:, n - 2:n], scalar1=hval[:, 0:1]),
                                         reads=[psk, "cst"], writes=[("carry", ch)])
                                    continue
                                U, Uk = U_r.next()
                                tcb, tck = tc_r.next()
                                parts.append((ch, ps, psk, U, Uk, tcb, tck))
                            for (ch, ps, psk, U, Uk, tcb, tck) in parts:
                                S.op("pool", lambda e, U=U, ch=ch: e.tensor_copy(out=U[:, 0:2], in_=carry[:, ch, :]),
                                     reads=[("carry", ch), "carry"], writes=[(Uk, "c")])
                                S.op("act", lambda e, U=U, ps=ps: e.activation(out=U[:, 2:514], in_=ps[:], func=AF.Copy),
                                     reads=[psk], writes=[(Uk, "d")])
                                S.op("pool", lambda e, U=U, ch=ch: e.tensor_copy(out=carry[:, ch, :], in_=U[:, 512:514]),
                                     reads=[(Uk, "d")], writes=[("carry", ch)])
                            for (ch, ps, psk, U, Uk, tcb, tck) in parts:
                                S.op("act", lambda e, U=U, tcb=tcb, ch=ch: e.activation(out=tcb[:], in_=U[:, 0:512], func=AF.Identity,
                                                                                        scale=cw[:, ch, 0:1], bias=cb[:, ch:ch + 1]),
                                     reads=[(Uk, "c"), (Uk, "d"), "smallc"], writes=[tck])
                            for (ch, ps, psk, U, Uk, tcb, tck) in parts:
                                S.op("dve", lambda e, U=U, tcb=tcb, ch=ch: e.scalar_tensor_tensor(out=tcb[:], in0=U[:, 1:513], scalar=cw[:, ch, 1:2],
                                                                                                in1=tcb[:], op0=ALU.mult, op1=ALU.add),
                                     reads=[(Uk, "c"), (Uk, "d"), tck, "smallc"], writes=[tck])
                            for (ch, ps, psk, U, Uk, tcb, tck) in parts:
                                S.op("dve", lambda e, ps=ps, tcb=tcb, ch=ch: e.scalar_tensor_tensor(out=tcb[:], in0=ps[:], scalar=cw[:, ch, 2:3],
                                                                                                 in1=tcb[:], op0=ALU.mult, op1=ALU.add),
                                     reads=[psk, tck, "smallc"], writes=[tck])
                                res_t.append((tcb, tck))
                            if halo:
                                continue
                            if pend:
                                emit_back(*pend.pop())
                            pend.append((c, res_t))
                    while pend:
                        emit_back(*pend.pop())
                    if ti + 1 < len(QTILES):
                        emit_a_back(ti + 1)
                    if halo:
                        continue
                    if stop_after < 9:
                        continue
                    alla = [("aT", c) for c in range(NCH)]
                    for half in range(2):
                        accs = [(PB[i], ("pb", i)) for i in range(4)]
                        for cgp, (c_lo, c_hi) in enumerate(((0, 8), (8, 16), (16, 22))):
                            wd, wdk = load_wb([(0, 512, "w_down", sw("w_down")[:, c_lo:c_hi, half * 512:(half + 1) * 512])], (c_hi - c_lo, 512))
                            for c in range(c_lo, c_hi):
                                for blk in range(4):
                                    ps, psk = accs[blk]
                                    S.op("pe", lambda e, ps=ps, c=c, blk=blk, wd=wd, c_lo=c_lo: e.matmul(
                                        ps[:], lhsT=aT[:, c, blk * 128:(blk + 1) * 128], rhs=wd[:, c - c_lo, :],
                                        start=(c == 0), stop=(c == NCH - 1)),
                                        reads=[wdk, ("aT", c)], writes=[psk], inc=(c == NCH - 1 or (c == c_hi - 1 and blk == 3)))
                        for blk in range(4):
                            ps, psk = accs[blk]
                            S.op("dve", lambda e, ps=ps, blk=blk, half=half: e.tensor_tensor(
                                out=H[:, blk, half * 512:(half + 1) * 512], in0=H[:, blk, half * 512:(half + 1) * 512], in1=ps[:], op=ALU.add),
                                reads=[psk, ("H", hp, blk)], writes=[("H", hp, blk)])
                    if dbg and not halo:
                        for blk in range(4):
                            r0 = o0 - 128 + blk * 128
                            S.dma("sp", d_h[2, r0:r0 + 128, :], H[:, blk, :], reads=[("H", hp, blk)])
                    fin = []
                    for blk in range(4):
                        ss, ssk = ss_r.next()
                        rs, rsk = rs_r.next()
                        S.op("act", lambda e, blk=blk, ss=ss: e.activation(out=junk[:], in_=H[:, blk, :], func=AF.Square, accum_out=ss[:]),
                             reads=[("H", hp, blk)], writes=["junk", ssk])
                        fin.append((blk, ss, ssk, rs, rsk))
                    for (blk, ss, ssk, rs, rsk) in fin:
                        S.op("act", lambda e, ss=ss, rs=rs: e.activation(out=rs[:], in_=ss[:], func=AF.Ln, scale=1.0 / D, bias=1e-6),
                             reads=[ssk], writes=[rsk])
                    for (blk, ss, ssk, rs, rsk) in fin:
                        S.op("act", lambda e, rs=rs: e.activation(out=rs[:], in_=rs[:], func=AF.Exp, scale=-0.5), reads=[rsk], writes=[rsk])
                    for (blk, ss, ssk, rs, rsk) in fin:
                        ob, obk = xin_r.next()
                        S.op("dve", lambda e, blk=blk, rs=rs, ob=ob: e.scalar_tensor_tensor(out=ob[:], in0=H[:, blk, :], scalar=rs[:], in1=gB_f[:],
                                                                                          op0=ALU.mult, op1=ALU.mult),
                             reads=[("H", hp, blk), rsk, "gB_f"], writes=[obk])
                        r0 = o0 - 128 + blk * 128
                        S.dma("pool", out[r0:r0 + 128, :], ob[:], reads=[obk])
                S.barrier()
        S.finish("sp")
    return nc


def _consts():
    bf = ml_dtypes.bfloat16
    k = np.arange(128)[:, None]
    q = np.arange(128)[None, :]
    cst = np.zeros((128, 8, 128), np.float32)
    cst[:, 0] = (k == q)
    cst[:, 1] = (k <= q)
    cst[:, 2] = (k >= q)
    cst[:, 3] = (k < q)
    cst[:, 4] = (k >= q)
    cst[:, 5] = (k < q)
    cst[:, 6] = 0.0
    cst[:, 7] = 1.0
    mask4 = np.concatenate([cst[:, 2], cst[:, 1], cst[:, 2], cst[:, 1]], axis=1)
    return cst.astype(bf), mask4.astype(bf)


def _rope_tables(h):
    gpos = (np.arange(NLOC) - (2048 if h == 0 else 0)).astype(np.float32)
    half = 32
    inv_freq = (1.0 / (np.float32(10000.0) ** (np.arange(half, dtype=np.float32) * np.float32(2.0 / 64)))).astype(np.float32)
    ang = (gpos[None, :] * inv_freq[:, None]).astype(np.float32)
    c = np.cos(ang).astype(np.float32)
    s = np.sin(ang).astype(np.float32)
    rc = np.concatenate([c, c, c, c], axis=0)
    rs = np.concatenate([-s, s, -s, s], axis=0)
    return np.ascontiguousarray(rc), np.ascontiguousarray(rs)


def make_in_maps(inp):
    bf = ml_dtypes.bfloat16
    f = lambda a: np.ascontiguousarray(np.asarray(a, dtype=np.float32))
    x = f(inp["x"])
    memv = f(inp["mem"])
    w_in = f(inp["w_in"][0])
    j = np.arange(1536)
    partner = np.where((j % 64) < 32, j + 32, j - 32)
    w_sw = np.ascontiguousarray(w_in[:, partner])
    cst, mask4 = _consts()
    common = {
        "w_in": w_in, "w_sw": w_sw,
        "g_mix": f(inp["ln_mix_g"][0]).reshape(1, D), "g_x": f(inp["ln_x_g"][0]).reshape(1, D),
        "g_mem": f(inp["ln_mem_g"][0]).reshape(1, D), "g_ffn": f(inp["ln_ffn_g"][0]).reshape(1, D),
        "g_f": f(inp["ln_f_g"]).reshape(1, D),
        "b_gate": np.ascontiguousarray(f(inp["b_gate"][0]).reshape(16, 128).T),
        "w_ba": f(inp["w_branch_a"][0]), "w_bb": f(inp["w_branch_b"][0]), "w_out": f(inp["w_out"][0]),
        "w_xq": f(inp["w_xq"][0]), "w_xkv": f(inp["w_xkv"][0]), "w_xo": f(inp["w_xo"][0]),
        "w_up": f(inp["w_up"][0]),
        "conv_w": np.ascontiguousarray(f(inp["conv_w"][0]).reshape(3, 44, 128).transpose(2, 1, 0)),
        "conv_b": np.ascontiguousarray(f(inp["conv_b"][0]).reshape(44, 128).T),
        "w_down": f(inp["w_down"][0]),
        "cst": cst, "mask4": mask4,
    }
    ropes = [_rope_tables(0), _rope_tables(1)]
    maps = []
    for c in range(8):
        b, h = c // 2, c % 2
        if h == 0:
            xl = np.zeros((NLOC, D), np.float32)
            xl[2048:] = x[b, :2048]
        else:
            xl = np.ascontiguousarray(x[b])
        onesv = np.ones((128, 2, 64), np.float32)
        onesv[:, 0, :] = float(h)
        m = dict(common)
        m.update({"xl": xl, "mem": np.ascontiguousarray(memv[b]), "rcos": ropes[h][0], "rsin": ropes[h][1],
                  "onesv": onesv.astype(bf), "hval": np.full((128, 1), float(h), np.float32)})
        maps.append(m)
    return maps


_NC_CACHE = {}


def kernel(**inputs):
    if "nc" not in _NC_CACHE:
        _NC_CACHE["nc"] = build()
    nc = _NC_CACHE["nc"]
    maps = make_in_maps(inputs)
    res = run_bass_kernel_spmd(nc, maps, core_ids=list(range(8)))
    outp = np.zeros((4, 4096, D), np.float32)
    for c in range(8):
        b, h = c // 2, c % 2
        outp[b, 2048 * h:2048 * (h + 1)] = np.asarray(res.results[c]["out"], dtype=np.float32)
    return outp
```
